# Optimizing a Trainium2 kernel written in Bass

```python
import math
import jax, jax.numpy as jnp
from jax import lax
import numpy as np

D_MODEL = 1024
BATCH = 2
SEQ = 8192
DEPTH = 4

N_MIXERS = 2
N_SSD_LAYERS = (DEPTH + 1) // 2
N_HGRN_LAYERS = DEPTH // 2

SSD_EXPAND = 2
SSD_D_INNER = SSD_EXPAND * D_MODEL
SSD_HEAD_DIM = 64
SSD_N_HEADS = SSD_D_INNER // SSD_HEAD_DIM
SSD_N_GROUPS = 4
SSD_HEADS_PER_GROUP = SSD_N_HEADS // SSD_N_GROUPS
SSD_D_STATE = 128
SSD_CONV_WIDTH = 4
SSD_CHUNK = 128
SSD_BC_DIM = SSD_N_GROUPS * SSD_D_STATE
SSD_CONV_DIM = SSD_D_INNER + 2 * SSD_BC_DIM
SSD_IN_DIM = SSD_D_INNER + SSD_CONV_DIM + SSD_N_HEADS

HGRN_EXPAND = 128
HGRN_N_HEADS = D_MODEL // HGRN_EXPAND
HGRN_HEAD_V = D_MODEL // HGRN_N_HEADS
HGRN_CHUNK = 64
HGRN_IN_DIM = 4 * D_MODEL

D_FF = 4 * D_MODEL
PLE_DIM = 256
DEEPNORM_ALPHA = (2.0 * DEPTH) ** 0.25
DEEPNORM_BETA = (8.0 * DEPTH) ** -0.25
LN_EPS = 1e-5
RMS_EPS = 1e-5

kernel_name = "hybrid_ssd_hgrn2_deepnorm_trunk"


def layer_norm(x, g, b):
    xf = x.astype(jnp.float32)
    mu = jnp.mean(xf, axis=-1, keepdims=True)
    xc = xf - mu
    var = jnp.mean(xc * xc, axis=-1, keepdims=True)
    y = xc * lax.rsqrt(var + LN_EPS) * g.astype(jnp.float32) + b.astype(jnp.float32)
    return y.astype(x.dtype)


def rms_norm(x, w):
    xf = x.astype(jnp.float32)
    return xf * lax.rsqrt(jnp.mean(xf * xf, axis=-1, keepdims=True) + RMS_EPS) * w.astype(jnp.float32)


def causal_depthwise_conv(u, w, b):
    k, c = w.shape
    out = lax.conv_general_dilated(u, w[:, None, :].astype(u.dtype), window_strides=(1,),
                                   padding=[(k - 1, 0)], dimension_numbers=('NWC', 'WIO', 'NWC'),
                                   feature_group_count=c)
    return out + b


def segsum_exp(a):
    t = a.shape[-1]
    cs = jnp.cumsum(a, axis=-1)
    seg = cs[..., :, None] - cs[..., None, :]
    mask = jnp.tril(jnp.ones((t, t), dtype=bool))
    return jnp.exp(jnp.where(mask, seg, -jnp.inf))


def ssd_mixer(u, w_in, conv_w, conv_b, dt_bias, a_log, d_skip, norm_w, w_out):
    bsz, seq, _ = u.shape
    nc = seq // SSD_CHUNK
    g_, r_, p_, n_ = SSD_N_GROUPS, SSD_HEADS_PER_GROUP, SSD_HEAD_DIM, SSD_D_STATE
    zxbcdt = u @ w_in
    z, xbc, dt = jnp.split(zxbcdt, [SSD_D_INNER, SSD_D_INNER + SSD_CONV_DIM], axis=-1)
    xbc = jax.nn.silu(causal_depthwise_conv(xbc, conv_w, conv_b))
    xs, bm, cm = jnp.split(xbc, [SSD_D_INNER, SSD_D_INNER + SSD_BC_DIM], axis=-1)
    xs = xs.astype(jnp.float32).reshape(bsz, nc, SSD_CHUNK, g_, r_, p_)
    bm = bm.astype(jnp.float32).reshape(bsz, nc, SSD_CHUNK, g_, n_)
    cm = cm.astype(jnp.float32).reshape(bsz, nc, SSD_CHUNK, g_, n_)
    dt = jax.nn.softplus(dt.astype(jnp.float32) + dt_bias.astype(jnp.float32))
    a = -jnp.exp(a_log.astype(jnp.float32))
    dt = dt.reshape(bsz, nc, SSD_CHUNK, g_, r_)
    da = dt * a.reshape(g_, r_)
    xdt = xs * dt[..., None]
    a_cs = jnp.cumsum(da, axis=2)
    decay = segsum_exp(jnp.moveaxis(da, 2, -1))
    cb = jnp.einsum('bclgn,bcsgn->bcgls', cm, bm)
    y_diag = jnp.einsum('bcgls,bcgrls,bcsgrp->bclgrp', cb, decay, xdt)
    decay_to_end = jnp.exp(a_cs[:, :, -1:] - a_cs)
    states = jnp.einsum('bclgn,bclgr,bclgrp->bcgrpn', bm, decay_to_end, xdt)
    chunk_decay = jnp.exp(a_cs[:, :, -1])

    def step(h, inp):
        st, dec = inp
        return h * dec[..., None, None] + st, h

    h0 = jnp.zeros((bsz, g_, r_, p_, n_), jnp.float32)
    _, prev = lax.scan(step, h0, (jnp.moveaxis(states, 1, 0), jnp.moveaxis(chunk_decay, 1, 0)))
    prev = jnp.moveaxis(prev, 0, 1)
    y_off = jnp.einsum('bclgn,bcgrpn,bclgr->bclgrp', cm, prev, jnp.exp(a_cs))
    y = y_diag + y_off + d_skip.astype(jnp.float32).reshape(g_, r_)[:, :, None] * xs
    y = y.reshape(bsz, seq, SSD_D_INNER) * jax.nn.silu(z.astype(jnp.float32))
    y = rms_norm(y.reshape(bsz, seq, g_, SSD_D_INNER // g_), norm_w.reshape(g_, SSD_D_INNER // g_))
    y = y.reshape(bsz, seq, SSD_D_INNER).astype(u.dtype)
    return y @ w_out


def hgrn2_mixer(u, w_in, lb, norm_w, w_out):
    bsz, seq, _ = u.shape
    nc = seq // HGRN_CHUNK
    h_, k_dim, v_dim, c_ = HGRN_N_HEADS, HGRN_EXPAND, HGRN_HEAD_V, HGRN_CHUNK
    q, fz, v, g = jnp.split(u @ w_in, 4, axis=-1)
    fz = fz.astype(jnp.float32)
    lb = lb.astype(jnp.float32)
    log_f = jnp.logaddexp(jnp.log(lb), jnp.log1p(-lb) + jax.nn.log_sigmoid(fz))
    k = (1.0 - lb) * jax.nn.sigmoid(-fz)

    def chunked(t, d):
        return jnp.moveaxis(t.astype(jnp.float32).reshape(bsz, nc, c_, h_, d), 1, 0)

    qc, kc, vc = chunked(q, k_dim), chunked(k, k_dim), chunked(v, v_dim)
    b_cs = jnp.cumsum(chunked(log_f, k_dim), axis=2)
    mask = jnp.tril(jnp.ones((c_, c_), dtype=bool))[None, :, :, None, None]

    def step(s_prev, inp):
        q_, kk, v_, b_ = inp
        diff = b_[:, :, None] - b_[:, None, :]
        dec = jnp.exp(jnp.where(mask, diff, -jnp.inf))
        scores = jnp.einsum('blhk,bshk,blshk->bhls', q_, kk, dec)
        o = jnp.einsum('bhls,bshv->blhv', scores, v_) + jnp.einsum('blhk,bhkv->blhv', q_ * jnp.exp(b_), s_prev)
        b_last = b_[:, -1]
        s_new = jnp.exp(b_last)[..., None] * s_prev + jnp.einsum(
            'bshk,bshv->bhkv', kk * jnp.exp(b_last[:, None] - b_), v_)
        return s_new, o

    s0 = jnp.zeros((bsz, h_, k_dim, v_dim), jnp.float32)
    _, o = lax.scan(step, s0, (qc, kc, vc, b_cs))
    o = jnp.moveaxis(o, 0, 1).reshape(bsz, seq, h_, v_dim)
    o = rms_norm(o, norm_w).reshape(bsz, seq, D_MODEL) * jax.nn.silu(g.astype(jnp.float32))
    return o.astype(u.dtype) @ w_out


def sq_relu_mlp(h, w1, w2):
    a = jax.nn.relu(h @ w1)
    return (a * a) @ w2


def setup_inputs(seed: int = 0) -> dict:
    key = jax.random.key(seed)
    ks = jax.random.split(key, 20)
    f32 = jnp.float32
    nrm = lambda k, shape, s: jax.random.normal(k, shape, f32) * s
    x = jax.random.normal(ks[0], (BATCH, SEQ, D_MODEL), f32)
    p = jax.random.normal(ks[1], (DEPTH, BATCH, SEQ, PLE_DIM), f32)
    ssd_w_in = nrm(ks[2], (N_SSD_LAYERS, D_MODEL, SSD_IN_DIM), D_MODEL ** -0.5)
    ssd_conv_w = nrm(ks[3], (N_SSD_LAYERS, SSD_CONV_WIDTH, SSD_CONV_DIM), SSD_CONV_WIDTH ** -0.5)
    ssd_conv_b = nrm(ks[4], (N_SSD_LAYERS, SSD_CONV_DIM), 0.02)
    dt0 = jnp.exp(jax.random.uniform(ks[5], (N_SSD_LAYERS, SSD_N_HEADS), f32,
                                     math.log(1e-3), math.log(1e-1)))
    ssd_dt_bias = dt0 + jnp.log(-jnp.expm1(-dt0))
    ssd_a_log = jnp.log(jax.random.uniform(ks[6], (N_SSD_LAYERS, SSD_N_HEADS), f32, 1.0, 16.0))
    ssd_d = 1.0 + nrm(ks[7], (N_SSD_LAYERS, SSD_N_HEADS), 0.02)
    ssd_norm_w = 1.0 + nrm(ks[8], (N_SSD_LAYERS, SSD_D_INNER), 0.02)
    ssd_w_out = nrm(ks[9], (N_SSD_LAYERS, SSD_D_INNER, D_MODEL), SSD_D_INNER ** -0.5 * DEEPNORM_BETA)
    hgrn_w_in = nrm(ks[10], (N_HGRN_LAYERS, D_MODEL, HGRN_IN_DIM), D_MODEL ** -0.5)
    hgrn_lower_bounds = nrm(ks[11], (N_HGRN_LAYERS, D_MODEL), 0.1)
    hgrn_norm_w = 1.0 + nrm(ks[12], (N_HGRN_LAYERS, HGRN_HEAD_V), 0.02)
    hgrn_w_out = nrm(ks[13], (N_HGRN_LAYERS, D_MODEL, D_MODEL), D_MODEL ** -0.5 * DEEPNORM_BETA)
    ln_g = 1.0 + nrm(ks[14], (DEPTH, 2, D_MODEL), 0.02)
    ln_b = nrm(ks[15], (DEPTH, 2, D_MODEL), 0.02)
    mlp_w1 = nrm(ks[16], (DEPTH, D_MODEL, D_FF), D_MODEL ** -0.5)
    mlp_w2 = nrm(ks[17], (DEPTH, D_FF, D_MODEL), D_FF ** -0.5 * DEEPNORM_BETA)
    ple_w_proj = nrm(ks[18], (DEPTH, PLE_DIM, D_MODEL), PLE_DIM ** -0.5)
    ple_w_gate = nrm(ks[19], (DEPTH, D_MODEL, D_MODEL), D_MODEL ** -0.5)
    return {"x": x, "p": p, "ssd_w_in": ssd_w_in, "ssd_conv_w": ssd_conv_w, "ssd_conv_b": ssd_conv_b,
            "ssd_dt_bias": ssd_dt_bias, "ssd_a_log": ssd_a_log, "ssd_d": ssd_d, "ssd_norm_w": ssd_norm_w,
            "ssd_w_out": ssd_w_out, "hgrn_w_in": hgrn_w_in, "hgrn_lower_bounds": hgrn_lower_bounds,
            "hgrn_norm_w": hgrn_norm_w, "hgrn_w_out": hgrn_w_out, "ln_g": ln_g, "ln_b": ln_b,
            "mlp_w1": mlp_w1, "mlp_w2": mlp_w2, "ple_w_proj": ple_w_proj, "ple_w_gate": ple_w_gate}


def reference(x, p, ssd_w_in, ssd_conv_w, ssd_conv_b, ssd_dt_bias, ssd_a_log, ssd_d, ssd_norm_w,
              ssd_w_out, hgrn_w_in, hgrn_lower_bounds, hgrn_norm_w, hgrn_w_out, ln_g, ln_b,
              mlp_w1, mlp_w2, ple_w_proj, ple_w_gate):
    lbs = jnp.cumsum(jax.nn.softmax(hgrn_lower_bounds.astype(jnp.float32), axis=0), axis=0)
    lbs = lbs - lbs[0:1]
    for i in range(DEPTH):
        j = i // N_MIXERS
        if i % N_MIXERS == 0:
            mix = ssd_mixer(x, ssd_w_in[j], ssd_conv_w[j], ssd_conv_b[j], ssd_dt_bias[j], ssd_a_log[j],
                            ssd_d[j], ssd_norm_w[j], ssd_w_out[j])
        else:
            mix = hgrn2_mixer(x, hgrn_w_in[j], lbs[j], hgrn_norm_w[j], hgrn_w_out[j])
        h = layer_norm(DEEPNORM_ALPHA * x + mix, ln_g[i, 0], ln_b[i, 0])
        h = layer_norm(DEEPNORM_ALPHA * h + sq_relu_mlp(h, mlp_w1[i], mlp_w2[i]), ln_g[i, 1], ln_b[i, 1])
        x = h + jax.nn.sigmoid(h @ ple_w_gate[i]) * (p[i] @ ple_w_proj[i])
    return x
```

```python
import numpy as np
from contextlib import ExitStack
import concourse.bass as bass
import concourse.mybir as mybir
from concourse.bass_utils import run_bass_kernel_spmd

F32, BF16 = mybir.dt.float32, mybir.dt.bfloat16
AF = mybir.ActivationFunctionType
ALU = mybir.AluOpType

D = 1024
DEPTH = 4
ALPHA = (2.0 * DEPTH) ** 0.25
EPS = 1e-5
SAME_ENGINE_SYNC = True


class Tk:
    __slots__ = ("name", "w", "r")

    def __init__(self, name=""):
        self.name = name
        self.w = {}
        self.r = {}


class Eng:
    def __init__(self, name, sem):
        self.name = name
        self.sem = sem
        self.n = 0
        self.waited = {}
        self.ops = []


class Prog:
    def __init__(self, nc, es):
        self.nc, self.es = nc, es
        self.E = {}
        for name in ["pe", "act", "dve", "pool", "sp"]:
            self.E[name] = Eng(name, es.enter_context(nc.semaphore("s_" + name)))
        self.dsems = []

    def dsem(self, name):
        d = Eng("d_" + name, self.es.enter_context(self.nc.semaphore("d_" + name)))
        self.dsems.append(d)
        return d

    def _waits(self, e, reads, writes, skip_same):
        need = {}
        for t in reads:
            for k, v in t.w.items():
                if need.get(k, 0) < v:
                    need[k] = v
        for t in writes:
            for k, v in t.w.items():
                if need.get(k, 0) < v:
                    need[k] = v
            for k, v in t.r.items():
                if need.get(k, 0) < v:
                    need[k] = v
        for k, v in need.items():
            if k is e and skip_same:
                continue
            if e.waited.get(k, 0) >= v:
                continue
            e.waited[k] = v
            e.ops.append(("w", k.sem, v))

    def op(self, en, fn, reads=(), writes=()):
        e = self.E[en]
        self._waits(e, reads, writes, en == "pe" or not SAME_ENGINE_SYNC)
        e.n += 1
        e.ops.append(("i", fn, e.sem, 1))
        for t in reads:
            t.r[e] = e.n
        for t in writes:
            t.w = {e: e.n}
            t.r = {}

    def dma(self, qn, d, out, in_, reads=(), writes=()):
        q = self.E[qn]
        self._waits(q, reads, writes, False)
        d.n += 16
        q.ops.append(("i", lambda eng: eng.dma_start(out=out, in_=in_), d.sem, 16))
        for t in reads:
            t.r[d] = d.n
        for t in writes:
            t.w = {d: d.n}
            t.r = {}

    def barrier(self):
        allk = list(self.E.values()) + self.dsems
        for e in self.E.values():
            for k in allk:
                if k is e or k.n == 0:
                    continue
                if e.waited.get(k, 0) < k.n:
                    e.waited[k] = k.n
                    e.ops.append(("w", k.sem, k.n))

    def replay(self, block):
        def run(e):
            def f(eng):
                for o in e.ops:
                    if o[0] == "w":
                        eng.wait_ge(o[1], o[2])
                    else:
                        o[1](eng).then_inc(o[2], o[3])
            return f
        block.tensor(run(self.E["pe"]))
        block.scalar(run(self.E["act"]))
        block.vector(run(self.E["dve"]))
        block.gpsimd(run(self.E["pool"]))
        block.sync(run(self.E["sp"]))


def A_(out, in_, func, **kw):
    return lambda e: e.activation(out=out, in_=in_, func=func, **kw)


def TT(out, in0, in1, op):
    return lambda e: e.tensor_tensor(out=out, in0=in0, in1=in1, op=op)


def TS(out, in0, s1, s2, op0, op1=None):
    if op1 is None:
        return lambda e: e.tensor_scalar(out=out, in0=in0, scalar1=s1, scalar2=None, op0=op0)
    return lambda e: e.tensor_scalar(out=out, in0=in0, scalar1=s1, scalar2=s2, op0=op0, op1=op1)


def STT(out, in0, sc, in1, op0, op1):
    return lambda e: e.scalar_tensor_tensor(out=out, in0=in0, scalar=sc, in1=in1, op0=op0, op1=op1)


def CP(out, in_):
    return lambda e: e.tensor_copy(out=out, in_=in_)


def MM(out, lhsT, rhs, start=True, stop=True):
    return lambda e: e.matmul(out, lhsT=lhsT, rhs=rhs, start=start, stop=stop)


def TR(out, in_, ident):
    return lambda e: e.transpose(out, in_, ident)


def build(T, layers, final_out=True):
    NSEG = T // 2048
    NT = T // 128
    BLK = 256
    TPB = BLK // 128
    NB = T // BLK
    nc = bass.Bass("TRN2", target_bir_lowering=False)

    def din(name, shape):
        return nc.dram_tensor(name, list(shape), F32, kind="ExternalInput").ap()

    x_d = din("x", [T, D])
    p_d = din("p", [DEPTH, T, 256])
    ssd_w_in = din("ssd_w_in", [2, D, 5152])
    ssd_cw = din("ssd_cw", [2, 128, 24 * 4])
    ssd_cb = din("ssd_cb", [2, 128, 24])
    ssd_dtb = din("ssd_dt_bias", [2, 32])
    ssd_alog = din("ssd_a_log", [2, 32])
    ssd_dd = din("ssd_d", [2, 32])
    ssd_nw = din("ssd_nw", [2, 128, 16])
    ssd_w_out = din("ssd_w_out", [2, 2048, D])
    hg_w_in = din("hgrn_w_in", [2, D, 4096])
    hg_lb = din("hgrn_lb", [2, 128, 8])
    hg_nw = din("hgrn_norm_w", [2, 128])
    hg_w_out = din("hgrn_w_out", [2, D, D])
    ln_g = din("ln_g", [DEPTH, 2, D])
    ln_b = din("ln_b", [DEPTH, 2, D])
    w1_d = din("mlp_w1", [DEPTH, D, 4096])
    w2_d = din("mlp_w2", [DEPTH, 4096, D])
    wp_d = din("ple_w_proj", [DEPTH, 256, D])
    wg_d = din("ple_w_gate", [DEPTH, D, D])
    cst_d = din("cst", [128, 5 * 128 + 512])
    out_d = nc.dram_tensor("out", [T, D], F32, kind="ExternalOutput").ap()
    xs_d = nc.dram_tensor("xscr", [T, D], F32).ap()
    hs_d = nc.dram_tensor("hscr", [T, D], F32).ap()

    es = ExitStack()
    with es:
        P = Prog(nc, es)
        op, dma = P.op, P.dma

        def sb(name, shape, dt=F32):
            return es.enter_context(nc.sbuf_tensor(name, list(shape), dt))

        cst = sb("cst_sb", [128, 5 * 128 + 512])
        k_cst = Tk("cst")
        d_cst = P.dsem("cst")
        dma("sp", d_cst, cst[:], cst_d, writes=[k_cst])
        identF = cst[:, 0:128]
        triU = cst[:, 128:256]
        SU = cst[:, 256:384]
        ones = cst[:, 384:512]
        maskBD = cst[:, 512:640]
        rst = cst[:, 640:1152]
        identB = sb("identB", [128, 128], BF16)
        k_idb = Tk()
        op("dve", CP(identB[:], identF), reads=[k_cst], writes=[k_idb])

        psF = es.enter_context(nc.psum_tensor("psF", [128, 6, 512], F32))
        psB = es.enter_context(nc.psum_tensor("psB", [128, 2, 1024], BF16))
        kF = [Tk("psF%d" % i) for i in range(6)]
        kB = [Tk("psB%d" % i) for i in range(2)]
        rot = {"F": 0, "B": 0}

        def bankF():
            i = rot["F"] % 6
            rot["F"] += 1
            return psF[:, i, :], kF[i]

        def bankB():
            i = rot["B"] % 2
            rot["B"] += 1
            return psB[:, i, :], kB[i]

        ARENA = 80 * 1024
        arena = sb("arena", [128, ARENA // 2], BF16)
        pools = {}
        ar = {"off": 0}

        def carve(shape, dt):
            n = 1
            for d_ in shape[1:]:
                n *= d_
            nb = n * (4 if dt == F32 else 2)
            nb = (nb + 31) // 32 * 32
            off = ar["off"]
            assert off + nb <= ARENA, ("arena overflow", off, nb)
            ar["off"] = off + nb
            a = arena[:, off // 2:(off + nb) // 2]
            if dt == F32:
                a = a.bitcast(F32)
            a = a[:, 0:n]
            if len(shape) == 3:
                a = a.rearrange("p (a b) -> p a b", a=shape[1])
            elif len(shape) == 4:
                a = a.rearrange("p (a b c) -> p a b c", a=shape[1], b=shape[2])
            return a

        def arena_reset():
            pools.clear()
            ar["off"] = 0

        class _T:
            def __init__(self, a):
                self.a = a

            def __getitem__(self, key):
                return self.a[key]

        def tmp(name, shape, dt=F32, bufs=2):
            if name not in pools:
                pools[name] = [[(_T(carve(shape, dt)), Tk(name)) for i in range(bufs)], 0]
            pl = pools[name]
            t, k = pl[0][pl[1] % len(pl[0])]
            pl[1] += 1
            return t, k

        regA = sb("regA", [128, 8 * 5152], BF16)
        regB = sb("regB", [128, 16 * 1024], BF16)
        kXT = [Tk("XT%d" % i) for i in range(16)]
        XTh = {}
        d_w = P.dsem("w")
        d_w2 = P.dsem("w2")
        d_x = [P.dsem("x%d" % i) for i in range(2)]
        d_st = P.dsem("st")
        d_par = P.dsem("par")
        d_mw = [P.dsem("mw%d" % i) for i in range(2)]
        d_pl = P.dsem("pl")
        d_h = [P.dsem("h%d" % i) for i in range(2)]
        d_p = [P.dsem("p%d" % i) for i in range(2)]
        k_xs = [Tk("xs%d" % i) for i in range(NT)]
        k_hs = [Tk("hs%d" % i) for i in range(NT)]

        gb = sb("gb", [128, 2, D])
        k_gb = Tk("gb")

        def load_ln(l, j):
            dma("sp", d_par, gb[:, 0, :], ln_g[l, j:j + 1, :].broadcast_to([128, D]), writes=[k_gb])
            dma("sp", d_par, gb[:, 1, :], ln_b[l, j:j + 1, :].broadcast_to([128, D]), writes=[k_gb])

        def layer_norm(src, k_src, dst_f32, k_dst, want_bf=True):
            st, k_st = tmp("ln_st", [128, 2, 6])
            for i in range(2):
                op("dve", (lambda o, i_: lambda e: e.bn_stats(out=o, in_=i_))(st[:, i, :], src[:, i * 512:(i + 1) * 512]),
                   reads=[k_src], writes=[k_st])
            mv, k_mv = tmp("ln_mv", [128, 4])
            op("dve", (lambda o, i_: lambda e: e.bn_aggr(out=o, in_=i_))(mv[:, 0:2], st[:].rearrange("p a b -> p (a b)")),
               reads=[k_st], writes=[k_mv])
            op("dve", TS(mv[:, 2:3], mv[:, 1:2], EPS, None, ALU.add), reads=[k_mv], writes=[k_mv])
            op("act", A_(mv[:, 2:3], mv[:, 2:3], AF.Sqrt), reads=[k_mv], writes=[k_mv])
            op("dve", (lambda o, i_: lambda e: e.reciprocal(out=o, in_=i_))(mv[:, 2:3], mv[:, 2:3]), reads=[k_mv], writes=[k_mv])
            op("dve", TS(mv[:, 3:4], mv[:, 0:1], mv[:, 2:3], -1.0, ALU.mult, ALU.mult), reads=[k_mv], writes=[k_mv])
            op("act", A_(dst_f32, src, AF.Identity, scale=mv[:, 2:3], bias=mv[:, 3:4]), reads=[k_src, k_mv], writes=[k_dst])
            op("dve", TT(dst_f32, dst_f32, gb[:, 0, :], ALU.mult), reads=[k_dst, k_gb], writes=[k_dst])
            op("dve", TT(dst_f32, dst_f32, gb[:, 1, :], ALU.add), reads=[k_dst, k_gb], writes=[k_dst])
            if not want_bf:
                return None, None
            hb, k_hbb = tmp("ln_hb", [128, D], BF16)
            op("act", A_(hb[:], dst_f32, AF.Copy), reads=[k_dst], writes=[k_hbb])
            return hb, k_hbb

        def to_XT(hb, k_hbb, tl):
            pb, kpb = bankB()
            for c in range(8):
                op("pe", TR(pb[:, c * 128:(c + 1) * 128], hb[:, c * 128:(c + 1) * 128], identB[:]),
                   reads=[k_hbb, k_idb], writes=[kpb])
            op("act", A_(XTh["XT"][:, :, tl * 128:(tl + 1) * 128], pb.rearrange("p (c t) -> p c t", c=8), AF.Copy),
               reads=[kpb], writes=[kXT[tl]])

        def mixer_epilogue(l, t, psm, kpsm, xsrc, k_xsrc):
            xt, k_xt = tmp("ep_x", [128, D], F32, bufs=1)
            dma("sp", d_x[t % 2], xt[:], xsrc[t * 128:(t + 1) * 128, :], reads=[k_xsrc[t]] if k_xsrc else [], writes=[k_xt])
            for nb in range(2):
                sl = slice(nb * 512, (nb + 1) * 512)
                op("dve", STT(xt[:, sl], xt[:, sl], ALPHA, psm[nb], ALU.mult, ALU.add),
                   reads=[k_xt, kpsm[nb]], writes=[k_xt])
            layer_norm(xt[:], k_xt, xt[:], k_xt, want_bf=False)
            op("act", A_(xt[:], xt[:], AF.Copy, scale=ALPHA), reads=[k_xt], writes=[k_xt])
            dma("sp", d_st, hs_d[t * 128:(t + 1) * 128, :], xt[:], reads=[k_xt], writes=[k_hs[t]])

        def load_xblock(src_d, k_src_tiles, blk):
            XTb, k_XTb = tmp("XTb", [128, 8, BLK], BF16, bufs=1)
            for a in range(TPB):
                t = blk * TPB + a
                xt, k_xt = tmp("ep_x", [128, D], F32, bufs=1)
                dma("sp", d_x[t % 2], xt[:], src_d[t * 128:(t + 1) * 128, :],
                    reads=[k_src_tiles[t]] if k_src_tiles else [], writes=[k_xt])
                xb, k_xb = tmp("xb", [128, D], BF16, bufs=1)
                op("dve", CP(xb[:], xt[:]), reads=[k_xt], writes=[k_xb])
                pb, kpb = bankB()
                for c in range(8):
                    op("pe", TR(pb[:, c * 128:(c + 1) * 128], xb[:, c * 128:(c + 1) * 128], identB[:]),
                       reads=[k_xb, k_idb], writes=[kpb])
                op("act", A_(XTb[:, :, a * 128:(a + 1) * 128], pb.rearrange("p (c t) -> p c t", c=8), AF.Copy),
                   reads=[kpb], writes=[k_XTb])
            return XTb, k_XTb

        def ssd_layer(l, src_d, k_src):
            j = l // 2
            Win = regA[:].rearrange("p (k n) -> p k n", k=8)
            Wout = regB[:].rearrange("p (k n) -> p k n", k=16)
            k_win, k_wout = Tk("win"), Tk("wout")
            wv = ssd_w_in[j].rearrange("(k p) n -> p k n", p=128)
            for kc in range(8):
                for (a, b) in ((0, 2048), (2048, 4096), (4096, 5152)):
                    dma("pool", d_w, Win[:, kc, a:b], wv[:, kc, a:b], writes=[k_win])
            wo = ssd_w_out[j].rearrange("(k p) n -> p k n", p=128)
            for kc in range(16):
                dma("pool", d_w2, Wout[:, kc, :], wo[:, kc, :], writes=[k_wout])
            par, _ = tmp("ssd_par", [128, 24 * 4 + 24 + 16 + 32 * 3], F32, bufs=1)
            k_par = Tk("par")
            cw = par[:, 0:96].rearrange("p (c k) -> p c k", k=4)
            cb = par[:, 96:120]
            nw = par[:, 120:136]
            dtb = par[:, 136:168]
            an = par[:, 168:200]
            dd = par[:, 200:232]
            dma("sp", d_par, par[:, 0:96], ssd_cw[j], writes=[k_par])
            dma("sp", d_par, cb, ssd_cb[j], writes=[k_par])
            dma("sp", d_par, nw, ssd_nw[j], writes=[k_par])
            dma("sp", d_par, dtb, ssd_dtb[j:j + 1, :].broadcast_to([128, 32]), writes=[k_par])
            dma("sp", d_par, an, ssd_alog[j:j + 1, :].broadcast_to([128, 32]), writes=[k_par])
            dma("sp", d_par, dd, ssd_dd[j:j + 1, :].broadcast_to([128, 32]), writes=[k_par])
            op("act", A_(an, an, AF.Exp), reads=[k_par], writes=[k_par])
            op("dve", TS(an, an, -1.0, None, ALU.mult), reads=[k_par], writes=[k_par])
            load_ln(l, 0)
            S, _ = tmp("ssd_S", [128, 2048], F32, bufs=1)
            Sbf, _ = tmp("ssd_Sbf", [128, 2048], BF16, bufs=1)
            halo, _ = tmp("ssd_halo", [128, 24, 3], F32, bufs=1)
            k_S, k_Sbf, k_halo = Tk("S"), Tk("Sbf"), Tk("halo")
            op("dve", lambda e: e.memset(S[:], 0.0), writes=[k_S])
            op("dve", lambda e: e.memset(Sbf[:], 0.0), writes=[k_Sbf])
            op("dve", lambda e: e.memset(halo[:], 0.0), writes=[k_halo])

            for blk in range(NB):
                XTb, k_XTb = load_xblock(src_d, k_src, blk)
                xbcT, k_xbcT = tmp("xbcT", [128, 24, BLK], BF16, bufs=1)
                for cc in range(24):
                    ps, kps = bankF()
                    for kc in range(8):
                        op("pe", MM(ps[:, 0:BLK], Win[:, kc, 2048 + cc * 128:2048 + (cc + 1) * 128], XTb[:, kc, :], kc == 0, kc == 7),
                           reads=[k_win, k_XTb], writes=[kps])
                    u, k_u = tmp("u", [128, BLK + 3])
                    op("act", A_(u[:, 0:3], halo[:, cc, :], AF.Copy), reads=[k_halo], writes=[k_u])
                    op("act", A_(u[:, 3:BLK + 3], ps[:, 0:BLK], AF.Copy), reads=[kps], writes=[k_u])
                    op("act", A_(halo[:, cc, :], u[:, BLK:BLK + 3], AF.Copy), reads=[k_u], writes=[k_halo])
                    acc, k_acc = tmp("acc", [128, BLK])
                    op("dve", TS(acc[:], u[:, 0:BLK], cw[:, cc, 0:1], cb[:, cc:cc + 1], ALU.mult, ALU.add),
                       reads=[k_u, k_par], writes=[k_acc])
                    for k in range(1, 4):
                        op("dve", STT(acc[:], u[:, k:k + BLK], cw[:, cc, k:k + 1], acc[:], ALU.mult, ALU.add),
                           reads=[k_u, k_par, k_acc], writes=[k_acc])
                    op("act", A_(xbcT[:, cc, :], acc[:], AF.Silu), reads=[k_acc], writes=[k_xbcT])
                for ti in range(TPB):
                    t = blk * TPB + ti
                    cs = slice(ti * 128, (ti + 1) * 128)
                    ps, kps = bankF()
                    for kc in range(8):
                        op("pe", MM(ps[:, 0:32], XTb[:, kc, cs], Win[:, kc, 5120:5152], kc == 0, kc == 7),
                           reads=[k_win, k_XTb], writes=[kps])
                    sm, k_sm = tmp("sm", [128, 8, 32])
                    dt_, da, acs, dif, ea, dte, cd, f2 = [sm[:, i, :] for i in range(8)]
                    op("dve", TT(dt_, ps[:, 0:32], dtb, ALU.add), reads=[kps, k_par], writes=[k_sm])
                    op("act", A_(dt_, dt_, AF.Exp), reads=[k_sm], writes=[k_sm])
                    op("act", A_(dt_, dt_, AF.Ln, bias=1.0), reads=[k_sm], writes=[k_sm])
                    op("dve", TT(da, dt_, an, ALU.mult), reads=[k_sm, k_par], writes=[k_sm])
                    ps2, kps2 = bankF()
                    op("pe", MM(ps2[:, 0:32], triU, da), reads=[k_cst, k_sm], writes=[kps2])
                    op("pe", MM(ps2[:, 32:64], ones, da), reads=[k_cst, k_sm], writes=[kps2])
                    op("act", A_(acs, ps2[:, 0:32], AF.Copy), reads=[kps2], writes=[k_sm])
                    op("dve", TT(dif, ps2[:, 32:64], acs, ALU.subtract), reads=[kps2, k_sm], writes=[k_sm])
                    op("act", A_(ea, acs, AF.Exp), reads=[k_sm], writes=[k_sm])
                    op("act", A_(dte, dif, AF.Exp), reads=[k_sm], writes=[k_sm])
                    op("act", A_(cd, ps2[:, 32:64], AF.Exp), reads=[kps2], writes=[k_sm])
                    op("dve", TT(f2, dt_, dte, ALU.mult), reads=[k_sm], writes=[k_sm])
                    op("dve", (lambda o, i_: lambda e: e.reciprocal(out=o, in_=i_))(dif, dt_), reads=[k_sm], writes=[k_sm])
                    op("dve", TT(dif, dif, dd, ALU.mult), reads=[k_sm, k_par], writes=[k_sm])
                    xdt, k_xdt = tmp("xdt", [128, 2048], BF16, bufs=1)
                    xw, k_xw = tmp("xw", [128, 2048], BF16, bufs=1)
                    for hf in range(2):
                        pb, kpb = bankB()
                        for c in range(8):
                            op("pe", TR(pb[:, c * 128:(c + 1) * 128], xbcT[:, hf * 8 + c, cs], identB[:]),
                               reads=[k_xbcT, k_idb], writes=[kpb])
                        sl = slice(hf * 1024, (hf + 1) * 1024)
                        pv = pb.rearrange("p (h q) -> p h q", q=64)
                        hsl = slice(hf * 16, (hf + 1) * 16)
                        op("dve", TT(xdt[:, sl].rearrange("p (h q) -> p h q", q=64), pv,
                                     dt_[:, hsl].unsqueeze(2).broadcast_to([128, 16, 64]), ALU.mult),
                           reads=[kpb, k_sm], writes=[k_xdt])
                        op("dve", TT(xw[:, sl].rearrange("p (h q) -> p h q", q=64), pv,
                                     f2[:, hsl].unsqueeze(2).broadcast_to([128, 16, 64]), ALU.mult),
                           reads=[kpb, k_sm], writes=[k_xw])
                    btok, k_btok = tmp("btok", [128, 512], BF16, bufs=1)
                    pb, kpb = bankB()
                    for g in range(4):
                        op("pe", TR(pb[:, g * 128:(g + 1) * 128], xbcT[:, 16 + g, cs], identB[:]),
                           reads=[k_xbcT, k_idb], writes=[kpb])
                    op("act", A_(btok[:], pb[:, 0:512], AF.Copy), reads=[kpb], writes=[k_btok])
                    ps3, kps3 = bankF()
                    for g in range(4):
                        op("pe", MM(ps3[:, g * 128:(g + 1) * 128], xbcT[:, 16 + g, cs], xbcT[:, 20 + g, cs]),
                           reads=[k_xbcT], writes=[kps3])
                    cbm, k_cbm = tmp("cbm", [128, 4, 128], F32, bufs=1)
                    op("dve", TT(cbm[:], ps3.rearrange("p (g l) -> p g l", g=4),
                                 triU.unsqueeze(1).broadcast_to([128, 4, 128]), ALU.mult),
                       reads=[kps3, k_cst], writes=[k_cbm])
                    def make_MT(g_):
                      MT, k_MT = tmp("MT", [128, 8, 128], BF16, bufs=2)
                      for q in (2 * g_, 2 * g_ + 1):
                        R, k_R = tmp("R", [128, 4, 128], F32, bufs=1)
                        op("dve", TT(R[:], triU.unsqueeze(1).broadcast_to([128, 4, 128]),
                                     da[:, q * 4:(q + 1) * 4].unsqueeze(2).broadcast_to([128, 4, 128]), ALU.mult),
                           reads=[k_cst, k_sm], writes=[k_R])
                        ps4, kps4 = bankF()
                        op("pe", MM(ps4, SU, R[:].rearrange("p h l -> p (h l)")), reads=[k_cst, k_R], writes=[kps4])
                        Ex, k_Ex = tmp("Ex", [128, 4, 128], F32, bufs=1)
                        op("act", A_(Ex[:].rearrange("p h l -> p (h l)"), ps4, AF.Exp), reads=[kps4], writes=[k_Ex])
                        op("dve", TT(MT[:, (q % 2) * 4:(q % 2 + 1) * 4, :], Ex[:],
                                     cbm[:, g_:g_ + 1, :].broadcast_to([128, 4, 128]), ALU.mult),
                           reads=[k_Ex, k_cbm], writes=[k_MT])
                      return MT, k_MT
                    yn, k_yn = tmp("yn", [128, 2048], BF16, bufs=1)
                    ss, k_ss = tmp("ss", [128, 8])
                    for g in range(4):
                        gs = slice(g * 512, (g + 1) * 512)
                        psD, kpsD = bankF()
                        MT, k_MT = make_MT(g)
                        for r in range(8):
                            h = g * 8 + r
                            op("pe", MM(psD[:, r * 64:(r + 1) * 64], MT[:, r, :], xdt[:, h * 64:(h + 1) * 64]),
                               reads=[k_MT, k_xdt], writes=[kpsD])
                        psO, kpsO = bankF()
                        op("pe", MM(psO, xbcT[:, 20 + g, cs], Sbf[:, gs]), reads=[k_xbcT, k_Sbf], writes=[kpsO])
                        psZ, kpsZ = bankF()
                        for kc in range(8):
                            op("pe", MM(psZ, XTb[:, kc, cs], Win[:, kc, gs], kc == 0, kc == 7),
                               reads=[k_win, k_XTb], writes=[kpsZ])
                        y, k_y = tmp("y", [128, 512], F32, bufs=1)
                        op("dve", TT(y[:].rearrange("p (h q) -> p h q", q=64), psO.rearrange("p (h q) -> p h q", q=64),
                                     ea[:, g * 8:(g + 1) * 8].unsqueeze(2).broadcast_to([128, 8, 64]), ALU.mult),
                           reads=[kpsO, k_sm], writes=[k_y])
                        op("dve", TT(y[:], y[:], psD, ALU.add), reads=[k_y, kpsD], writes=[k_y])
                        y2, k_y2 = tmp("y2", [128, 512], F32, bufs=1)
                        op("dve", TT(y2[:].rearrange("p (h q) -> p h q", q=64), xdt[:, gs].rearrange("p (h q) -> p h q", q=64),
                                     dif[:, g * 8:(g + 1) * 8].unsqueeze(2).broadcast_to([128, 8, 64]), ALU.mult),
                           reads=[k_xdt, k_sm], writes=[k_y2])
                        op("dve", TT(y[:], y[:], y2[:], ALU.add), reads=[k_y, k_y2], writes=[k_y])
                        sz, k_sz = tmp("sz", [128, 512], F32, bufs=1)
                        op("act", A_(sz[:], psZ, AF.Silu), reads=[kpsZ], writes=[k_sz])
                        op("dve", TT(y[:], y[:], sz[:], ALU.mult), reads=[k_y, k_sz], writes=[k_y])
                        op("act", A_(y2[:], y[:], AF.Square, accum_out=ss[:, g:g + 1]), reads=[k_y], writes=[k_y2, k_ss])
                        op("dve", TS(ss[:, 4 + g:5 + g], ss[:, g:g + 1], 1.0 / 512, EPS, ALU.mult, ALU.add), reads=[k_ss], writes=[k_ss])
                        op("act", A_(ss[:, 4 + g:5 + g], ss[:, 4 + g:5 + g], AF.Sqrt), reads=[k_ss], writes=[k_ss])
                        op("dve", (lambda o, i_: lambda e: e.reciprocal(out=o, in_=i_))(ss[:, 4 + g:5 + g], ss[:, 4 + g:5 + g]),
                           reads=[k_ss], writes=[k_ss])
                        op("dve", TS(yn[:, gs], y[:], ss[:, 4 + g:5 + g], None, ALU.mult), reads=[k_y, k_ss], writes=[k_yn])
                    for g in range(4):
                        gs = slice(g * 512, (g + 1) * 512)
                        psU, kpsU = bankF()
                        op("pe", MM(psU, btok[:, g * 128:(g + 1) * 128], xw[:, gs]), reads=[k_btok, k_xw], writes=[kpsU])
                        op("dve", TT(S[:, gs].rearrange("p (h q) -> p h q", q=64), S[:, gs].rearrange("p (h q) -> p h q", q=64),
                                     cd[:, g * 8:(g + 1) * 8].unsqueeze(2).broadcast_to([128, 8, 64]), ALU.mult),
                           reads=[k_S, k_sm], writes=[k_S])
                        op("dve", TT(S[:, gs], S[:, gs], psU, ALU.add), reads=[k_S, kpsU], writes=[k_S])
                    op("act", A_(Sbf[:], S[:], AF.Copy), reads=[k_S], writes=[k_Sbf])
                    yT, k_yT = tmp("yT", [128, 16, 128], BF16, bufs=1)
                    for hf in range(2):
                        pb, kpb = bankB()
                        for c in range(8):
                            op("pe", TR(pb[:, c * 128:(c + 1) * 128], yn[:, (hf * 8 + c) * 128:(hf * 8 + c + 1) * 128], identB[:]),
                               reads=[k_yn, k_idb], writes=[kpb])
                        op("dve", TT(yT[:, hf * 8:(hf + 1) * 8, :], pb.rearrange("p (c t) -> p c t", c=8),
                                     nw[:, hf * 8:(hf + 1) * 8].unsqueeze(2).broadcast_to([128, 8, 128]), ALU.mult),
                           reads=[kpb, k_par], writes=[k_yT])
                    psm, kpsm = [], []
                    for nb in range(2):
                        pm, kpm = bankF()
                        for c in range(16):
                            op("pe", MM(pm, yT[:, c, :], Wout[:, c, nb * 512:(nb + 1) * 512], c == 0, c == 15),
                               reads=[k_yT, k_wout], writes=[kpm])
                        psm.append(pm)
                        kpsm.append(kpm)
                    mixer_epilogue(l, t, psm, kpsm, src_d, k_src)

        def hgrn_layer(l, src_d, k_src):
            j = l // 2
            Win = regA[:, 0:8 * 4096].rearrange("p (k n) -> p k n", k=8)
            Wout = regB[:, 0:8 * 1024].rearrange("p (k n) -> p k n", k=8)
            k_win, k_wout = Tk("win"), Tk("wout")
            wv = hg_w_in[j].rearrange("(k p) n -> p k n", p=128)
            for kc in range(8):
                for (a, b) in ((0, 2048), (2048, 4096)):
                    dma("pool", d_w, Win[:, kc, a:b], wv[:, kc, a:b], writes=[k_win])
            wo = hg_w_out[j].rearrange("(k p) n -> p k n", p=128)
            for kc in range(8):
                dma("pool", d_w2, Wout[:, kc, :], wo[:, kc, :], writes=[k_wout])
            par, _ = tmp("hg_par", [128, 8 * 5 + 128], F32, bufs=1)
            k_par = Tk("hpar")
            lb0, lb1, lbv, oml, noml = [par[:, i * 8:(i + 1) * 8] for i in range(5)]
            nwb = par[:, 40:168]
            dma("sp", d_par, lb0, hg_lb[0], writes=[k_par])
            dma("sp", d_par, lb1, hg_lb[1], writes=[k_par])
            dma("sp", d_par, nwb, hg_nw[j:j + 1, :].broadcast_to([128, 128]), writes=[k_par])
            if j == 0:
                op("dve", lambda e: e.memset(lbv, 0.0), writes=[k_par])
            else:
                op("dve", TT(lbv, lb1, lb0, ALU.subtract), reads=[k_par], writes=[k_par])
                op("act", A_(lbv, lbv, AF.Sigmoid), reads=[k_par], writes=[k_par])
            op("dve", TS(oml, lbv, -1.0, 1.0, ALU.mult, ALU.add), reads=[k_par], writes=[k_par])
            op("dve", TS(noml, oml, -1.0, None, ALU.mult), reads=[k_par], writes=[k_par])
            load_ln(l, 0)
            S, _ = tmp("hg_S", [128, 8, 128], F32, bufs=1)
            Sbf = [tmp("hg_Sbf", [128, 8, 128], BF16, bufs=2)[0] for i in range(2)]
            k_S, k_Sbf = Tk("hS"), [Tk("hSbf0"), Tk("hSbf1")]
            op("dve", lambda e: e.memset(S[:], 0.0), writes=[k_S])
            op("dve", lambda e: e.memset(Sbf[0][:], 0.0), writes=[k_Sbf[0]])

            for blk in range(NB):
                XTb, k_XTb = load_xblock(src_d, k_src, blk)
                qt, k_qt = tmp("qt", [128, 8, BLK], BF16, bufs=1)
                kt, k_kt = tmp("kt", [128, 8, BLK], BF16, bufs=1)
                qc, k_qc = tmp("qc", [128, 8, BLK], BF16, bufs=1)
                ketok, k_ketok = tmp("ketok", [128, TPB, 8, 128], BF16, bufs=1)
                ebl, k_ebl = tmp("ebl", [128, 8, BLK // 64], F32, bufs=1)
                for h in range(8):
                    psq, kpsq = bankF()
                    for kc in range(8):
                        op("pe", MM(psq[:, 0:BLK], Win[:, kc, h * 128:(h + 1) * 128], XTb[:, kc, :], kc == 0, kc == 7),
                           reads=[k_win, k_XTb], writes=[kpsq])
                    psf, kpsf = bankF()
                    for kc in range(8):
                        op("pe", MM(psf[:, 0:BLK], Win[:, kc, 1024 + h * 128:1024 + (h + 1) * 128], XTb[:, kc, :], kc == 0, kc == 7),
                           reads=[k_win, k_XTb], writes=[kpsf])
                    sg, k_sg = tmp("sg", [128, BLK], F32, bufs=1)
                    op("act", A_(sg[:], psf[:, 0:BLK], AF.Sigmoid), reads=[kpsf], writes=[k_sg])
                    ff, k_ff = tmp("ff", [128, BLK], F32, bufs=1)
                    op("dve", TS(ff[:], sg[:], oml[:, h:h + 1], lbv[:, h:h + 1], ALU.mult, ALU.add), reads=[k_sg, k_par], writes=[k_ff])
                    op("act", A_(ff[:], ff[:], AF.Ln), reads=[k_ff], writes=[k_ff])
                    bb, k_bb = tmp("bb", [128, BLK], F32, bufs=1)
                    op("dve", (lambda o, d0, d1: lambda e: e.tensor_tensor_scan(out=o, data0=d0, data1=d1, initial=0.0,
                                                                                 op0=ALU.mult, op1=ALU.add))(bb[:], rst[:, 0:BLK], ff[:]),
                       reads=[k_ff, k_cst], writes=[k_bb])
                    kk, k_kk = tmp("kk", [128, BLK], F32, bufs=1)
                    op("dve", TS(kk[:], sg[:], noml[:, h:h + 1], oml[:, h:h + 1], ALU.mult, ALU.add), reads=[k_sg, k_par], writes=[k_kk])
                    eb, k_eb = tmp("eb", [128, BLK], F32, bufs=1)
                    op("act", A_(eb[:], bb[:], AF.Exp), reads=[k_bb], writes=[k_eb])
                    op("dve", TT(qt[:, h, :], psq[:, 0:BLK], eb[:], ALU.mult), reads=[kpsq, k_eb], writes=[k_qt])
                    op("act", A_(ebl[:, h, :], eb[:, 63:BLK:64], AF.Copy), reads=[k_eb], writes=[k_ebl])
                    nbm, k_nbm = tmp("nbm", [128, BLK // 64], F32, bufs=1)
                    op("dve", TS(nbm[:], bb[:, 31:BLK:64], -1.0, None, ALU.mult), reads=[k_bb], writes=[k_nbm])
                    ebc, k_ebc = tmp("ebc", [128, BLK], F32, bufs=1)
                    enb, k_enb = tmp("enb", [128, BLK], F32, bufs=1)
                    for c in range(BLK // 64):
                        csl = slice(c * 64, (c + 1) * 64)
                        op("act", A_(ebc[:, csl], bb[:, csl], AF.Exp, bias=nbm[:, c:c + 1]), reads=[k_bb, k_nbm], writes=[k_ebc])
                        op("act", A_(enb[:, csl], bb[:, csl], AF.Exp, scale=-1.0, bias=bb[:, c * 64 + 31:c * 64 + 32]),
                           reads=[k_bb], writes=[k_enb])
                    op("dve", TT(qc[:, h, :], psq[:, 0:BLK], ebc[:], ALU.mult), reads=[kpsq, k_ebc], writes=[k_qc])
                    op("dve", TT(kt[:, h, :], kk[:], enb[:], ALU.mult), reads=[k_kk, k_enb], writes=[k_kt])
                    ee, k_ee = tmp("ee", [128, BLK], F32, bufs=1)
                    for c in range(BLK // 64):
                        op("act", A_(ee[:, c * 64:(c + 1) * 64], bb[:, c * 64:(c + 1) * 64], AF.Exp, scale=-1.0,
                                     bias=bb[:, c * 64 + 63:c * 64 + 64]), reads=[k_bb], writes=[k_ee])
                    ke, k_ke = tmp("ke", [128, BLK], BF16)
                    op("dve", TT(ke[:], kk[:], ee[:], ALU.mult), reads=[k_kk, k_ee], writes=[k_ke])
                    pb, kpb = bankB()
                    for ti in range(TPB):
                        op("pe", TR(pb[:, ti * 128:(ti + 1) * 128], ke[:, ti * 128:(ti + 1) * 128], identB[:]),
                           reads=[k_ke, k_idb], writes=[kpb])
                    op("act", A_(ketok[:, :, h, :], pb[:, 0:TPB * 128].rearrange("p (a k) -> p a k", a=TPB), AF.Copy),
                       reads=[kpb], writes=[k_ketok])
                for ti in range(TPB):
                    t = blk * TPB + ti
                    cs = slice(ti * 128, (ti + 1) * 128)
                    vb, k_vb = tmp("vb", [128, 1024], BF16, bufs=1)
                    sgt, k_sgt = tmp("sgt", [128, 1024], F32, bufs=1)
                    for nb in range(2):
                        ps, kps = bankF()
                        for kc in range(8):
                            op("pe", MM(ps, XTb[:, kc, cs], Win[:, kc, 2048 + nb * 512:2048 + (nb + 1) * 512], kc == 0, kc == 7),
                               reads=[k_win, k_XTb], writes=[kps])
                        op("act", A_(vb[:, nb * 512:(nb + 1) * 512], ps, AF.Copy), reads=[kps], writes=[k_vb])
                        ps, kps = bankF()
                        for kc in range(8):
                            op("pe", MM(ps, XTb[:, kc, cs], Win[:, kc, 3072 + nb * 512:3072 + (nb + 1) * 512], kc == 0, kc == 7),
                               reads=[k_win, k_XTb], writes=[kps])
                        op("act", A_(sgt[:, nb * 512:(nb + 1) * 512], ps, AF.Silu), reads=[kps], writes=[k_sgt])
                    scm, k_scm = tmp("scm", [128, 8, 128], BF16, bufs=1)
                    for hf in range(2):
                        ps, kps = bankF()
                        for hh in range(4):
                            h = hf * 4 + hh
                            op("pe", MM(ps[:, hh * 128:(hh + 1) * 128], kt[:, h, cs], qc[:, h, cs]), reads=[k_kt, k_qc], writes=[kps])
                        scl, k_scl = tmp("scl", [128, 512], F32, bufs=1)
                        op("dve", TS(scl[:], ps, -1e30, 1e30, ALU.max, ALU.min), reads=[kps], writes=[k_scl])
                        op("dve", TT(scm[:, hf * 4:(hf + 1) * 4, :], scl[:].rearrange("p (h l) -> p h l", h=4),
                                     maskBD.unsqueeze(1).broadcast_to([128, 4, 128]), ALU.mult),
                           reads=[k_scl, k_cst], writes=[k_scm])

                    def state_update(half, dst):
                        rs = slice(half * 64, (half + 1) * 64)
                        c = ti * 2 + half
                        for hf in range(2):
                            ps, kps = bankF()
                            for hh in range(4):
                                h = hf * 4 + hh
                                op("pe", MM(ps[:, hh * 128:(hh + 1) * 128], ketok[rs, ti, h, :], vb[rs, h * 128:(h + 1) * 128]),
                                   reads=[k_ketok, k_vb], writes=[kps])
                            hs_ = slice(hf * 4, (hf + 1) * 4)
                            op("dve", TT(S[:, hs_, :], S[:, hs_, :], ebl[:, hs_, c:c + 1].broadcast_to([128, 4, 128]), ALU.mult),
                               reads=[k_S, k_ebl], writes=[k_S])
                            op("dve", TT(S[:, hs_, :], S[:, hs_, :], ps.rearrange("p (h v) -> p h v", h=4), ALU.add),
                               reads=[k_S, kps], writes=[k_S])
                        op("act", A_(Sbf[dst][:], S[:], AF.Copy), reads=[k_S], writes=[k_Sbf[dst]])

                    state_update(0, 1)
                    pso = []
                    for hf in range(2):
                        ps, kps = bankF()
                        for hh in range(4):
                            h = hf * 4 + hh
                            o_ = ps[:, hh * 128:(hh + 1) * 128]
                            op("pe", MM(o_, scm[:, h, :], vb[:, h * 128:(h + 1) * 128], True, False), reads=[k_scm, k_vb], writes=[kps])
                            op("pe", MM(o_[0:64, :], qt[:, h, ti * 128:ti * 128 + 64], Sbf[0][:, h, :], False, False),
                               reads=[k_qt, k_Sbf[0]], writes=[kps])
                            op("pe", MM(o_[64:128, :], qt[:, h, ti * 128 + 64:ti * 128 + 128], Sbf[1][:, h, :], False, True),
                               reads=[k_qt, k_Sbf[1]], writes=[kps])
                        pso.append((ps, kps))
                    state_update(1, 0)
                    on, k_on = tmp("on", [128, 8, 128], F32, bufs=1)
                    ssq, k_ssq = tmp("ssq", [128, 16])
                    junk, k_junk = tmp("junk", [128, 128])
                    for hf in range(2):
                        ps, kps = pso[hf]
                        for hh in range(4):
                            h = hf * 4 + hh
                            op("act", A_(junk[:], ps[:, hh * 128:(hh + 1) * 128], AF.Square, accum_out=ssq[:, h:h + 1]),
                               reads=[kps], writes=[k_junk, k_ssq])
                    op("dve", TS(ssq[:, 8:16], ssq[:, 0:8], 1.0 / 128, EPS, ALU.mult, ALU.add), reads=[k_ssq], writes=[k_ssq])
                    op("act", A_(ssq[:, 8:16], ssq[:, 8:16], AF.Sqrt), reads=[k_ssq], writes=[k_ssq])
                    op("dve", (lambda o, i_: lambda e: e.reciprocal(out=o, in_=i_))(ssq[:, 8:16], ssq[:, 8:16]), reads=[k_ssq], writes=[k_ssq])
                    for hf in range(2):
                        ps, kps = pso[hf]
                        hs_ = slice(hf * 4, (hf + 1) * 4)
                        op("dve", TT(on[:, hs_, :], ps.rearrange("p (h v) -> p h v", h=4),
                                     ssq[:, 8 + hf * 4:12 + hf * 4].unsqueeze(2).broadcast_to([128, 4, 128]), ALU.mult),
                           reads=[kps, k_ssq], writes=[k_on])
                    op("dve", TT(on[:], on[:], nwb.unsqueeze(1).broadcast_to([128, 8, 128]), ALU.mult), reads=[k_on, k_par], writes=[k_on])
                    onb, k_onb = tmp("onb", [128, 1024], BF16, bufs=1)
                    op("dve", TT(onb[:], on[:].rearrange("p h v -> p (h v)"), sgt[:], ALU.mult), reads=[k_on, k_sgt], writes=[k_onb])
                    onT, k_onT = tmp("onT", [128, 8, 128], BF16, bufs=1)
                    pb, kpb = bankB()
                    for c in range(8):
                        op("pe", TR(pb[:, c * 128:(c + 1) * 128], onb[:, c * 128:(c + 1) * 128], identB[:]), reads=[k_onb, k_idb], writes=[kpb])
                    op("act", A_(onT[:], pb.rearrange("p (c t) -> p c t", c=8), AF.Copy), reads=[kpb], writes=[k_onT])
                    psm, kpsm = [], []
                    for nb in range(2):
                        pm, kpm = bankF()
                        for c in range(8):
                            op("pe", MM(pm, onT[:, c, :], Wout[:, c, nb * 512:(nb + 1) * 512], c == 0, c == 7),
                               reads=[k_onT, k_wout], writes=[kpm])
                        psm.append(pm)
                        kpsm.append(kpm)
                    mixer_epilogue(l, t, psm, kpsm, src_d, k_src)

        def mlp_ple_segment(l, seg, dst_d, k_dst, last):
            Xacc = regA[:, 0:16 * 2048].bitcast(F32).rearrange("p (t d) -> p t d", t=16)
            kX = [Tk("Xacc%d" % i) for i in range(16)]
            XTh["XT"] = tmp("XT", [128, 8, 2048], BF16, bufs=1)[0]
            XT = XTh["XT"]
            for tl in range(16):
                t = seg * 16 + tl
                dma("sp", d_h[tl % 2], Xacc[:, tl, :], hs_d[t * 128:(t + 1) * 128, :], reads=[k_hs[t]], writes=[kX[tl]])
                hb, k_hbb = tmp("ln_hb", [128, D], BF16)
                op("act", A_(hb[:], Xacc[:, tl, :], AF.Copy, scale=1.0 / ALPHA), reads=[kX[tl]], writes=[k_hbb])
                to_XT(hb, k_hbb, tl)
            load_ln(l, 1)
            wbuf = [regB[:, i * 8192:(i + 1) * 8192] for i in range(2)]
            k_wb = [Tk("mwb0"), Tk("mwb1")]
            w1v = w1_d[l].rearrange("(k p) n -> p k n", p=128)
            w2v = w2_d[l].rearrange("(k p) n -> p k n", p=128)
            for e8 in range(8):
                wb = wbuf[e8 % 2]
                kwb = k_wb[e8 % 2]
                w1e = wb[:, 0:4096].rearrange("p (k n) -> p k n", k=8)
                w2e = wb[:, 4096:8192].rearrange("p (k n) -> p k n", k=4)
                for kc in range(8):
                    dma("pool", d_mw[e8 % 2], w1e[:, kc, :], w1v[:, kc, e8 * 512:(e8 + 1) * 512], writes=[kwb])
                for fc in range(4):
                    dma("pool", d_mw[e8 % 2], w2e[:, fc, :], w2v[:, e8 * 4 + fc, :], writes=[kwb])
                for tb in range(4):
                    hid, k_hid = tmp("hid", [128, 4, 512], BF16, bufs=2)
                    for fc in range(4):
                        ps, kps = bankF()
                        for kc in range(8):
                            op("pe", MM(ps, w1e[:, kc, fc * 128:(fc + 1) * 128], XT[:, kc, tb * 512:(tb + 1) * 512], kc == 0, kc == 7),
                               reads=[kwb] + kXT[tb * 4:tb * 4 + 4], writes=[kps])
                        rl, k_rl = tmp("rl", [128, 512])
                        op("act", A_(rl[:], ps, AF.Relu), reads=[kps], writes=[k_rl])
                        op("dve", TT(hid[:, fc, :], rl[:], rl[:], ALU.mult), reads=[k_rl], writes=[k_hid])
                    for ti in range(4):
                        tl = tb * 4 + ti
                        for nb in range(2):
                            ps, kps = bankF()
                            for fc in range(4):
                                op("pe", MM(ps, hid[:, fc, ti * 128:(ti + 1) * 128], w2e[:, fc, nb * 512:(nb + 1) * 512], fc == 0, fc == 3),
                                   reads=[k_hid, kwb], writes=[kps])
                            xa = Xacc[:, tl, nb * 512:(nb + 1) * 512]
                            op("dve", TT(xa, xa, ps, ALU.add), reads=[kX[tl], kps], writes=[kX[tl]])
            Wg = regB[:, 0:8192].rearrange("p (k n) -> p k n", k=8)
            Wp = regB[:, 8192:8192 + 2048].rearrange("p (k n) -> p k n", k=2)
            wgv = wg_d[l].rearrange("(k p) n -> p k n", p=128)
            wpv = wp_d[l].rearrange("(k p) n -> p k n", p=128)
            for kc in range(8):
                dma("pool", d_pl, Wg[:, kc, :], wgv[:, kc, :], writes=[k_wb[0]])
            for kc in range(2):
                dma("pool", d_pl, Wp[:, kc, :], wpv[:, kc, :], writes=[k_wb[1]])
            for tl in range(16):
                t = seg * 16 + tl
                hb, k_hbb = layer_norm(Xacc[:, tl, :], kX[tl], Xacc[:, tl, :], kX[tl])
                to_XT(hb, k_hbb, tl)
                pt, k_pt = tmp("pt", [128, 256])
                dma("sp", d_p[tl % 2], pt[:], p_d[l, t * 128:(t + 1) * 128, :], writes=[k_pt])
                ptb, k_ptb = tmp("ptb", [128, 256], BF16)
                op("dve", CP(ptb[:], pt[:]), reads=[k_pt], writes=[k_ptb])
                pb, kpb = bankB()
                for c in range(2):
                    op("pe", TR(pb[:, c * 128:(c + 1) * 128], ptb[:, c * 128:(c + 1) * 128], identB[:]), reads=[k_ptb, k_idb], writes=[kpb])
                pT, k_pT = tmp("pT", [128, 2, 128], BF16)
                op("act", A_(pT[:], pb[:, 0:256].rearrange("p (c t) -> p c t", c=2), AF.Copy), reads=[kpb], writes=[k_pT])
                xn, k_xn = tmp("xn", [128, D])
                for nb in range(2):
                    psg, kpsg = bankF()
                    for kc in range(8):
                        op("pe", MM(psg, XT[:, kc, tl * 128:(tl + 1) * 128], Wg[:, kc, nb * 512:(nb + 1) * 512], kc == 0, kc == 7),
                           reads=[kXT[tl], k_wb[0]], writes=[kpsg])
                    psp, kpsp = bankF()
                    for kc in range(2):
                        op("pe", MM(psp, pT[:, kc, :], Wp[:, kc, nb * 512:(nb + 1) * 512], kc == 0, kc == 1),
                           reads=[k_pT, k_wb[1]], writes=[kpsp])
                    sgm, k_sgm = tmp("sgm", [128, 512])
                    op("act", A_(sgm[:], psg, AF.Sigmoid), reads=[kpsg], writes=[k_sgm])
                    op("dve", TT(sgm[:], sgm[:], psp, ALU.mult), reads=[k_sgm, kpsp], writes=[k_sgm])
                    op("dve", TT(xn[:, nb * 512:(nb + 1) * 512], sgm[:], Xacc[:, tl, nb * 512:(nb + 1) * 512], ALU.add),
                       reads=[k_sgm, kX[tl]], writes=[k_xn])
                dma("sp", d_st, dst_d[t * 128:(t + 1) * 128, :], xn[:], reads=[k_xn], writes=[k_dst[t]])

        src_d, k_src = x_d, None
        k_out = [Tk("out%d" % i) for i in range(NT)]
        for li, l in enumerate(layers):
            last = li == len(layers) - 1
            if l % 2 == 0:
                ssd_layer(l, src_d, k_src)
            else:
                hgrn_layer(l, src_d, k_src)
            P.barrier()
            arena_reset()
            dst_d, k_dst = (out_d, k_out) if last else (xs_d, k_xs)
            for seg in range(NSEG):
                mlp_ple_segment(l, seg, dst_d, k_dst, last)
                P.barrier()
                arena_reset()
            src_d, k_src = xs_d, k_xs
        P.barrier()
        block = es.enter_context(nc.Block())
        P.replay(block)
    return nc


def make_cst():
    c = np.zeros((128, 5 * 128 + 512), np.float32)
    i = np.arange(128)
    c[:, 0:128] = np.eye(128)
    c[:, 128:256] = (i[:, None] <= i[None, :])
    c[:, 256:384] = (i[:, None] > i[None, :])
    c[:, 384:512] = 1.0
    c[:, 512:640] = (i[:, None] <= i[None, :]) & ((i[:, None] // 64) == (i[None, :] // 64))
    r = np.ones(512, np.float32)
    r[0::64] = 0.0
    c[:, 640:1152] = r[None, :]
    return c


def prep_weights(inp):
    f = lambda a: np.ascontiguousarray(np.asarray(a, dtype=np.float32))
    w = {}
    for k in ["ssd_w_in", "ssd_dt_bias", "ssd_a_log", "ssd_d", "ssd_w_out", "hgrn_w_in", "hgrn_norm_w", "hgrn_w_out",
              "ln_g", "ln_b", "mlp_w1", "mlp_w2", "ple_w_proj", "ple_w_gate"]:
        w[k] = f(inp[k])
    cw = f(inp["ssd_conv_w"])
    w["ssd_cw"] = f(cw.reshape(2, 4, 24, 128).transpose(0, 3, 2, 1).reshape(2, 128, 96))
    w["ssd_cb"] = f(f(inp["ssd_conv_b"]).reshape(2, 24, 128).transpose(0, 2, 1))
    w["ssd_nw"] = f(f(inp["ssd_norm_w"]).reshape(2, 16, 128).transpose(0, 2, 1))
    w["hgrn_lb"] = f(f(inp["hgrn_lower_bounds"]).reshape(2, 8, 128).transpose(0, 2, 1))
    w["cst"] = make_cst()
    return w


def run(inp, T, layers, ncores=8):
    w = prep_weights(inp)
    x = np.asarray(inp["x"], np.float32)
    p = np.asarray(inp["p"], np.float32)
    nc = build(T, layers)
    in_maps = []
    for c in range(ncores):
        b = c if c < 2 else 0
        m = dict(w)
        m["x"] = np.ascontiguousarray(x[b, :T])
        m["p"] = np.ascontiguousarray(p[:, b, :T])
        in_maps.append(m)
    res = run_bass_kernel_spmd(nc, in_maps, core_ids=list(range(ncores)))
    return np.stack([res.results[0]["out"], res.results[1]["out"]], axis=0)


def kernel(**inputs):
    return run(inputs, 8192, [0, 1, 2, 3]).astype(np.float32)
```

```python
import numpy as np
from contextlib import ExitStack
import concourse.bass as bass
import concourse.mybir as mybir
from concourse.bass_utils import run_bass_kernel_spmd

F32, BF16 = mybir.dt.float32, mybir.dt.bfloat16
AF = mybir.ActivationFunctionType
ALU = mybir.AluOpType

D = 1024
DEPTH = 4
ALPHA = (2.0 * DEPTH) ** 0.25
EPS = 1e-5
SAME_ENGINE_SYNC = True


class Tk:
    __slots__ = ("name", "w", "r")

    def __init__(self, name=""):
        self.name = name
        self.w = {}
        self.r = {}


class Eng:
    def __init__(self, name, sem):
        self.name = name
        self.sem = sem
        self.n = 0
        self.waited = {}
        self.ops = []


class Prog:
    def __init__(self, nc, es):
        self.nc, self.es = nc, es
        self.E = {}
        for name in ["pe", "act", "dve", "pool", "sp"]:
            self.E[name] = Eng(name, es.enter_context(nc.semaphore("s_" + name)))
        self.dsems = []
        self.skip = False
        self.dset = set()

    def dsem(self, name):
        d = Eng("d_" + name, self.es.enter_context(self.nc.semaphore("d_" + name)))
        self.dsems.append(d)
        self.dset.add(d)
        return d

    def _waits(self, e, reads, writes, skip_same):
        need = {}
        for t in reads:
            for k, v in t.w.items():
                if need.get(k, 0) < v:
                    need[k] = v
        for t in writes:
            for k, v in t.w.items():
                if need.get(k, 0) < v:
                    need[k] = v
            for k, v in t.r.items():
                if need.get(k, 0) < v:
                    need[k] = v
        for k, v in need.items():
            if k in self.dset:
                v = k.n
            if k is e and skip_same:
                continue
            if e.waited.get(k, 0) >= v:
                continue
            e.waited[k] = v
            e.ops.append(("w", k.sem, v))

    def op(self, en, fn, reads=(), writes=()):
        if self.skip:
            return
        e = self.E[en]
        self._waits(e, reads, writes, en == "pe" or not SAME_ENGINE_SYNC)
        e.n += 1
        e.ops.append(("i", fn, e.sem, 1))
        for t in reads:
            t.r[e] = e.n
        for t in writes:
            t.w = {e: e.n}
            t.r = {}

    def cc(self, d, fn, reads=(), writes=()):
        q = self.E["pool"]
        self._waits(q, reads, writes, False)
        d.n += 1
        q.ops.append(("i", fn, d.sem, 1))
        for t in reads:
            t.r[d] = d.n
        for t in writes:
            t.w = {d: d.n}
            t.r = {}

    def dma(self, qn, d, out, in_, reads=(), writes=()):
        if self.skip:
            return
        q = self.E[qn]
        self._waits(q, reads, writes, False)
        d.n += 16
        q.ops.append(("i", lambda eng: eng.dma_start(out=out, in_=in_), d.sem, 16))
        for t in reads:
            t.r[d] = d.n
        for t in writes:
            t.w[d] = d.n
            t.r = {}

    def barrier(self):
        allk = list(self.E.values()) + self.dsems
        for e in self.E.values():
            for k in allk:
                if k is e or k.n == 0:
                    continue
                if e.waited.get(k, 0) < k.n:
                    e.waited[k] = k.n
                    e.ops.append(("w", k.sem, k.n))

    def replay(self, block):
        def run(e):
            def f(eng):
                for o in e.ops:
                    if o[0] == "w":
                        eng.wait_ge(o[1], o[2])
                    else:
                        o[1](eng).then_inc(o[2], o[3])
            return f
        block.tensor(run(self.E["pe"]))
        block.scalar(run(self.E["act"]))
        block.vector(run(self.E["dve"]))
        block.gpsimd(run(self.E["pool"]))
        block.sync(run(self.E["sp"]))


def A_(out, in_, func, **kw):
    return lambda e: e.activation(out=out, in_=in_, func=func, **kw)


def TT(out, in0, in1, op):
    return lambda e: e.tensor_tensor(out=out, in0=in0, in1=in1, op=op)


def TS(out, in0, s1, s2, op0, op1=None):
    if op1 is None:
        return lambda e: e.tensor_scalar(out=out, in0=in0, scalar1=s1, scalar2=None, op0=op0)
    return lambda e: e.tensor_scalar(out=out, in0=in0, scalar1=s1, scalar2=s2, op0=op0, op1=op1)


def STT(out, in0, sc, in1, op0, op1):
    return lambda e: e.scalar_tensor_tensor(out=out, in0=in0, scalar=sc, in1=in1, op0=op0, op1=op1)


def CP(out, in_):
    return lambda e: e.tensor_copy(out=out, in_=in_)


def MM(out, lhsT, rhs, start=True, stop=True):
    return lambda e: e.matmul(out, lhsT=lhsT, rhs=rhs, start=start, stop=stop)


def TR(out, in_, ident):
    return lambda e: e.transpose(out, in_, ident)


def build(T, layers, MULTI=False):
    NSEG = T // 2048
    NT = T // 128
    BLK = 256
    TPB = BLK // 128
    NB = T // BLK
    nc = bass.Bass("TRN2", target_bir_lowering=False)

    def din(name, shape):
        return nc.dram_tensor(name, list(shape), F32, kind="ExternalInput").ap()

    x_d = din("x", [T, D])
    p_d = din("p", [DEPTH, T, 256])
    ssd_w_in = din("ssd_w_in", [2, D, 5152])
    ssd_cw = din("ssd_cw", [2, 128, 24 * 4])
    ssd_cb = din("ssd_cb", [2, 128, 24])
    ssd_dtb = din("ssd_dt_bias", [2, 32])
    ssd_alog = din("ssd_a_log", [2, 32])
    ssd_dd = din("ssd_d", [2, 32])
    ssd_nw = din("ssd_nw", [2, 128, 16])
    ssd_w_out = din("ssd_w_out", [2, 2048, D])
    hg_w_in = din("hgrn_w_in", [2, D, 4096])
    hg_lb = din("hgrn_lb", [2, 128, 8])
    hg_nw = din("hgrn_norm_w", [2, 128])
    hg_w_out = din("hgrn_w_out", [2, D, D])
    ln_g = din("ln_g", [DEPTH, 2, D])
    ln_b = din("ln_b", [DEPTH, 2, D])
    w1_d = din("mlp_w1", [DEPTH, D, 4096])
    w2_d = din("mlp_w2", [DEPTH, 4096, D])
    wp_d = din("ple_w_proj", [DEPTH, 256, D])
    wg_d = din("ple_w_gate", [DEPTH, D, D])
    cst_d = din("cst", [128, 5 * 128 + 512])
    msk_d = din("msk", [128, 18])
    xhT_d = din("xhT", [128, 24])
    out_d = nc.dram_tensor("out", [T, D], F32, kind="ExternalOutput").ap()
    xs_d = nc.dram_tensor("xscr", [T, D], F32).ap()
    hs_d = nc.dram_tensor("hscr", [T, D], F32).ap()

    es = ExitStack()
    with es:
        P = Prog(nc, es)
        op, dma = P.op, P.dma

        def sb(name, shape, dt=F32):
            return es.enter_context(nc.sbuf_tensor(name, list(shape), dt))

        cst = sb("cst_sb", [128, 5 * 128 + 512])
        k_cst = Tk("cst")
        d_cst = P.dsem("cst")
        dma("sp", d_cst, cst[:], cst_d, writes=[k_cst])
        identF = cst[:, 0:128]
        triU = cst[:, 128:256]
        SU = cst[:, 256:384]
        ones = cst[:, 384:512]
        maskBD = cst[:, 512:640]
        rst = cst[:, 640:1152]
        msk = sb("msk_sb", [128, 18])
        xh_sb = sb("xh_sb", [128, 24])
        k_msk, k_xh = Tk("msk"), Tk("xh")
        dma("sp", d_cst, msk[:], msk_d, writes=[k_msk])
        dma("sp", d_cst, xh_sb[:], xhT_d, writes=[k_xh])
        oh6, pr6, sel6 = msk[:, 0:6], msk[:, 6:12], msk[:, 12:18]
        identB = sb("identB", [128, 128], BF16)
        k_idb = Tk()
        op("dve", CP(identB[:], identF), reads=[k_cst], writes=[k_idb])

        psF = es.enter_context(nc.psum_tensor("psF", [128, 6, 512], F32))
        psB = es.enter_context(nc.psum_tensor("psB", [128, 2, 1024], BF16))
        kF = [Tk("psF%d" % i) for i in range(6)]
        kB = [Tk("psB%d" % i) for i in range(2)]
        rot = {"F": 0, "B": 0}

        def bankF():
            i = rot["F"] % 6
            rot["F"] += 1
            return psF[:, i, :], kF[i]

        def bankB():
            i = rot["B"] % 2
            rot["B"] += 1
            return psB[:, i, :], kB[i]

        ARENA = 80 * 1024
        arena = sb("arena", [128, ARENA // 2], BF16)
        pools = {}
        ar = {"off": 0}

        def carve(shape, dt):
            n = 1
            for d_ in shape[1:]:
                n *= d_
            nb = n * (4 if dt == F32 else 2)
            nb = (nb + 31) // 32 * 32
            off = ar["off"]
            assert off + nb <= ARENA, ("arena overflow", off, nb)
            ar["off"] = off + nb
            a = arena[:, off // 2:(off + nb) // 2]
            if dt == F32:
                a = a.bitcast(F32)
            a = a[:, 0:n]
            if len(shape) == 3:
                a = a.rearrange("p (a b) -> p a b", a=shape[1])
            elif len(shape) == 4:
                a = a.rearrange("p (a b c) -> p a b c", a=shape[1], b=shape[2])
            return a

        def arena_reset():
            pools.clear()
            ar["off"] = 0

        def arena_mark():
            return (ar["off"], set(pools.keys()))

        def arena_release(m):
            for k in list(pools.keys()):
                if k not in m[1]:
                    del pools[k]
            ar["off"] = m[0]

        cur = {"phaseA": False}

        def SK(b):
            P.skip = bool(b) and cur["phaseA"]

        class _T:
            def __init__(self, a):
                self.a = a

            def __getitem__(self, key):
                return self.a[key]

        def tmp(name, shape, dt=F32, bufs=2):
            if name not in pools:
                pools[name] = [[(_T(carve(shape, dt)), Tk(name)) for i in range(bufs)], 0]
            pl = pools[name]
            t, k = pl[0][pl[1] % len(pl[0])]
            pl[1] += 1
            return t, k

        regA = sb("regA", [128, 8 * 5152], BF16)
        regB = sb("regB", [128, 16 * 1024], BF16)
        kXT = [Tk("XT%d" % i) for i in range(16)]
        XTh = {}
        d_w = P.dsem("w")
        d_w2 = P.dsem("w2")
        d_x = [P.dsem("x%d" % i) for i in range(2)]
        d_st = P.dsem("st")
        d_par = P.dsem("par")
        d_mw = [P.dsem("mw%d" % i) for i in range(2)]
        d_pl = P.dsem("pl")
        d_h = [P.dsem("h%d" % i) for i in range(2)]
        d_p = [P.dsem("p%d" % i) for i in range(2)]
        k_xs = [Tk("xs%d" % i) for i in range(NT)]
        k_hs = [Tk("hs%d" % i) for i in range(NT)]

        gb = sb("gb", [128, 2, D])
        k_gb = Tk("gb")

        def load_ln(l, j):
            dma("sp", d_par, gb[:, 0, :], ln_g[l, j:j + 1, :].broadcast_to([128, D]), writes=[k_gb])
            dma("sp", d_par, gb[:, 1, :], ln_b[l, j:j + 1, :].broadcast_to([128, D]), writes=[k_gb])

        def layer_norm(src, k_src, dst_f32, k_dst, want_bf=True):
            st, k_st = tmp("ln_st", [128, 2, 6])
            for i in range(2):
                op("dve", (lambda o, i_: lambda e: e.bn_stats(out=o, in_=i_))(st[:, i, :], src[:, i * 512:(i + 1) * 512]),
                   reads=[k_src], writes=[k_st])
            mv, k_mv = tmp("ln_mv", [128, 4])
            op("dve", (lambda o, i_: lambda e: e.bn_aggr(out=o, in_=i_))(mv[:, 0:2], st[:].rearrange("p a b -> p (a b)")),
               reads=[k_st], writes=[k_mv])
            op("dve", TS(mv[:, 2:3], mv[:, 1:2], EPS, None, ALU.add), reads=[k_mv], writes=[k_mv])
            op("act", A_(mv[:, 2:3], mv[:, 2:3], AF.Sqrt), reads=[k_mv], writes=[k_mv])
            op("dve", (lambda o, i_: lambda e: e.reciprocal(out=o, in_=i_))(mv[:, 2:3], mv[:, 2:3]), reads=[k_mv], writes=[k_mv])
            op("dve", TS(mv[:, 3:4], mv[:, 0:1], mv[:, 2:3], -1.0, ALU.mult, ALU.mult), reads=[k_mv], writes=[k_mv])
            op("act", A_(dst_f32, src, AF.Identity, scale=mv[:, 2:3], bias=mv[:, 3:4]), reads=[k_src, k_mv], writes=[k_dst])
            op("dve", TT(dst_f32, dst_f32, gb[:, 0, :], ALU.mult), reads=[k_dst, k_gb], writes=[k_dst])
            op("dve", TT(dst_f32, dst_f32, gb[:, 1, :], ALU.add), reads=[k_dst, k_gb], writes=[k_dst])
            if not want_bf:
                return None, None
            hb, k_hbb = tmp("ln_hb", [128, D], BF16)
            op("act", A_(hb[:], dst_f32, AF.Copy), reads=[k_dst], writes=[k_hbb])
            return hb, k_hbb

        def to_XT(hb, k_hbb, tl):
            pb, kpb = bankB()
            for c in range(8):
                op("pe", TR(pb[:, c * 128:(c + 1) * 128], hb[:, c * 128:(c + 1) * 128], identB[:]),
                   reads=[k_hbb, k_idb], writes=[kpb])
            op("act", A_(XTh["XT"][:, :, tl * 128:(tl + 1) * 128], pb.rearrange("p (c t) -> p c t", c=8), AF.Copy),
               reads=[kpb], writes=[kXT[tl]])

        def mixer_epilogue(l, t, psm, kpsm, xsrc, k_xsrc):
            xt, k_xt = tmp("ep_x", [128, D], F32, bufs=1)
            dma("sp", d_x[t % 2], xt[:], xsrc[t * 128:(t + 1) * 128, :], reads=[k_xsrc[t]] if k_xsrc else [], writes=[k_xt])
            for nb in range(2):
                sl = slice(nb * 512, (nb + 1) * 512)
                op("dve", STT(xt[:, sl], xt[:, sl], ALPHA, psm[nb], ALU.mult, ALU.add),
                   reads=[k_xt, kpsm[nb]], writes=[k_xt])
            layer_norm(xt[:], k_xt, xt[:], k_xt, want_bf=False)
            op("act", A_(xt[:], xt[:], AF.Copy, scale=ALPHA), reads=[k_xt], writes=[k_xt])
            dma("sp", d_st, hs_d[t * 128:(t + 1) * 128, :], xt[:], reads=[k_xt], writes=[k_hs[t]])

        d_xc = [P.dsem("xc0"), P.dsem("xc1")]
        d_cc = P.dsem("cc")

        def allreduce(bin_, bout, k_bin, k_bout):
            P.cc(d_cc, lambda g: g.collective_compute("AllReduce", ALU.add, replica_groups=[list(range(8))],
                                                      ins=[bin_.ap().opt()], outs=[bout.ap().opt()]),
                 reads=[k_bin], writes=[k_bout])

        def exchange(tag, S2d, k_S, W_S, Dt, k_D, W_D, viewS, bcD):
            W = W_S + W_D
            bin_ = nc.dram_tensor("xin_" + tag, [6 * 128, W], F32)
            bout = nc.dram_tensor("xout_" + tag, [6 * 128, W], F32)
            k_bin, k_bout = Tk("bin"), Tk("bout")
            for j in range(6):
                stg, k_stg = tmp("xstg", [128, W], F32, bufs=2)
                op("dve", TS(stg[:, 0:W_S], S2d, oh6[:, j:j + 1], None, ALU.mult), reads=[k_S, k_msk], writes=[k_stg])
                op("dve", TS(stg[:, W_S:W], Dt, oh6[:, j:j + 1], None, ALU.mult), reads=[k_D, k_msk], writes=[k_stg])
                dma("sp", d_xc[0], bin_.ap()[j * 128:(j + 1) * 128, :], stg[:], reads=[k_stg], writes=[k_bin])
            allreduce(bin_, bout, k_bin, k_bout)
            op("dve", lambda e: e.memset(S2d, 0.0), reads=[k_bin], writes=[k_S])
            for j in range(6):
                stg, k_stg = tmp("xstg", [128, W], F32, bufs=2)
                dma("pool", d_xc[1], stg[:], bout.ap()[j * 128:(j + 1) * 128, :], reads=[k_bout], writes=[k_stg])
                de, k_de = tmp("xde", [128, W_D], F32, bufs=2)
                op("dve", TS(de[:], stg[:, W_S:W], 1.0, pr6[:, j:j + 1], ALU.subtract, ALU.mult), reads=[k_stg, k_msk], writes=[k_de])
                op("dve", TS(de[:], de[:], 1.0, None, ALU.add), reads=[k_de], writes=[k_de])
                op("dve", TT(viewS(S2d), viewS(S2d), bcD(de[:]), ALU.mult), reads=[k_S, k_de], writes=[k_S])
                op("dve", STT(S2d, stg[:, 0:W_S], pr6[:, j:j + 1], S2d, ALU.mult, ALU.add), reads=[k_stg, k_S, k_msk], writes=[k_S])

        def halo_exchange(tag, xl, k_xl):
            bin_ = nc.dram_tensor("hin_" + tag, [6 * 128, 24], F32)
            bout = nc.dram_tensor("hout_" + tag, [6 * 128, 24], F32)
            k_bin, k_bout = Tk("hbin"), Tk("hbout")
            stg, k_stg = tmp("hstg", [128, 6, 24], F32, bufs=1)
            for j in range(6):
                op("dve", TS(stg[:, j, :], xl, oh6[:, j:j + 1], None, ALU.mult), reads=[k_xl, k_msk], writes=[k_stg])
            dma("sp", d_xc[0], bin_.ap().rearrange("(j p) w -> p j w", p=128), stg[:], reads=[k_stg], writes=[k_bin])
            allreduce(bin_, bout, k_bin, k_bout)
            stg2, k_stg2 = tmp("hstg2", [128, 6, 24], F32, bufs=1)
            dma("pool", d_xc[1], stg2[:], bout.ap().rearrange("(j p) w -> p j w", p=128), reads=[k_bout], writes=[k_stg2])
            op("dve", lambda e: e.memset(xh_sb[:], 0.0), writes=[k_xh])
            for j in range(6):
                op("dve", STT(xh_sb[:], stg2[:, j, :], sel6[:, j:j + 1], xh_sb[:], ALU.mult, ALU.add),
                   reads=[k_stg2, k_xh, k_msk], writes=[k_xh])

        def load_xblock(src_d, k_src_tiles, blk):
            XTb, k_XTb = tmp("XTb", [128, 8, BLK], BF16, bufs=1)
            for a in range(TPB):
                t = blk * TPB + a
                xt, k_xt = tmp("ep_x", [128, D], F32, bufs=1)
                dma("sp", d_x[t % 2], xt[:], src_d[t * 128:(t + 1) * 128, :],
                    reads=[k_src_tiles[t]] if k_src_tiles else [], writes=[k_xt])
                xb, k_xb = tmp("xb", [128, D], BF16, bufs=1)
                op("dve", CP(xb[:], xt[:]), reads=[k_xt], writes=[k_xb])
                pb, kpb = bankB()
                for c in range(8):
                    op("pe", TR(pb[:, c * 128:(c + 1) * 128], xb[:, c * 128:(c + 1) * 128], identB[:]),
                       reads=[k_xb, k_idb], writes=[kpb])
                op("act", A_(XTb[:, :, a * 128:(a + 1) * 128], pb.rearrange("p (c t) -> p c t", c=8), AF.Copy),
                   reads=[kpb], writes=[k_XTb])
            return XTb, k_XTb

        def ssd_layer(l, src_d, k_src):
            j = l // 2
            Win = regA[:].rearrange("p (k n) -> p k n", k=8)
            Wout = regB[:].rearrange("p (k n) -> p k n", k=16)
            k_win, k_wout = Tk("win"), Tk("wout")
            wv = ssd_w_in[j].rearrange("(k p) n -> p k n", p=128)
            for kc in range(8):
                for (a, b) in ((0, 2048), (2048, 4096), (4096, 5152)):
                    dma("pool", d_w, Win[:, kc, a:b], wv[:, kc, a:b], writes=[k_win])
            wo = ssd_w_out[j].rearrange("(k p) n -> p k n", p=128)
            for kc in range(16):
                dma("pool", d_w2, Wout[:, kc, :], wo[:, kc, :], writes=[k_wout])
            par, _ = tmp("ssd_par", [128, 24 * 4 + 24 + 16 + 32 * 3], F32, bufs=1)
            k_par = Tk("par")
            cw = par[:, 0:96].rearrange("p (c k) -> p c k", k=4)
            cb = par[:, 96:120]
            nw = par[:, 120:136]
            dtb = par[:, 136:168]
            an = par[:, 168:200]
            dd = par[:, 200:232]
            dma("sp", d_par, par[:, 0:96], ssd_cw[j], writes=[k_par])
            dma("sp", d_par, cb, ssd_cb[j], writes=[k_par])
            dma("sp", d_par, nw, ssd_nw[j], writes=[k_par])
            dma("sp", d_par, dtb, ssd_dtb[j:j + 1, :].broadcast_to([128, 32]), writes=[k_par])
            dma("sp", d_par, an, ssd_alog[j:j + 1, :].broadcast_to([128, 32]), writes=[k_par])
            dma("sp", d_par, dd, ssd_dd[j:j + 1, :].broadcast_to([128, 32]), writes=[k_par])
            op("act", A_(an, an, AF.Exp), reads=[k_par], writes=[k_par])
            op("dve", TS(an, an, -1.0, None, ALU.mult), reads=[k_par], writes=[k_par])
            load_ln(l, 0)
            S, _ = tmp("ssd_S", [128, 2048], F32, bufs=1)
            Sbf, _ = tmp("ssd_Sbf", [128, 2048], BF16, bufs=1)
            halo, _ = tmp("ssd_halo", [128, 24, 3], F32, bufs=1)
            k_S, k_Sbf, k_halo = Tk("S"), Tk("Sbf"), Tk("halo")
            op("dve", lambda e: e.memset(S[:], 0.0), writes=[k_S])
            op("dve", lambda e: e.memset(Sbf[:], 0.0), writes=[k_Sbf])

            halo0, _ = tmp("ssd_halo0", [128, 24, 3], F32, bufs=1)
            totsum, _ = tmp("ssd_totsum", [128, 64], F32, bufs=1)
            k_tot = Tk("tot")
            dtot, _ = tmp("ssd_dtot", [128, 32], F32, bufs=1)
            k_dtot = Tk("dtot")
            k_h0 = Tk("halo0")
            if MULTI:
                xhb, k_xhb = tmp("xhb", [128, 8, 3], BF16, bufs=1)
                op("dve", CP(xhb[:], xh_sb[:].rearrange("p (k t) -> p k t", k=8)), reads=[k_xh], writes=[k_xhb])
                psh, kpsh = bankF()
                for cc in range(24):
                    for kc in range(8):
                        op("pe", MM(psh[:, cc * 3:(cc + 1) * 3], Win[:, kc, 2048 + cc * 128:2048 + (cc + 1) * 128], xhb[:, kc, :], kc == 0, kc == 7),
                           reads=[k_win, k_xhb], writes=[kpsh])
                op("act", A_(halo0[:].rearrange("p c t -> p (c t)"), psh[:, 0:72], AF.Copy), reads=[kpsh], writes=[k_h0])
            else:
                op("dve", lambda e: e.memset(halo0[:], 0.0), writes=[k_h0])
            mark = arena_mark()

            def run_phase(phaseA):
                cur["phaseA"] = phaseA
                op("act", A_(halo[:], halo0[:], AF.Copy), reads=[k_h0], writes=[k_halo])
                if phaseA:
                    op("dve", lambda e: e.memset(S[:], 0.0), writes=[k_S])
                    op("dve", lambda e: e.memset(totsum[:], 0.0), writes=[k_tot])
                for blk in range(NB):
                    XTb, k_XTb = load_xblock(src_d, k_src, blk)
                    xbcT, k_xbcT = tmp("xbcT", [128, 24, BLK], BF16, bufs=1)
                    for cc in range(24):
                        SK(cc >= 20)
                        ps, kps = bankF()
                        for kc in range(8):
                            op("pe", MM(ps[:, 0:BLK], Win[:, kc, 2048 + cc * 128:2048 + (cc + 1) * 128], XTb[:, kc, :], kc == 0, kc == 7),
                               reads=[k_win, k_XTb], writes=[kps])
                        u, k_u = tmp("u", [128, BLK + 3])
                        op("act", A_(u[:, 0:3], halo[:, cc, :], AF.Copy), reads=[k_halo], writes=[k_u])
                        op("act", A_(u[:, 3:BLK + 3], ps[:, 0:BLK], AF.Copy), reads=[kps], writes=[k_u])
                        op("act", A_(halo[:, cc, :], u[:, BLK:BLK + 3], AF.Copy), reads=[k_u], writes=[k_halo])
                        acc, k_acc = tmp("acc", [128, BLK])
                        op("dve", TS(acc[:], u[:, 0:BLK], cw[:, cc, 0:1], cb[:, cc:cc + 1], ALU.mult, ALU.add),
                           reads=[k_u, k_par], writes=[k_acc])
                        for k in range(1, 4):
                            op("dve", STT(acc[:], u[:, k:k + BLK], cw[:, cc, k:k + 1], acc[:], ALU.mult, ALU.add),
                               reads=[k_u, k_par, k_acc], writes=[k_acc])
                        op("act", A_(xbcT[:, cc, :], acc[:], AF.Silu), reads=[k_acc], writes=[k_xbcT])
                    SK(False)
                    for ti in range(TPB):
                        t = blk * TPB + ti
                        cs = slice(ti * 128, (ti + 1) * 128)
                        ps, kps = bankF()
                        for kc in range(8):
                            op("pe", MM(ps[:, 0:32], XTb[:, kc, cs], Win[:, kc, 5120:5152], kc == 0, kc == 7),
                               reads=[k_win, k_XTb], writes=[kps])
                        sm, k_sm = tmp("sm", [128, 8, 32])
                        dt_, da, acs, dif, ea, dte, cd, f2 = [sm[:, i, :] for i in range(8)]
                        op("dve", TT(dt_, ps[:, 0:32], dtb, ALU.add), reads=[kps, k_par], writes=[k_sm])
                        op("act", A_(dt_, dt_, AF.Exp), reads=[k_sm], writes=[k_sm])
                        op("act", A_(dt_, dt_, AF.Ln, bias=1.0), reads=[k_sm], writes=[k_sm])
                        op("dve", TT(da, dt_, an, ALU.mult), reads=[k_sm, k_par], writes=[k_sm])
                        ps2, kps2 = bankF()
                        op("pe", MM(ps2[:, 0:32], triU, da), reads=[k_cst, k_sm], writes=[kps2])
                        op("pe", MM(ps2[:, 32:64], ones, da), reads=[k_cst, k_sm], writes=[kps2])
                        op("act", A_(acs, ps2[:, 0:32], AF.Copy), reads=[kps2], writes=[k_sm])
                        op("dve", TT(dif, ps2[:, 32:64], acs, ALU.subtract), reads=[kps2, k_sm], writes=[k_sm])
                        op("act", A_(ea, acs, AF.Exp), reads=[k_sm], writes=[k_sm])
                        op("act", A_(dte, dif, AF.Exp), reads=[k_sm], writes=[k_sm])
                        op("act", A_(cd, ps2[:, 32:64], AF.Exp), reads=[kps2], writes=[k_sm])
                        if phaseA:
                            op("dve", TT(totsum[:, 0:32], totsum[:, 0:32], ps2[:, 32:64], ALU.add), reads=[k_tot, kps2], writes=[k_tot])
                        op("dve", TT(f2, dt_, dte, ALU.mult), reads=[k_sm], writes=[k_sm])
                        op("dve", (lambda o, i_: lambda e: e.reciprocal(out=o, in_=i_))(dif, dt_), reads=[k_sm], writes=[k_sm])
                        op("dve", TT(dif, dif, dd, ALU.mult), reads=[k_sm, k_par], writes=[k_sm])
                        xdt, k_xdt = tmp("xdt", [128, 2048], BF16, bufs=1)
                        xw, k_xw = tmp("xw", [128, 2048], BF16, bufs=1)
                        for hf in range(2):
                            pb, kpb = bankB()
                            for c in range(8):
                                op("pe", TR(pb[:, c * 128:(c + 1) * 128], xbcT[:, hf * 8 + c, cs], identB[:]),
                                   reads=[k_xbcT, k_idb], writes=[kpb])
                            sl = slice(hf * 1024, (hf + 1) * 1024)
                            pv = pb.rearrange("p (h q) -> p h q", q=64)
                            hsl = slice(hf * 16, (hf + 1) * 16)
                            SK(True)
                            op("dve", TT(xdt[:, sl].rearrange("p (h q) -> p h q", q=64), pv,
                                         dt_[:, hsl].unsqueeze(2).broadcast_to([128, 16, 64]), ALU.mult),
                               reads=[kpb, k_sm], writes=[k_xdt])
                            SK(False)
                            op("dve", TT(xw[:, sl].rearrange("p (h q) -> p h q", q=64), pv,
                                         f2[:, hsl].unsqueeze(2).broadcast_to([128, 16, 64]), ALU.mult),
                               reads=[kpb, k_sm], writes=[k_xw])
                        btok, k_btok = tmp("btok", [128, 512], BF16, bufs=1)
                        pb, kpb = bankB()
                        for g in range(4):
                            op("pe", TR(pb[:, g * 128:(g + 1) * 128], xbcT[:, 16 + g, cs], identB[:]),
                               reads=[k_xbcT, k_idb], writes=[kpb])
                        op("act", A_(btok[:], pb[:, 0:512], AF.Copy), reads=[kpb], writes=[k_btok])
                        SK(True)
                        ps3, kps3 = bankF()
                        for g in range(4):
                            op("pe", MM(ps3[:, g * 128:(g + 1) * 128], xbcT[:, 16 + g, cs], xbcT[:, 20 + g, cs]),
                               reads=[k_xbcT], writes=[kps3])
                        cbm, k_cbm = tmp("cbm", [128, 4, 128], F32, bufs=1)
                        op("dve", TT(cbm[:], ps3.rearrange("p (g l) -> p g l", g=4),
                                     triU.unsqueeze(1).broadcast_to([128, 4, 128]), ALU.mult),
                           reads=[kps3, k_cst], writes=[k_cbm])
                        def make_MT(g_):
                          MT, k_MT = tmp("MT", [128, 8, 128], BF16, bufs=2)
                          for q in (2 * g_, 2 * g_ + 1):
                            R, k_R = tmp("R", [128, 4, 128], F32, bufs=1)
                            op("dve", TT(R[:], triU.unsqueeze(1).broadcast_to([128, 4, 128]),
                                         da[:, q * 4:(q + 1) * 4].unsqueeze(2).broadcast_to([128, 4, 128]), ALU.mult),
                               reads=[k_cst, k_sm], writes=[k_R])
                            ps4, kps4 = bankF()
                            op("pe", MM(ps4, SU, R[:].rearrange("p h l -> p (h l)")), reads=[k_cst, k_R], writes=[kps4])
                            Ex, k_Ex = tmp("Ex", [128, 4, 128], F32, bufs=1)
                            op("act", A_(Ex[:].rearrange("p h l -> p (h l)"), ps4, AF.Exp), reads=[kps4], writes=[k_Ex])
                            op("dve", TT(MT[:, (q % 2) * 4:(q % 2 + 1) * 4, :], Ex[:],
                                         cbm[:, g_:g_ + 1, :].broadcast_to([128, 4, 128]), ALU.mult),
                               reads=[k_Ex, k_cbm], writes=[k_MT])
                          return MT, k_MT
                        yn, k_yn = tmp("yn", [128, 2048], BF16, bufs=1)
                        ss, k_ss = tmp("ss", [128, 8])
                        for g in range(4):
                            gs = slice(g * 512, (g + 1) * 512)
                            psD, kpsD = bankF()
                            MT, k_MT = make_MT(g)
                            for r in range(8):
                                h = g * 8 + r
                                op("pe", MM(psD[:, r * 64:(r + 1) * 64], MT[:, r, :], xdt[:, h * 64:(h + 1) * 64]),
                                   reads=[k_MT, k_xdt], writes=[kpsD])
                            psO, kpsO = bankF()
                            op("pe", MM(psO, xbcT[:, 20 + g, cs], Sbf[:, gs]), reads=[k_xbcT, k_Sbf], writes=[kpsO])
                            psZ, kpsZ = bankF()
                            for kc in range(8):
                                op("pe", MM(psZ, XTb[:, kc, cs], Win[:, kc, gs], kc == 0, kc == 7),
                                   reads=[k_win, k_XTb], writes=[kpsZ])
                            y, k_y = tmp("y", [128, 512], F32, bufs=1)
                            op("dve", TT(y[:].rearrange("p (h q) -> p h q", q=64), psO.rearrange("p (h q) -> p h q", q=64),
                                         ea[:, g * 8:(g + 1) * 8].unsqueeze(2).broadcast_to([128, 8, 64]), ALU.mult),
                               reads=[kpsO, k_sm], writes=[k_y])
                            op("dve", TT(y[:], y[:], psD, ALU.add), reads=[k_y, kpsD], writes=[k_y])
                            y2, k_y2 = tmp("y2", [128, 512], F32, bufs=1)
                            op("dve", TT(y2[:].rearrange("p (h q) -> p h q", q=64), xdt[:, gs].rearrange("p (h q) -> p h q", q=64),
                                         dif[:, g * 8:(g + 1) * 8].unsqueeze(2).broadcast_to([128, 8, 64]), ALU.mult),
                               reads=[k_xdt, k_sm], writes=[k_y2])
                            op("dve", TT(y[:], y[:], y2[:], ALU.add), reads=[k_y, k_y2], writes=[k_y])
                            sz, k_sz = tmp("sz", [128, 512], F32, bufs=1)
                            op("act", A_(sz[:], psZ, AF.Silu), reads=[kpsZ], writes=[k_sz])
                            op("dve", TT(y[:], y[:], sz[:], ALU.mult), reads=[k_y, k_sz], writes=[k_y])
                            op("act", A_(y2[:], y[:], AF.Square, accum_out=ss[:, g:g + 1]), reads=[k_y], writes=[k_y2, k_ss])
                            op("dve", TS(ss[:, 4 + g:5 + g], ss[:, g:g + 1], 1.0 / 512, EPS, ALU.mult, ALU.add), reads=[k_ss], writes=[k_ss])
                            op("act", A_(ss[:, 4 + g:5 + g], ss[:, 4 + g:5 + g], AF.Sqrt), reads=[k_ss], writes=[k_ss])
                            op("dve", (lambda o, i_: lambda e: e.reciprocal(out=o, in_=i_))(ss[:, 4 + g:5 + g], ss[:, 4 + g:5 + g]),
                               reads=[k_ss], writes=[k_ss])
                            op("dve", TS(yn[:, gs], y[:], ss[:, 4 + g:5 + g], None, ALU.mult), reads=[k_y, k_ss], writes=[k_yn])
                        SK(False)
                        for g in range(4):
                            gs = slice(g * 512, (g + 1) * 512)
                            psU, kpsU = bankF()
                            op("pe", MM(psU, btok[:, g * 128:(g + 1) * 128], xw[:, gs]), reads=[k_btok, k_xw], writes=[kpsU])
                            op("dve", TT(S[:, gs].rearrange("p (h q) -> p h q", q=64), S[:, gs].rearrange("p (h q) -> p h q", q=64),
                                         cd[:, g * 8:(g + 1) * 8].unsqueeze(2).broadcast_to([128, 8, 64]), ALU.mult),
                               reads=[k_S, k_sm], writes=[k_S])
                            op("dve", TT(S[:, gs], S[:, gs], psU, ALU.add), reads=[k_S, kpsU], writes=[k_S])
                        SK(True)
                        op("act", A_(Sbf[:], S[:], AF.Copy), reads=[k_S], writes=[k_Sbf])
                        yT, k_yT = tmp("yT", [128, 16, 128], BF16, bufs=1)
                        for hf in range(2):
                            pb, kpb = bankB()
                            for c in range(8):
                                op("pe", TR(pb[:, c * 128:(c + 1) * 128], yn[:, (hf * 8 + c) * 128:(hf * 8 + c + 1) * 128], identB[:]),
                                   reads=[k_yn, k_idb], writes=[kpb])
                            op("dve", TT(yT[:, hf * 8:(hf + 1) * 8, :], pb.rearrange("p (c t) -> p c t", c=8),
                                         nw[:, hf * 8:(hf + 1) * 8].unsqueeze(2).broadcast_to([128, 8, 128]), ALU.mult),
                               reads=[kpb, k_par], writes=[k_yT])
                        psm, kpsm = [], []
                        for nb in range(2):
                            pm, kpm = bankF()
                            for c in range(16):
                                op("pe", MM(pm, yT[:, c, :], Wout[:, c, nb * 512:(nb + 1) * 512], c == 0, c == 15),
                                   reads=[k_yT, k_wout], writes=[kpm])
                            psm.append(pm)
                            kpsm.append(kpm)
                        mixer_epilogue(l, t, psm, kpsm, src_d, k_src)
                        SK(False)

            if MULTI:
                run_phase(True)
                op("act", A_(dtot[:], totsum[:, 0:32], AF.Exp), reads=[k_tot], writes=[k_dtot])
                P.barrier()
                arena_release(mark)
                exchange("l%d" % l, S[:], k_S, 2048, dtot[:], k_dtot, 32,
                         lambda a: a.rearrange("p (h q) -> p h q", q=64), lambda d_: d_.unsqueeze(2).broadcast_to([128, 32, 64]))
                op("act", A_(Sbf[:], S[:], AF.Copy), reads=[k_S], writes=[k_Sbf])
                P.barrier()
                arena_release(mark)
            run_phase(False)

        def hgrn_layer(l, src_d, k_src):
            j = l // 2
            Win = regA[:, 0:8 * 4096].rearrange("p (k n) -> p k n", k=8)
            Wout = regB[:, 0:8 * 1024].rearrange("p (k n) -> p k n", k=8)
            k_win, k_wout = Tk("win"), Tk("wout")
            wv = hg_w_in[j].rearrange("(k p) n -> p k n", p=128)
            for kc in range(8):
                for (a, b) in ((0, 2048), (2048, 4096)):
                    dma("pool", d_w, Win[:, kc, a:b], wv[:, kc, a:b], writes=[k_win])
            wo = hg_w_out[j].rearrange("(k p) n -> p k n", p=128)
            for kc in range(8):
                dma("pool", d_w2, Wout[:, kc, :], wo[:, kc, :], writes=[k_wout])
            par, _ = tmp("hg_par", [128, 8 * 5 + 128], F32, bufs=1)
            k_par = Tk("hpar")
            lb0, lb1, lbv, oml, noml = [par[:, i * 8:(i + 1) * 8] for i in range(5)]
            nwb = par[:, 40:168]
            dma("sp", d_par, lb0, hg_lb[0], writes=[k_par])
            dma("sp", d_par, lb1, hg_lb[1], writes=[k_par])
            dma("sp", d_par, nwb, hg_nw[j:j + 1, :].broadcast_to([128, 128]), writes=[k_par])
            if j == 0:
                op("dve", lambda e: e.memset(lbv, 0.0), writes=[k_par])
            else:
                op("dve", TT(lbv, lb1, lb0, ALU.subtract), reads=[k_par], writes=[k_par])
                op("act", A_(lbv, lbv, AF.Sigmoid), reads=[k_par], writes=[k_par])
            op("dve", TS(oml, lbv, -1.0, 1.0, ALU.mult, ALU.add), reads=[k_par], writes=[k_par])
            op("dve", TS(noml, oml, -1.0, None, ALU.mult), reads=[k_par], writes=[k_par])
            load_ln(l, 0)
            S, _ = tmp("hg_S", [128, 8, 128], F32, bufs=1)
            Sbf = [tmp("hg_Sbf", [128, 8, 128], BF16, bufs=2)[0] for i in range(2)]
            k_S, k_Sbf = Tk("hS"), [Tk("hSbf0"), Tk("hSbf1")]
            op("dve", lambda e: e.memset(S[:], 0.0), writes=[k_S])
            op("dve", lambda e: e.memset(Sbf[0][:], 0.0), writes=[k_Sbf[0]])

            dtot, _ = tmp("hg_dtot", [128, 8], F32, bufs=1)
            k_dtot = Tk("hdtot")
            mark = arena_mark()

            def run_phase(phaseA):
                cur["phaseA"] = phaseA
                if phaseA:
                    op("dve", lambda e: e.memset(S[:], 0.0), writes=[k_S])
                    op("dve", lambda e: e.memset(dtot[:], 1.0), writes=[k_dtot])
                for blk in range(NB):
                    XTb, k_XTb = load_xblock(src_d, k_src, blk)
                    qt, k_qt = tmp("qt", [128, 8, BLK], BF16, bufs=1)
                    kt, k_kt = tmp("kt", [128, 8, BLK], BF16, bufs=1)
                    qc, k_qc = tmp("qc", [128, 8, BLK], BF16, bufs=1)
                    ketok, k_ketok = tmp("ketok", [128, TPB, 8, 128], BF16, bufs=1)
                    ebl, k_ebl = tmp("ebl", [128, 8, BLK // 64], F32, bufs=1)
                    for h in range(8):
                        SK(True)
                        psq, kpsq = bankF()
                        for kc in range(8):
                            op("pe", MM(psq[:, 0:BLK], Win[:, kc, h * 128:(h + 1) * 128], XTb[:, kc, :], kc == 0, kc == 7),
                               reads=[k_win, k_XTb], writes=[kpsq])
                        SK(False)
                        psf, kpsf = bankF()
                        for kc in range(8):
                            op("pe", MM(psf[:, 0:BLK], Win[:, kc, 1024 + h * 128:1024 + (h + 1) * 128], XTb[:, kc, :], kc == 0, kc == 7),
                               reads=[k_win, k_XTb], writes=[kpsf])
                        sg, k_sg = tmp("sg", [128, BLK], F32, bufs=1)
                        op("act", A_(sg[:], psf[:, 0:BLK], AF.Sigmoid), reads=[kpsf], writes=[k_sg])
                        ff, k_ff = tmp("ff", [128, BLK], F32, bufs=1)
                        op("dve", TS(ff[:], sg[:], oml[:, h:h + 1], lbv[:, h:h + 1], ALU.mult, ALU.add), reads=[k_sg, k_par], writes=[k_ff])
                        op("act", A_(ff[:], ff[:], AF.Ln), reads=[k_ff], writes=[k_ff])
                        bb, k_bb = tmp("bb", [128, BLK], F32, bufs=1)
                        op("dve", (lambda o, d0, d1: lambda e: e.tensor_tensor_scan(out=o, data0=d0, data1=d1, initial=0.0,
                                                                                     op0=ALU.mult, op1=ALU.add))(bb[:], rst[:, 0:BLK], ff[:]),
                           reads=[k_ff, k_cst], writes=[k_bb])
                        kk, k_kk = tmp("kk", [128, BLK], F32, bufs=1)
                        op("dve", TS(kk[:], sg[:], noml[:, h:h + 1], oml[:, h:h + 1], ALU.mult, ALU.add), reads=[k_sg, k_par], writes=[k_kk])
                        eb, k_eb = tmp("eb", [128, BLK], F32, bufs=1)
                        op("act", A_(eb[:], bb[:], AF.Exp), reads=[k_bb], writes=[k_eb])
                        SK(True)
                        op("dve", TT(qt[:, h, :], psq[:, 0:BLK], eb[:], ALU.mult), reads=[kpsq, k_eb], writes=[k_qt])
                        SK(False)
                        op("act", A_(ebl[:, h, :], eb[:, 63:BLK:64], AF.Copy), reads=[k_eb], writes=[k_ebl])
                        SK(True)
                        nbm, k_nbm = tmp("nbm", [128, BLK // 64], F32, bufs=1)
                        op("dve", TS(nbm[:], bb[:, 31:BLK:64], -1.0, None, ALU.mult), reads=[k_bb], writes=[k_nbm])
                        ebc, k_ebc = tmp("ebc", [128, BLK], F32, bufs=1)
                        enb, k_enb = tmp("enb", [128, BLK], F32, bufs=1)
                        for c in range(BLK // 64):
                            csl = slice(c * 64, (c + 1) * 64)
                            op("act", A_(ebc[:, csl], bb[:, csl], AF.Exp, bias=nbm[:, c:c + 1]), reads=[k_bb, k_nbm], writes=[k_ebc])
                            op("act", A_(enb[:, csl], bb[:, csl], AF.Exp, scale=-1.0, bias=bb[:, c * 64 + 31:c * 64 + 32]),
                               reads=[k_bb], writes=[k_enb])
                        op("dve", TT(qc[:, h, :], psq[:, 0:BLK], ebc[:], ALU.mult), reads=[kpsq, k_ebc], writes=[k_qc])
                        op("dve", TT(kt[:, h, :], kk[:], enb[:], ALU.mult), reads=[k_kk, k_enb], writes=[k_kt])
                        SK(False)
                        ee, k_ee = tmp("ee", [128, BLK], F32, bufs=1)
                        for c in range(BLK // 64):
                            op("act", A_(ee[:, c * 64:(c + 1) * 64], bb[:, c * 64:(c + 1) * 64], AF.Exp, scale=-1.0,
                                         bias=bb[:, c * 64 + 63:c * 64 + 64]), reads=[k_bb], writes=[k_ee])
                        ke, k_ke = tmp("ke", [128, BLK], BF16)
                        op("dve", TT(ke[:], kk[:], ee[:], ALU.mult), reads=[k_kk, k_ee], writes=[k_ke])
                        pb, kpb = bankB()
                        for ti in range(TPB):
                            op("pe", TR(pb[:, ti * 128:(ti + 1) * 128], ke[:, ti * 128:(ti + 1) * 128], identB[:]),
                               reads=[k_ke, k_idb], writes=[kpb])
                        op("act", A_(ketok[:, :, h, :], pb[:, 0:TPB * 128].rearrange("p (a k) -> p a k", a=TPB), AF.Copy),
                           reads=[kpb], writes=[k_ketok])
                    for ti in range(TPB):
                        t = blk * TPB + ti
                        cs = slice(ti * 128, (ti + 1) * 128)
                        vb, k_vb = tmp("vb", [128, 1024], BF16, bufs=1)
                        sgt, k_sgt = tmp("sgt", [128, 1024], F32, bufs=1)
                        for nb in range(2):
                            ps, kps = bankF()
                            for kc in range(8):
                                op("pe", MM(ps, XTb[:, kc, cs], Win[:, kc, 2048 + nb * 512:2048 + (nb + 1) * 512], kc == 0, kc == 7),
                                   reads=[k_win, k_XTb], writes=[kps])
                            op("act", A_(vb[:, nb * 512:(nb + 1) * 512], ps, AF.Copy), reads=[kps], writes=[k_vb])
                            SK(True)
                            ps, kps = bankF()
                            for kc in range(8):
                                op("pe", MM(ps, XTb[:, kc, cs], Win[:, kc, 3072 + nb * 512:3072 + (nb + 1) * 512], kc == 0, kc == 7),
                                   reads=[k_win, k_XTb], writes=[kps])
                            op("act", A_(sgt[:, nb * 512:(nb + 1) * 512], ps, AF.Silu), reads=[kps], writes=[k_sgt])
                            SK(False)
                        SK(True)
                        scm, k_scm = tmp("scm", [128, 8, 128], BF16, bufs=1)
                        for hf in range(2):
                            ps, kps = bankF()
                            for hh in range(4):
                                h = hf * 4 + hh
                                op("pe", MM(ps[:, hh * 128:(hh + 1) * 128], kt[:, h, cs], qc[:, h, cs]), reads=[k_kt, k_qc], writes=[kps])
                            scl, k_scl = tmp("scl", [128, 512], F32, bufs=1)
                            op("dve", TS(scl[:], ps, -1e30, 1e30, ALU.max, ALU.min), reads=[kps], writes=[k_scl])
                            op("dve", TT(scm[:, hf * 4:(hf + 1) * 4, :], scl[:].rearrange("p (h l) -> p h l", h=4),
                                         maskBD.unsqueeze(1).broadcast_to([128, 4, 128]), ALU.mult),
                               reads=[k_scl, k_cst], writes=[k_scm])

                        SK(False)

                        def state_update(half, dst):
                            rs = slice(half * 64, (half + 1) * 64)
                            c = ti * 2 + half
                            if phaseA:
                                op("dve", TT(dtot[:], dtot[:], ebl[:, :, c], ALU.mult), reads=[k_dtot, k_ebl], writes=[k_dtot])
                            for hf in range(2):
                                ps, kps = bankF()
                                for hh in range(4):
                                    h = hf * 4 + hh
                                    op("pe", MM(ps[:, hh * 128:(hh + 1) * 128], ketok[rs, ti, h, :], vb[rs, h * 128:(h + 1) * 128]),
                                       reads=[k_ketok, k_vb], writes=[kps])
                                hs_ = slice(hf * 4, (hf + 1) * 4)
                                op("dve", TT(S[:, hs_, :], S[:, hs_, :], ebl[:, hs_, c:c + 1].broadcast_to([128, 4, 128]), ALU.mult),
                                   reads=[k_S, k_ebl], writes=[k_S])
                                op("dve", TT(S[:, hs_, :], S[:, hs_, :], ps.rearrange("p (h v) -> p h v", h=4), ALU.add),
                                   reads=[k_S, kps], writes=[k_S])
                            op("act", A_(Sbf[dst][:], S[:], AF.Copy), reads=[k_S], writes=[k_Sbf[dst]])

                        state_update(0, 1)
                        SK(True)
                        pso = []
                        for hf in range(2):
                            ps, kps = bankF()
                            for hh in range(4):
                                h = hf * 4 + hh
                                o_ = ps[:, hh * 128:(hh + 1) * 128]
                                op("pe", MM(o_, scm[:, h, :], vb[:, h * 128:(h + 1) * 128], True, False), reads=[k_scm, k_vb], writes=[kps])
                                op("pe", MM(o_[0:64, :], qt[:, h, ti * 128:ti * 128 + 64], Sbf[0][:, h, :], False, False),
                                   reads=[k_qt, k_Sbf[0]], writes=[kps])
                                op("pe", MM(o_[64:128, :], qt[:, h, ti * 128 + 64:ti * 128 + 128], Sbf[1][:, h, :], False, True),
                                   reads=[k_qt, k_Sbf[1]], writes=[kps])
                            pso.append((ps, kps))
                        SK(False)
                        state_update(1, 0)
                        SK(True)
                        on, k_on = tmp("on", [128, 8, 128], F32, bufs=1)
                        ssq, k_ssq = tmp("ssq", [128, 16])
                        junk, k_junk = tmp("junk", [128, 128])
                        for hf in range(2):
                            ps, kps = pso[hf]
                            for hh in range(4):
                                h = hf * 4 + hh
                                op("act", A_(junk[:], ps[:, hh * 128:(hh + 1) * 128], AF.Square, accum_out=ssq[:, h:h + 1]),
                                   reads=[kps], writes=[k_junk, k_ssq])
                        op("dve", TS(ssq[:, 8:16], ssq[:, 0:8], 1.0 / 128, EPS, ALU.mult, ALU.add), reads=[k_ssq], writes=[k_ssq])
                        op("act", A_(ssq[:, 8:16], ssq[:, 8:16], AF.Sqrt), reads=[k_ssq], writes=[k_ssq])
                        op("dve", (lambda o, i_: lambda e: e.reciprocal(out=o, in_=i_))(ssq[:, 8:16], ssq[:, 8:16]), reads=[k_ssq], writes=[k_ssq])
                        for hf in range(2):
                            ps, kps = pso[hf]
                            hs_ = slice(hf * 4, (hf + 1) * 4)
                            op("dve", TT(on[:, hs_, :], ps.rearrange("p (h v) -> p h v", h=4),
                                         ssq[:, 8 + hf * 4:12 + hf * 4].unsqueeze(2).broadcast_to([128, 4, 128]), ALU.mult),
                               reads=[kps, k_ssq], writes=[k_on])
                        op("dve", TT(on[:], on[:], nwb.unsqueeze(1).broadcast_to([128, 8, 128]), ALU.mult), reads=[k_on, k_par], writes=[k_on])
                        onb, k_onb = tmp("onb", [128, 1024], BF16, bufs=1)
                        op("dve", TT(onb[:], on[:].rearrange("p h v -> p (h v)"), sgt[:], ALU.mult), reads=[k_on, k_sgt], writes=[k_onb])
                        onT, k_onT = tmp("onT", [128, 8, 128], BF16, bufs=1)
                        pb, kpb = bankB()
                        for c in range(8):
                            op("pe", TR(pb[:, c * 128:(c + 1) * 128], onb[:, c * 128:(c + 1) * 128], identB[:]), reads=[k_onb, k_idb], writes=[kpb])
                        op("act", A_(onT[:], pb.rearrange("p (c t) -> p c t", c=8), AF.Copy), reads=[kpb], writes=[k_onT])
                        psm, kpsm = [], []
                        for nb in range(2):
                            pm, kpm = bankF()
                            for c in range(8):
                                op("pe", MM(pm, onT[:, c, :], Wout[:, c, nb * 512:(nb + 1) * 512], c == 0, c == 7),
                                   reads=[k_onT, k_wout], writes=[kpm])
                            psm.append(pm)
                            kpsm.append(kpm)
                        mixer_epilogue(l, t, psm, kpsm, src_d, k_src)
                        SK(False)

            if MULTI:
                run_phase(True)
                P.barrier()
                arena_release(mark)
                exchange("l%d" % l, S[:].rearrange("p h v -> p (h v)"), k_S, 1024, dtot[:], k_dtot, 8,
                         lambda a: a.rearrange("p (h v) -> p h v", v=128), lambda d_: d_.unsqueeze(2).broadcast_to([128, 8, 128]))
                op("act", A_(Sbf[0][:], S[:], AF.Copy), reads=[k_S], writes=[k_Sbf[0]])
                P.barrier()
                arena_release(mark)
            run_phase(False)

        def mlp_ple_segment(l, seg, dst_d, k_dst, last):
            Xacc = regA[:, 0:16 * 2048].bitcast(F32).rearrange("p (t d) -> p t d", t=16)
            kX = [Tk("Xacc%d" % i) for i in range(16)]
            XTh["XT"] = tmp("XT", [128, 8, 2048], BF16, bufs=1)[0]
            XT = XTh["XT"]
            for tl in range(16):
                t = seg * 16 + tl
                dma("sp", d_h[tl % 2], Xacc[:, tl, :], hs_d[t * 128:(t + 1) * 128, :], reads=[k_hs[t]], writes=[kX[tl]])
                hb, k_hbb = tmp("ln_hb", [128, D], BF16)
                op("act", A_(hb[:], Xacc[:, tl, :], AF.Copy, scale=1.0 / ALPHA), reads=[kX[tl]], writes=[k_hbb])
                to_XT(hb, k_hbb, tl)
            load_ln(l, 1)
            wbuf = [regB[:, i * 8192:(i + 1) * 8192] for i in range(2)]
            k_wb = [Tk("mwb0"), Tk("mwb1")]
            w1v = w1_d[l].rearrange("(k p) n -> p k n", p=128)
            w2v = w2_d[l].rearrange("(k p) n -> p k n", p=128)
            for e8 in range(8):
                wb = wbuf[e8 % 2]
                kwb = k_wb[e8 % 2]
                w1e = wb[:, 0:4096].rearrange("p (k n) -> p k n", k=8)
                w2e = wb[:, 4096:8192].rearrange("p (k n) -> p k n", k=4)
                for kc in range(8):
                    dma("pool", d_mw[e8 % 2], w1e[:, kc, :], w1v[:, kc, e8 * 512:(e8 + 1) * 512], writes=[kwb])
                for fc in range(4):
                    dma("pool", d_mw[e8 % 2], w2e[:, fc, :], w2v[:, e8 * 4 + fc, :], writes=[kwb])
                for tb in range(4):
                    hid, k_hid = tmp("hid", [128, 4, 512], BF16, bufs=2)
                    for fc in range(4):
                        ps, kps = bankF()
                        for kc in range(8):
                            op("pe", MM(ps, w1e[:, kc, fc * 128:(fc + 1) * 128], XT[:, kc, tb * 512:(tb + 1) * 512], kc == 0, kc == 7),
                               reads=[kwb] + kXT[tb * 4:tb * 4 + 4], writes=[kps])
                        rl, k_rl = tmp("rl", [128, 512])
                        op("act", A_(rl[:], ps, AF.Relu), reads=[kps], writes=[k_rl])
                        op("dve", TT(hid[:, fc, :], rl[:], rl[:], ALU.mult), reads=[k_rl], writes=[k_hid])
                    for ti in range(4):
                        tl = tb * 4 + ti
                        for nb in range(2):
                            ps, kps = bankF()
                            for fc in range(4):
                                op("pe", MM(ps, hid[:, fc, ti * 128:(ti + 1) * 128], w2e[:, fc, nb * 512:(nb + 1) * 512], fc == 0, fc == 3),
                                   reads=[k_hid, kwb], writes=[kps])
                            xa = Xacc[:, tl, nb * 512:(nb + 1) * 512]
                            op("dve", TT(xa, xa, ps, ALU.add), reads=[kX[tl], kps], writes=[kX[tl]])
            Wg = regB[:, 0:8192].rearrange("p (k n) -> p k n", k=8)
            Wp = regB[:, 8192:8192 + 2048].rearrange("p (k n) -> p k n", k=2)
            wgv = wg_d[l].rearrange("(k p) n -> p k n", p=128)
            wpv = wp_d[l].rearrange("(k p) n -> p k n", p=128)
            for kc in range(8):
                dma("pool", d_pl, Wg[:, kc, :], wgv[:, kc, :], writes=[k_wb[0]])
            for kc in range(2):
                dma("pool", d_pl, Wp[:, kc, :], wpv[:, kc, :], writes=[k_wb[1]])
            for tl in range(16):
                t = seg * 16 + tl
                hb, k_hbb = layer_norm(Xacc[:, tl, :], kX[tl], Xacc[:, tl, :], kX[tl])
                to_XT(hb, k_hbb, tl)
                pt, k_pt = tmp("pt", [128, 256])
                dma("sp", d_p[tl % 2], pt[:], p_d[l, t * 128:(t + 1) * 128, :], writes=[k_pt])
                ptb, k_ptb = tmp("ptb", [128, 256], BF16)
                op("dve", CP(ptb[:], pt[:]), reads=[k_pt], writes=[k_ptb])
                pb, kpb = bankB()
                for c in range(2):
                    op("pe", TR(pb[:, c * 128:(c + 1) * 128], ptb[:, c * 128:(c + 1) * 128], identB[:]), reads=[k_ptb, k_idb], writes=[kpb])
                pT, k_pT = tmp("pT", [128, 2, 128], BF16)
                op("act", A_(pT[:], pb[:, 0:256].rearrange("p (c t) -> p c t", c=2), AF.Copy), reads=[kpb], writes=[k_pT])
                xn, k_xn = tmp("xn", [128, D])
                for nb in range(2):
                    psg, kpsg = bankF()
                    for kc in range(8):
                        op("pe", MM(psg, XT[:, kc, tl * 128:(tl + 1) * 128], Wg[:, kc, nb * 512:(nb + 1) * 512], kc == 0, kc == 7),
                           reads=[kXT[tl], k_wb[0]], writes=[kpsg])
                    psp, kpsp = bankF()
                    for kc in range(2):
                        op("pe", MM(psp, pT[:, kc, :], Wp[:, kc, nb * 512:(nb + 1) * 512], kc == 0, kc == 1),
                           reads=[k_pT, k_wb[1]], writes=[kpsp])
                    sgm, k_sgm = tmp("sgm", [128, 512])
                    op("act", A_(sgm[:], psg, AF.Sigmoid), reads=[kpsg], writes=[k_sgm])
                    op("dve", TT(sgm[:], sgm[:], psp, ALU.mult), reads=[k_sgm, kpsp], writes=[k_sgm])
                    op("dve", TT(xn[:, nb * 512:(nb + 1) * 512], sgm[:], Xacc[:, tl, nb * 512:(nb + 1) * 512], ALU.add),
                       reads=[k_sgm, kX[tl]], writes=[k_xn])
                dma("sp", d_st, dst_d[t * 128:(t + 1) * 128, :], xn[:], reads=[k_xn], writes=[k_dst[t]])
                if MULTI and tl == 15 and (l + 1) in layers and (l + 1) % 2 == 0:
                    xnb, k_xnb = tmp("ln_hb", [128, D], BF16)
                    op("act", A_(xnb[:], xn[:], AF.Copy), reads=[k_xn], writes=[k_xnb])
                    pb, kpb = bankB()
                    for c in range(8):
                        op("pe", TR(pb[:, c * 128:(c + 1) * 128], xnb[:, c * 128:(c + 1) * 128], identB[:]), reads=[k_xnb, k_idb], writes=[kpb])
                    xl, k_xl = tmp("xl", [128, 24], F32, bufs=1)
                    op("act", A_(xl[:].rearrange("p (c t) -> p c t", c=8), pb.rearrange("p (c t) -> p c t", c=8)[:, :, 125:128], AF.Copy),
                       reads=[kpb], writes=[k_xl])
                    halo_exchange("l%d" % l, xl[:], k_xl)

        src_d, k_src = x_d, None
        k_out = [Tk("out%d" % i) for i in range(NT)]
        for li, l in enumerate(layers):
            last = li == len(layers) - 1
            if l % 2 == 0:
                ssd_layer(l, src_d, k_src)
            else:
                hgrn_layer(l, src_d, k_src)
            P.barrier()
            arena_reset()
            dst_d, k_dst = (out_d, k_out) if last else (xs_d, k_xs)
            for seg in range(NSEG):
                mlp_ple_segment(l, seg, dst_d, k_dst, last)
                P.barrier()
                arena_reset()
            src_d, k_src = xs_d, k_xs
        P.barrier()
        block = es.enter_context(nc.Block())
        P.replay(block)
    return nc


def make_cst():
    c = np.zeros((128, 5 * 128 + 512), np.float32)
    i = np.arange(128)
    c[:, 0:128] = np.eye(128)
    c[:, 128:256] = (i[:, None] <= i[None, :])
    c[:, 256:384] = (i[:, None] > i[None, :])
    c[:, 384:512] = 1.0
    c[:, 512:640] = (i[:, None] <= i[None, :]) & ((i[:, None] // 64) == (i[None, :] // 64))
    r = np.ones(512, np.float32)
    r[0::64] = 0.0
    c[:, 640:1152] = r[None, :]
    return c


def prep_weights(inp):
    f = lambda a: np.ascontiguousarray(np.asarray(a, dtype=np.float32))
    w = {}
    for k in ["ssd_w_in", "ssd_dt_bias", "ssd_a_log", "ssd_d", "ssd_w_out", "hgrn_w_in", "hgrn_norm_w", "hgrn_w_out",
              "ln_g", "ln_b", "mlp_w1", "mlp_w2", "ple_w_proj", "ple_w_gate"]:
        w[k] = f(inp[k])
    cw = f(inp["ssd_conv_w"])
    w["ssd_cw"] = f(cw.reshape(2, 4, 24, 128).transpose(0, 3, 2, 1).reshape(2, 128, 96))
    w["ssd_cb"] = f(f(inp["ssd_conv_b"]).reshape(2, 24, 128).transpose(0, 2, 1))
    w["ssd_nw"] = f(f(inp["ssd_norm_w"]).reshape(2, 16, 128).transpose(0, 2, 1))
    w["hgrn_lb"] = f(f(inp["hgrn_lower_bounds"]).reshape(2, 8, 128).transpose(0, 2, 1))
    w["cst"] = make_cst()
    return w


def run(inp, T, layers, ncores=8):
    w = prep_weights(inp)
    x = np.asarray(inp["x"], np.float32)
    p = np.asarray(inp["p"], np.float32)
    nc = build(T, layers)
    in_maps = []
    for c in range(ncores):
        b = c if c < 2 else 0
        m = dict(w)
        m["x"] = np.ascontiguousarray(x[b, :T])
        m["p"] = np.ascontiguousarray(p[:, b, :T])
        m["msk"] = np.zeros((128, 18), np.float32)
        m["xhT"] = np.zeros((128, 24), np.float32)
        in_maps.append(m)
    res = run_bass_kernel_spmd(nc, in_maps, core_ids=list(range(ncores)))
    return np.stack([res.results[0]["out"], res.results[1]["out"]], axis=0)


def run_multi(inp, layers, nseq=4, trace=False):
    w = prep_weights(inp)
    x = np.asarray(inp["x"], np.float32)
    p = np.asarray(inp["p"], np.float32)
    TS_ = 2048
    nc = build(TS_, layers, MULTI=True)
    in_maps = []
    for c in range(8):
        b, r = c // 4, c % 4
        m = dict(w)
        m["x"] = np.ascontiguousarray(x[b, r * TS_:(r + 1) * TS_])
        m["p"] = np.ascontiguousarray(p[:, b, r * TS_:(r + 1) * TS_])
        msk = np.zeros((128, 18), np.float32)
        slot = lambda bb, rr: bb * 3 + rr
        if r < 3:
            msk[:, slot(b, r)] = 1.0
        for rr in range(r):
            msk[:, 6 + slot(b, rr)] = 1.0
        if r > 0:
            msk[:, 12 + slot(b, r - 1)] = 1.0
        m["msk"] = msk
        xh = np.zeros((128, 24), np.float32)
        if r > 0:
            prev = x[b, r * TS_ - 3:r * TS_, :]
            xh = np.ascontiguousarray(prev.reshape(3, 8, 128).transpose(2, 1, 0).reshape(128, 24))
        m["xhT"] = xh
        in_maps.append(m)
    res = run_bass_kernel_spmd(nc, in_maps, core_ids=list(range(8)))
    out = np.zeros((2, 4 * TS_, D), np.float32)
    for c in range(8):
        out[c // 4, (c % 4) * TS_:(c % 4 + 1) * TS_] = res.results[c]["out"]
    return out


def kernel(**inputs):
    return run_multi(inputs, [0, 1, 2, 3]).astype(np.float32)
```

```python
import numpy as np
from contextlib import ExitStack
import concourse.bass as bass
import concourse.mybir as mybir
from concourse.bass_utils import run_bass_kernel_spmd

F32, BF16 = mybir.dt.float32, mybir.dt.bfloat16
AF = mybir.ActivationFunctionType
ALU = mybir.AluOpType

D = 1024
DEPTH = 4
ALPHA = (2.0 * DEPTH) ** 0.25
EPS = 1e-5
SAME_ENGINE_SYNC = True


class Tk:
    __slots__ = ("name", "w", "r")

    def __init__(self, name=""):
        self.name = name
        self.w = {}
        self.r = {}


class Eng:
    def __init__(self, name, sem):
        self.name = name
        self.sem = sem
        self.n = 0
        self.waited = {}
        self.ops = []


class Prog:
    def __init__(self, nc, es):
        self.nc, self.es = nc, es
        self.E = {}
        for name in ["pe", "act", "dve", "pool", "sp"]:
            self.E[name] = Eng(name, es.enter_context(nc.semaphore("s_" + name)))
        self.dsems = []
        self.skip = False
        self.dset = set()

    def dsem(self, name):
        d = Eng("d_" + name, self.es.enter_context(self.nc.semaphore("d_" + name)))
        self.dsems.append(d)
        self.dset.add(d)
        return d

    def _waits(self, e, reads, writes, skip_same):
        need = {}
        for t in reads:
            for k, v in t.w.items():
                if need.get(k, 0) < v:
                    need[k] = v
        for t in writes:
            for k, v in t.w.items():
                if need.get(k, 0) < v:
                    need[k] = v
            for k, v in t.r.items():
                if need.get(k, 0) < v:
                    need[k] = v
        for k, v in need.items():
            if k in self.dset:
                v = k.n
            if k is e and skip_same:
                continue
            if e.waited.get(k, 0) >= v:
                continue
            e.waited[k] = v
            e.ops.append(("w", k.sem, v))

    def op(self, en, fn, reads=(), writes=()):
        if self.skip:
            return
        e = self.E[en]
        self._waits(e, reads, writes, en == "pe" or not SAME_ENGINE_SYNC)
        e.n += 1
        e.ops.append(("i", fn, e.sem, 1))
        for t in reads:
            t.r[e] = e.n
        for t in writes:
            t.w = {e: e.n}
            t.r = {}

    def cc(self, d, fn, reads=(), writes=()):
        q = self.E["pool"]
        self._waits(q, reads, writes, False)
        d.n += 1
        q.ops.append(("i", fn, d.sem, 1))
        for t in reads:
            t.r[d] = d.n
        for t in writes:
            t.w = {d: d.n}
            t.r = {}

    def dma(self, qn, d, out, in_, reads=(), writes=()):
        if self.skip:
            return
        q = self.E[qn]
        self._waits(q, reads, writes, False)
        d.n += 16
        q.ops.append(("i", lambda eng: eng.dma_start(out=out, in_=in_), d.sem, 16))
        for t in reads:
            t.r[d] = d.n
        for t in writes:
            t.w[d] = d.n
            t.r = {}

    def barrier(self):
        allk = list(self.E.values()) + self.dsems
        for e in self.E.values():
            for k in allk:
                if k is e or k.n == 0:
                    continue
                if e.waited.get(k, 0) < k.n:
                    e.waited[k] = k.n
                    e.ops.append(("w", k.sem, k.n))

    def replay(self, block):
        def run(e):
            def f(eng):
                for o in e.ops:
                    if o[0] == "w":
                        eng.wait_ge(o[1], o[2])
                    else:
                        o[1](eng).then_inc(o[2], o[3])
            return f
        block.tensor(run(self.E["pe"]))
        block.scalar(run(self.E["act"]))
        block.vector(run(self.E["dve"]))
        block.gpsimd(run(self.E["pool"]))
        block.sync(run(self.E["sp"]))


def A_(out, in_, func, **kw):
    return lambda e: e.activation(out=out, in_=in_, func=func, **kw)


def TT(out, in0, in1, op):
    return lambda e: e.tensor_tensor(out=out, in0=in0, in1=in1, op=op)


def TS(out, in0, s1, s2, op0, op1=None):
    if op1 is None:
        return lambda e: e.tensor_scalar(out=out, in0=in0, scalar1=s1, scalar2=None, op0=op0)
    return lambda e: e.tensor_scalar(out=out, in0=in0, scalar1=s1, scalar2=s2, op0=op0, op1=op1)


def STT(out, in0, sc, in1, op0, op1):
    return lambda e: e.scalar_tensor_tensor(out=out, in0=in0, scalar=sc, in1=in1, op0=op0, op1=op1)


def CP(out, in_):
    return lambda e: e.tensor_copy(out=out, in_=in_)


def MM(out, lhsT, rhs, start=True, stop=True):
    return lambda e: e.matmul(out, lhsT=lhsT, rhs=rhs, start=start, stop=stop)


def TR(out, in_, ident):
    return lambda e: e.transpose(out, in_, ident)


def build(T, layers, MULTI=False):
    NSEG = T // 2048
    NT = T // 128
    BLK = 256
    TPB = BLK // 128
    NB = T // BLK
    nc = bass.Bass("TRN2", target_bir_lowering=False)

    def din(name, shape):
        return nc.dram_tensor(name, list(shape), F32, kind="ExternalInput").ap()

    x_d = din("x", [T, D])
    p_d = din("p", [DEPTH, T, 256])
    ssd_w_in = din("ssd_w_in", [2, D, 5152])
    ssd_cw = din("ssd_cw", [2, 128, 24 * 4])
    ssd_cb = din("ssd_cb", [2, 128, 24])
    ssd_dtb = din("ssd_dt_bias", [2, 32])
    ssd_alog = din("ssd_a_log", [2, 32])
    ssd_dd = din("ssd_d", [2, 32])
    ssd_nw = din("ssd_nw", [2, 128, 16])
    ssd_w_out = din("ssd_w_out", [2, 2048, D])
    hg_w_in = din("hgrn_w_in", [2, D, 4096])
    hg_lb = din("hgrn_lb", [2, 128, 8])
    hg_nw = din("hgrn_norm_w", [2, 128])
    hg_w_out = din("hgrn_w_out", [2, D, D])
    ln_g = din("ln_g", [DEPTH, 2, D])
    ln_b = din("ln_b", [DEPTH, 2, D])
    w1_d = din("mlp_w1", [DEPTH, D, 4096])
    w2_d = din("mlp_w2", [DEPTH, 4096, D])
    wp_d = din("ple_w_proj", [DEPTH, 256, D])
    wg_d = din("ple_w_gate", [DEPTH, D, D])
    cst_d = din("cst", [128, 5 * 128 + 512])
    msk_d = din("msk", [128, 18])
    xhT_d = din("xhT", [128, 24])
    out_d = nc.dram_tensor("out", [T, D], F32, kind="ExternalOutput").ap()
    xs_d = nc.dram_tensor("xscr", [T, D], F32).ap()
    hs_d = nc.dram_tensor("hscr", [T, D], F32).ap()

    es = ExitStack()
    with es:
        P = Prog(nc, es)
        op, dma = P.op, P.dma

        def sb(name, shape, dt=F32):
            return es.enter_context(nc.sbuf_tensor(name, list(shape), dt))

        cst = sb("cst_sb", [128, 5 * 128 + 512])
        k_cst = Tk("cst")
        d_cst = P.dsem("cst")
        dma("sp", d_cst, cst[:], cst_d, writes=[k_cst])
        identF = cst[:, 0:128]
        triU = cst[:, 128:256]
        SU = cst[:, 256:384]
        ones = cst[:, 384:512]
        maskBD = cst[:, 512:640]
        rst = cst[:, 640:1152]
        msk = sb("msk_sb", [128, 18])
        xh_sb = sb("xh_sb", [128, 24])
        k_msk, k_xh = Tk("msk"), Tk("xh")
        dma("sp", d_cst, msk[:], msk_d, writes=[k_msk])
        dma("sp", d_cst, xh_sb[:], xhT_d, writes=[k_xh])
        oh6, pr6, sel6 = msk[:, 0:6], msk[:, 6:12], msk[:, 12:18]
        identB = sb("identB", [128, 128], BF16)
        k_idb = Tk()
        op("dve", CP(identB[:], identF), reads=[k_cst], writes=[k_idb])

        psF = es.enter_context(nc.psum_tensor("psF", [128, 6, 512], F32))
        psB = es.enter_context(nc.psum_tensor("psB", [128, 2, 1024], BF16))
        kF = [Tk("psF%d" % i) for i in range(6)]
        kB = [Tk("psB%d" % i) for i in range(2)]
        rot = {"F": 0, "B": 0}

        def bankF():
            i = rot["F"] % 6
            rot["F"] += 1
            return psF[:, i, :], kF[i]

        def bankB():
            i = rot["B"] % 2
            rot["B"] += 1
            return psB[:, i, :], kB[i]

        ARENA = 80 * 1024
        arena = sb("arena", [128, ARENA // 2], BF16)
        pools = {}
        ar = {"off": 0}

        def carve(shape, dt):
            n = 1
            for d_ in shape[1:]:
                n *= d_
            nb = n * (4 if dt == F32 else 2)
            nb = (nb + 31) // 32 * 32
            off = ar["off"]
            assert off + nb <= ARENA, ("arena overflow", off, nb)
            ar["off"] = off + nb
            ar["hw"] = max(ar.get("hw", 0), off + nb)
            a = arena[:, off // 2:(off + nb) // 2]
            if dt == F32:
                a = a.bitcast(F32)
            a = a[:, 0:n]
            if len(shape) == 3:
                a = a.rearrange("p (a b) -> p a b", a=shape[1])
            elif len(shape) == 4:
                a = a.rearrange("p (a b c) -> p a b c", a=shape[1], b=shape[2])
            return a

        def arena_reset():
            print("arena high-water", ar.get("hw", 0))
            ar["hw"] = 0
            pools.clear()
            ar["off"] = 0

        def arena_mark():
            return (ar["off"], set(pools.keys()))

        def arena_release(m):
            for k in list(pools.keys()):
                if k not in m[1]:
                    del pools[k]
            ar["off"] = m[0]

        cur = {"phaseA": False}

        def SK(b):
            P.skip = bool(b) and cur["phaseA"]

        class _T:
            def __init__(self, a):
                self.a = a

            def __getitem__(self, key):
                return self.a[key]

        def tmp(name, shape, dt=F32, bufs=2):
            if name not in pools:
                pools[name] = [[(_T(carve(shape, dt)), Tk(name)) for i in range(bufs)], 0]
            pl = pools[name]
            t, k = pl[0][pl[1] % len(pl[0])]
            pl[1] += 1
            return t, k

        regA = sb("regA", [128, 8 * 5152], BF16)
        regB = sb("regB", [128, 16 * 1024], BF16)
        kXT = [Tk("XT%d" % i) for i in range(16)]
        XTh = {}
        d_w = P.dsem("w")
        d_w2 = P.dsem("w2")
        d_x = [P.dsem("x%d" % i) for i in range(2)]
        d_st = P.dsem("st")
        d_par = P.dsem("par")
        d_mw = [P.dsem("mw%d" % i) for i in range(2)]
        d_pl = P.dsem("pl")
        d_h = [P.dsem("h%d" % i) for i in range(2)]
        d_p = [P.dsem("p%d" % i) for i in range(2)]
        k_xs = [Tk("xs%d" % i) for i in range(NT)]
        k_hs = [Tk("hs%d" % i) for i in range(NT)]

        gb = sb("gb", [128, 2, D])
        k_gb = Tk("gb")

        def load_ln(l, j):
            dma("sp", d_par, gb[:, 0, :], ln_g[l, j:j + 1, :].broadcast_to([128, D]), writes=[k_gb])
            dma("sp", d_par, gb[:, 1, :], ln_b[l, j:j + 1, :].broadcast_to([128, D]), writes=[k_gb])

        def layer_norm(src, k_src, dst_f32, k_dst, want_bf=True):
            st, k_st = tmp("ln_st", [128, 2, 6])
            for i in range(2):
                op("dve", (lambda o, i_: lambda e: e.bn_stats(out=o, in_=i_))(st[:, i, :], src[:, i * 512:(i + 1) * 512]),
                   reads=[k_src], writes=[k_st])
            mv, k_mv = tmp("ln_mv", [128, 4])
            op("dve", (lambda o, i_: lambda e: e.bn_aggr(out=o, in_=i_))(mv[:, 0:2], st[:].rearrange("p a b -> p (a b)")),
               reads=[k_st], writes=[k_mv])
            op("dve", TS(mv[:, 2:3], mv[:, 1:2], EPS, None, ALU.add), reads=[k_mv], writes=[k_mv])
            op("act", A_(mv[:, 2:3], mv[:, 2:3], AF.Sqrt), reads=[k_mv], writes=[k_mv])
            op("dve", (lambda o, i_: lambda e: e.reciprocal(out=o, in_=i_))(mv[:, 2:3], mv[:, 2:3]), reads=[k_mv], writes=[k_mv])
            op("dve", TS(mv[:, 3:4], mv[:, 0:1], mv[:, 2:3], -1.0, ALU.mult, ALU.mult), reads=[k_mv], writes=[k_mv])
            op("act", A_(dst_f32, src, AF.Identity, scale=mv[:, 2:3], bias=mv[:, 3:4]), reads=[k_src, k_mv], writes=[k_dst])
            op("dve", TT(dst_f32, dst_f32, gb[:, 0, :], ALU.mult), reads=[k_dst, k_gb], writes=[k_dst])
            op("dve", TT(dst_f32, dst_f32, gb[:, 1, :], ALU.add), reads=[k_dst, k_gb], writes=[k_dst])
            if not want_bf:
                return None, None
            hb, k_hbb = tmp("ln_hb", [128, D], BF16)
            op("act", A_(hb[:], dst_f32, AF.Copy), reads=[k_dst], writes=[k_hbb])
            return hb, k_hbb

        def to_XT(hb, k_hbb, tl):
            pb, kpb = bankB()
            for c in range(8):
                op("pe", TR(pb[:, c * 128:(c + 1) * 128], hb[:, c * 128:(c + 1) * 128], identB[:]),
                   reads=[k_hbb, k_idb], writes=[kpb])
            op("act", A_(XTh["XT"][:, :, tl * 128:(tl + 1) * 128], pb.rearrange("p (c t) -> p c t", c=8), AF.Copy),
               reads=[kpb], writes=[kXT[tl]])

        def mixer_epilogue(l, t, psm, kpsm, xsrc, k_xsrc):
            xt, k_xt = tmp("ep_x", [128, D], F32, bufs=1)
            dma("sp", d_x[t % 2], xt[:], xsrc[t * 128:(t + 1) * 128, :], reads=[k_xsrc[t]] if k_xsrc else [], writes=[k_xt])
            for nb in range(2):
                sl = slice(nb * 512, (nb + 1) * 512)
                op("dve", STT(xt[:, sl], xt[:, sl], ALPHA, psm[nb], ALU.mult, ALU.add),
                   reads=[k_xt, kpsm[nb]], writes=[k_xt])
            layer_norm(xt[:], k_xt, xt[:], k_xt, want_bf=False)
            op("act", A_(xt[:], xt[:], AF.Copy, scale=ALPHA), reads=[k_xt], writes=[k_xt])
            dma("sp", d_st, hs_d[t * 128:(t + 1) * 128, :], xt[:], reads=[k_xt], writes=[k_hs[t]])

        d_xc = [P.dsem("xc0"), P.dsem("xc1")]
        d_cc = P.dsem("cc")

        def allreduce(bin_, bout, k_bin, k_bout):
            P.cc(d_cc, lambda g: g.collective_compute("AllReduce", ALU.add, replica_groups=[list(range(8))],
                                                      ins=[bin_.ap().opt()], outs=[bout.ap().opt()]),
                 reads=[k_bin], writes=[k_bout])

        def exchange(tag, S2d, k_S, W_S, Dt, k_D, W_D, viewS, bcD):
            W = W_S + W_D
            bin_ = nc.dram_tensor("xin_" + tag, [6 * 128, W], F32)
            bout = nc.dram_tensor("xout_" + tag, [6 * 128, W], F32)
            k_bin, k_bout = Tk("bin"), Tk("bout")
            for j in range(6):
                stg, k_stg = tmp("xstg", [128, W], F32, bufs=2)
                op("dve", TS(stg[:, 0:W_S], S2d, oh6[:, j:j + 1], None, ALU.mult), reads=[k_S, k_msk], writes=[k_stg])
                op("dve", TS(stg[:, W_S:W], Dt, oh6[:, j:j + 1], None, ALU.mult), reads=[k_D, k_msk], writes=[k_stg])
                dma("sp", d_xc[0], bin_.ap()[j * 128:(j + 1) * 128, :], stg[:], reads=[k_stg], writes=[k_bin])
            allreduce(bin_, bout, k_bin, k_bout)
            op("dve", lambda e: e.memset(S2d, 0.0), reads=[k_bin], writes=[k_S])
            for j in range(6):
                stg, k_stg = tmp("xstg", [128, W], F32, bufs=2)
                dma("pool", d_xc[1], stg[:], bout.ap()[j * 128:(j + 1) * 128, :], reads=[k_bout], writes=[k_stg])
                de, k_de = tmp("xde", [128, W_D], F32, bufs=2)
                op("dve", TS(de[:], stg[:, W_S:W], 1.0, pr6[:, j:j + 1], ALU.subtract, ALU.mult), reads=[k_stg, k_msk], writes=[k_de])
                op("dve", TS(de[:], de[:], 1.0, None, ALU.add), reads=[k_de], writes=[k_de])
                op("dve", TT(viewS(S2d), viewS(S2d), bcD(de[:]), ALU.mult), reads=[k_S, k_de], writes=[k_S])
                op("dve", STT(S2d, stg[:, 0:W_S], pr6[:, j:j + 1], S2d, ALU.mult, ALU.add), reads=[k_stg, k_S, k_msk], writes=[k_S])

        def halo_exchange(tag, xl, k_xl):
            bin_ = nc.dram_tensor("hin_" + tag, [6 * 128, 24], F32)
            bout = nc.dram_tensor("hout_" + tag, [6 * 128, 24], F32)
            k_bin, k_bout = Tk("hbin"), Tk("hbout")
            stg, k_stg = tmp("hstg", [128, 6, 24], F32, bufs=1)
            for j in range(6):
                op("dve", TS(stg[:, j, :], xl, oh6[:, j:j + 1], None, ALU.mult), reads=[k_xl, k_msk], writes=[k_stg])
            dma("sp", d_xc[0], bin_.ap().rearrange("(j p) w -> p j w", p=128), stg[:], reads=[k_stg], writes=[k_bin])
            allreduce(bin_, bout, k_bin, k_bout)
            stg2, k_stg2 = tmp("hstg2", [128, 6, 24], F32, bufs=1)
            dma("pool", d_xc[1], stg2[:], bout.ap().rearrange("(j p) w -> p j w", p=128), reads=[k_bout], writes=[k_stg2])
            op("dve", lambda e: e.memset(xh_sb[:], 0.0), writes=[k_xh])
            for j in range(6):
                op("dve", STT(xh_sb[:], stg2[:, j, :], sel6[:, j:j + 1], xh_sb[:], ALU.mult, ALU.add),
                   reads=[k_stg2, k_xh, k_msk], writes=[k_xh])

        def load_xblock(src_d, k_src_tiles, blk):
            XTb, k_XTb = tmp("XTb", [128, 8, BLK], BF16, bufs=1)
            for a in range(TPB):
                t = blk * TPB + a
                xt, k_xt = tmp("ep_x", [128, D], F32, bufs=1)
                dma("sp", d_x[t % 2], xt[:], src_d[t * 128:(t + 1) * 128, :],
                    reads=[k_src_tiles[t]] if k_src_tiles else [], writes=[k_xt])
                xb, k_xb = tmp("xb", [128, D], BF16, bufs=1)
                op("dve", CP(xb[:], xt[:]), reads=[k_xt], writes=[k_xb])
                pb, kpb = bankB()
                for c in range(8):
                    op("pe", TR(pb[:, c * 128:(c + 1) * 128], xb[:, c * 128:(c + 1) * 128], identB[:]),
                       reads=[k_xb, k_idb], writes=[kpb])
                op("act", A_(XTb[:, :, a * 128:(a + 1) * 128], pb.rearrange("p (c t) -> p c t", c=8), AF.Copy),
                   reads=[kpb], writes=[k_XTb])
            return XTb, k_XTb

        def ssd_layer(l, src_d, k_src):
            j = l // 2
            Win = regA[:].rearrange("p (k n) -> p k n", k=8)
            Wout = regB[:].rearrange("p (k n) -> p k n", k=16)
            k_win, k_wout = Tk("win"), Tk("wout")
            wv = ssd_w_in[j].rearrange("(k p) n -> p k n", p=128)
            for kc in range(8):
                for (a, b) in ((0, 2048), (2048, 4096), (4096, 5152)):
                    dma("pool", d_w, Win[:, kc, a:b], wv[:, kc, a:b], writes=[k_win])
            wo = ssd_w_out[j].rearrange("(k p) n -> p k n", p=128)
            for kc in range(16):
                dma("pool", d_w2, Wout[:, kc, :], wo[:, kc, :], writes=[k_wout])
            par, _ = tmp("ssd_par", [128, 24 * 4 + 24 + 16 + 32 * 3], F32, bufs=1)
            k_par = Tk("par")
            cw = par[:, 0:96].rearrange("p (c k) -> p c k", k=4)
            cb = par[:, 96:120]
            nw = par[:, 120:136]
            dtb = par[:, 136:168]
            an = par[:, 168:200]
            dd = par[:, 200:232]
            dma("sp", d_par, par[:, 0:96], ssd_cw[j], writes=[k_par])
            dma("sp", d_par, cb, ssd_cb[j], writes=[k_par])
            dma("sp", d_par, nw, ssd_nw[j], writes=[k_par])
            dma("sp", d_par, dtb, ssd_dtb[j:j + 1, :].broadcast_to([128, 32]), writes=[k_par])
            dma("sp", d_par, an, ssd_alog[j:j + 1, :].broadcast_to([128, 32]), writes=[k_par])
            dma("sp", d_par, dd, ssd_dd[j:j + 1, :].broadcast_to([128, 32]), writes=[k_par])
            op("act", A_(an, an, AF.Exp), reads=[k_par], writes=[k_par])
            op("dve", TS(an, an, -1.0, None, ALU.mult), reads=[k_par], writes=[k_par])
            load_ln(l, 0)
            S, _ = tmp("ssd_S", [128, 2048], F32, bufs=1)
            Sbf, _ = tmp("ssd_Sbf", [128, 2048], BF16, bufs=1)
            halo, _ = tmp("ssd_halo", [128, 24, 3], F32, bufs=1)
            k_S, k_Sbf, k_halo = Tk("S"), Tk("Sbf"), Tk("halo")
            op("dve", lambda e: e.memset(S[:], 0.0), writes=[k_S])
            op("dve", lambda e: e.memset(Sbf[:], 0.0), writes=[k_Sbf])

            halo0, _ = tmp("ssd_halo0", [128, 24, 3], F32, bufs=1)
            totsum, _ = tmp("ssd_totsum", [128, 64], F32, bufs=1)
            k_tot = Tk("tot")
            dtot, _ = tmp("ssd_dtot", [128, 32], F32, bufs=1)
            k_dtot = Tk("dtot")
            k_h0 = Tk("halo0")
            if MULTI:
                xhb, k_xhb = tmp("xhb", [128, 8, 3], BF16, bufs=1)
                op("dve", CP(xhb[:], xh_sb[:].rearrange("p (k t) -> p k t", k=8)), reads=[k_xh], writes=[k_xhb])
                psh, kpsh = bankF()
                for cc in range(24):
                    for kc in range(8):
                        op("pe", MM(psh[:, cc * 3:(cc + 1) * 3], Win[:, kc, 2048 + cc * 128:2048 + (cc + 1) * 128], xhb[:, kc, :], kc == 0, kc == 7),
                           reads=[k_win, k_xhb], writes=[kpsh])
                op("act", A_(halo0[:].rearrange("p c t -> p (c t)"), psh[:, 0:72], AF.Copy), reads=[kpsh], writes=[k_h0])
            else:
                op("dve", lambda e: e.memset(halo0[:], 0.0), writes=[k_h0])
            mark = arena_mark()

            def run_phase(phaseA):
                cur["phaseA"] = phaseA
                op("act", A_(halo[:], halo0[:], AF.Copy), reads=[k_h0], writes=[k_halo])
                if phaseA:
                    op("dve", lambda e: e.memset(S[:], 0.0), writes=[k_S])
                    op("dve", lambda e: e.memset(totsum[:], 0.0), writes=[k_tot])
                for blk in range(NB):
                    XTb, k_XTb = load_xblock(src_d, k_src, blk)
                    xbcT, k_xbcT = tmp("xbcT", [128, 24, BLK], BF16, bufs=1)
                    pend_silu = []
                    for cc in range(24):
                        SK(cc >= 20)
                        ps, kps = bankF()
                        for kc in range(8):
                            op("pe", MM(ps[:, 0:BLK], Win[:, kc, 2048 + cc * 128:2048 + (cc + 1) * 128], XTb[:, kc, :], kc == 0, kc == 7),
                               reads=[k_win, k_XTb], writes=[kps])
                        u, k_u = tmp("u", [128, BLK + 3])
                        op("act", A_(u[:, 0:3], halo[:, cc, :], AF.Copy), reads=[k_halo], writes=[k_u])
                        op("act", A_(u[:, 3:BLK + 3], ps[:, 0:BLK], AF.Copy), reads=[kps], writes=[k_u])
                        op("act", A_(halo[:, cc, :], u[:, BLK:BLK + 3], AF.Copy), reads=[k_u], writes=[k_halo])
                        acc, k_acc = tmp("acc", [128, BLK])
                        op("dve", TS(acc[:], u[:, 0:BLK], cw[:, cc, 0:1], cb[:, cc:cc + 1], ALU.mult, ALU.add),
                           reads=[k_u, k_par], writes=[k_acc])
                        for k in range(1, 4):
                            op("dve", STT(acc[:], u[:, k:k + BLK], cw[:, cc, k:k + 1], acc[:], ALU.mult, ALU.add),
                               reads=[k_u, k_par, k_acc], writes=[k_acc])
                        pend_silu.append((cc, acc, k_acc))
                        if len(pend_silu) > 1:
                            c0, a0, ka0 = pend_silu.pop(0)
                            SK(c0 >= 20)
                            op("act", A_(xbcT[:, c0, :], a0[:], AF.Silu), reads=[ka0], writes=[k_xbcT])
                            SK(cc >= 20)
                    for (c0, a0, ka0) in pend_silu:
                        SK(c0 >= 20)
                        op("act", A_(xbcT[:, c0, :], a0[:], AF.Silu), reads=[ka0], writes=[k_xbcT])
                    SK(False)
                    for ti in range(TPB):
                        t = blk * TPB + ti
                        cs = slice(ti * 128, (ti + 1) * 128)
                        ps, kps = bankF()
                        for kc in range(8):
                            op("pe", MM(ps[:, 0:32], XTb[:, kc, cs], Win[:, kc, 5120:5152], kc == 0, kc == 7),
                               reads=[k_win, k_XTb], writes=[kps])
                        sm, k_sm = tmp("sm", [128, 8, 32])
                        dt_, da, acs, dif, ea, dte, cd, f2 = [sm[:, i, :] for i in range(8)]
                        op("dve", TT(dt_, ps[:, 0:32], dtb, ALU.add), reads=[kps, k_par], writes=[k_sm])
                        op("act", A_(dt_, dt_, AF.Exp), reads=[k_sm], writes=[k_sm])
                        op("act", A_(dt_, dt_, AF.Ln, bias=1.0), reads=[k_sm], writes=[k_sm])
                        op("dve", TT(da, dt_, an, ALU.mult), reads=[k_sm, k_par], writes=[k_sm])
                        ps2, kps2 = bankF()
                        op("pe", MM(ps2[:, 0:32], triU, da), reads=[k_cst, k_sm], writes=[kps2])
                        op("pe", MM(ps2[:, 32:64], ones, da), reads=[k_cst, k_sm], writes=[kps2])
                        op("act", A_(acs, ps2[:, 0:32], AF.Copy), reads=[kps2], writes=[k_sm])
                        op("dve", TT(dif, ps2[:, 32:64], acs, ALU.subtract), reads=[kps2, k_sm], writes=[k_sm])
                        op("act", A_(ea, acs, AF.Exp), reads=[k_sm], writes=[k_sm])
                        op("act", A_(dte, dif, AF.Exp), reads=[k_sm], writes=[k_sm])
                        op("act", A_(cd, ps2[:, 32:64], AF.Exp), reads=[kps2], writes=[k_sm])
                        if phaseA:
                            op("dve", TT(totsum[:, 0:32], totsum[:, 0:32], ps2[:, 32:64], ALU.add), reads=[k_tot, kps2], writes=[k_tot])
                        op("dve", TT(f2, dt_, dte, ALU.mult), reads=[k_sm], writes=[k_sm])
                        op("dve", (lambda o, i_: lambda e: e.reciprocal(out=o, in_=i_))(dif, dt_), reads=[k_sm], writes=[k_sm])
                        op("dve", TT(dif, dif, dd, ALU.mult), reads=[k_sm, k_par], writes=[k_sm])
                        xdt, k_xdt = tmp("xdt", [128, 2048], BF16, bufs=1)
                        xw, k_xw = tmp("xw", [128, 2048], BF16, bufs=1)
                        for hf in range(2):
                            pb, kpb = bankB()
                            for c in range(8):
                                op("pe", TR(pb[:, c * 128:(c + 1) * 128], xbcT[:, hf * 8 + c, cs], identB[:]),
                                   reads=[k_xbcT, k_idb], writes=[kpb])
                            sl = slice(hf * 1024, (hf + 1) * 1024)
                            pv = pb.rearrange("p (h q) -> p h q", q=64)
                            hsl = slice(hf * 16, (hf + 1) * 16)
                            SK(True)
                            op("dve", TT(xdt[:, sl].rearrange("p (h q) -> p h q", q=64), pv,
                                         dt_[:, hsl].unsqueeze(2).broadcast_to([128, 16, 64]), ALU.mult),
                               reads=[kpb, k_sm], writes=[k_xdt])
                            SK(False)
                            op("dve", TT(xw[:, sl].rearrange("p (h q) -> p h q", q=64), pv,
                                         f2[:, hsl].unsqueeze(2).broadcast_to([128, 16, 64]), ALU.mult),
                               reads=[kpb, k_sm], writes=[k_xw])
                        btok, k_btok = tmp("btok", [128, 512], BF16, bufs=1)
                        pb, kpb = bankB()
                        for g in range(4):
                            op("pe", TR(pb[:, g * 128:(g + 1) * 128], xbcT[:, 16 + g, cs], identB[:]),
                               reads=[k_xbcT, k_idb], writes=[kpb])
                        op("act", A_(btok[:], pb[:, 0:512], AF.Copy), reads=[kpb], writes=[k_btok])
                        SK(True)
                        ps3, kps3 = bankF()
                        for g in range(4):
                            op("pe", MM(ps3[:, g * 128:(g + 1) * 128], xbcT[:, 16 + g, cs], xbcT[:, 20 + g, cs]),
                               reads=[k_xbcT], writes=[kps3])
                        cbm, k_cbm = tmp("cbm", [128, 4, 128], F32, bufs=1)
                        op("dve", TT(cbm[:], ps3.rearrange("p (g l) -> p g l", g=4),
                                     triU.unsqueeze(1).broadcast_to([128, 4, 128]), ALU.mult),
                           reads=[kps3, k_cst], writes=[k_cbm])
                        def make_MT(g_):
                          MT, k_MT = tmp("MT", [128, 8, 128], BF16, bufs=2)
                          for q in (2 * g_, 2 * g_ + 1):
                            R, k_R = tmp("R", [128, 4, 128], F32, bufs=1)
                            op("dve", TT(R[:], triU.unsqueeze(1).broadcast_to([128, 4, 128]),
                                         da[:, q * 4:(q + 1) * 4].unsqueeze(2).broadcast_to([128, 4, 128]), ALU.mult),
                               reads=[k_cst, k_sm], writes=[k_R])
                            ps4, kps4 = bankF()
                            op("pe", MM(ps4, SU, R[:].rearrange("p h l -> p (h l)")), reads=[k_cst, k_R], writes=[kps4])
                            Ex, k_Ex = tmp("Ex", [128, 4, 128], F32, bufs=1)
                            op("act", A_(Ex[:].rearrange("p h l -> p (h l)"), ps4, AF.Exp), reads=[kps4], writes=[k_Ex])
                            op("dve", TT(MT[:, (q % 2) * 4:(q % 2 + 1) * 4, :], Ex[:],
                                         cbm[:, g_:g_ + 1, :].broadcast_to([128, 4, 128]), ALU.mult),
                               reads=[k_Ex, k_cbm], writes=[k_MT])
                          return MT, k_MT
                        yn, k_yn = tmp("yn", [128, 2048], BF16, bufs=1)
                        ss, k_ss = tmp("ss", [128, 8])
                        for g in range(4):
                            gs = slice(g * 512, (g + 1) * 512)
                            psD, kpsD = bankF()
                            MT, k_MT = make_MT(g)
                            for r in range(8):
                                h = g * 8 + r
                                op("pe", MM(psD[:, r * 64:(r + 1) * 64], MT[:, r, :], xdt[:, h * 64:(h + 1) * 64]),
                                   reads=[k_MT, k_xdt], writes=[kpsD])
                            psO, kpsO = bankF()
                            op("pe", MM(psO, xbcT[:, 20 + g, cs], Sbf[:, gs]), reads=[k_xbcT, k_Sbf], writes=[kpsO])
                            psZ, kpsZ = bankF()
                            for kc in range(8):
                                op("pe", MM(psZ, XTb[:, kc, cs], Win[:, kc, gs], kc == 0, kc == 7),
                                   reads=[k_win, k_XTb], writes=[kpsZ])
                            y, k_y = tmp("y", [128, 512], F32, bufs=1)
                            op("dve", TT(y[:].rearrange("p (h q) -> p h q", q=64), psO.rearrange("p (h q) -> p h q", q=64),
                                         ea[:, g * 8:(g + 1) * 8].unsqueeze(2).broadcast_to([128, 8, 64]), ALU.mult),
                               reads=[kpsO, k_sm], writes=[k_y])
                            op("dve", TT(y[:], y[:], psD, ALU.add), reads=[k_y, kpsD], writes=[k_y])
                            y2, k_y2 = tmp("y2", [128, 512], F32, bufs=1)
                            op("dve", TT(y2[:].rearrange("p (h q) -> p h q", q=64), xdt[:, gs].rearrange("p (h q) -> p h q", q=64),
                                         dif[:, g * 8:(g + 1) * 8].unsqueeze(2).broadcast_to([128, 8, 64]), ALU.mult),
                               reads=[k_xdt, k_sm], writes=[k_y2])
                            op("dve", TT(y[:], y[:], y2[:], ALU.add), reads=[k_y, k_y2], writes=[k_y])
                            sz, k_sz = tmp("sz", [128, 512], F32, bufs=1)
                            op("act", A_(sz[:], psZ, AF.Silu), reads=[kpsZ], writes=[k_sz])
                            op("dve", TT(y[:], y[:], sz[:], ALU.mult), reads=[k_y, k_sz], writes=[k_y])
                            op("act", A_(y2[:], y[:], AF.Square, accum_out=ss[:, g:g + 1]), reads=[k_y], writes=[k_y2, k_ss])
                            op("dve", TS(ss[:, 4 + g:5 + g], ss[:, g:g + 1], 1.0 / 512, EPS, ALU.mult, ALU.add), reads=[k_ss], writes=[k_ss])
                            op("act", A_(ss[:, 4 + g:5 + g], ss[:, 4 + g:5 + g], AF.Sqrt), reads=[k_ss], writes=[k_ss])
                            op("dve", (lambda o, i_: lambda e: e.reciprocal(out=o, in_=i_))(ss[:, 4 + g:5 + g], ss[:, 4 + g:5 + g]),
                               reads=[k_ss], writes=[k_ss])
                            op("dve", TS(yn[:, gs], y[:], ss[:, 4 + g:5 + g], None, ALU.mult), reads=[k_y, k_ss], writes=[k_yn])
                        SK(False)
                        for g in range(4):
                            gs = slice(g * 512, (g + 1) * 512)
                            psU, kpsU = bankF()
                            op("pe", MM(psU, btok[:, g * 128:(g + 1) * 128], xw[:, gs]), reads=[k_btok, k_xw], writes=[kpsU])
                            op("dve", TT(S[:, gs].rearrange("p (h q) -> p h q", q=64), S[:, gs].rearrange("p (h q) -> p h q", q=64),
                                         cd[:, g * 8:(g + 1) * 8].unsqueeze(2).broadcast_to([128, 8, 64]), ALU.mult),
                               reads=[k_S, k_sm], writes=[k_S])
                            op("dve", TT(S[:, gs], S[:, gs], psU, ALU.add), reads=[k_S, kpsU], writes=[k_S])
                        SK(True)
                        op("act", A_(Sbf[:], S[:], AF.Copy), reads=[k_S], writes=[k_Sbf])
                        yT, k_yT = tmp("yT", [128, 16, 128], BF16, bufs=1)
                        for hf in range(2):
                            pb, kpb = bankB()
                            for c in range(8):
                                op("pe", TR(pb[:, c * 128:(c + 1) * 128], yn[:, (hf * 8 + c) * 128:(hf * 8 + c + 1) * 128], identB[:]),
                                   reads=[k_yn, k_idb], writes=[kpb])
                            op("dve", TT(yT[:, hf * 8:(hf + 1) * 8, :], pb.rearrange("p (c t) -> p c t", c=8),
                                         nw[:, hf * 8:(hf + 1) * 8].unsqueeze(2).broadcast_to([128, 8, 128]), ALU.mult),
                               reads=[kpb, k_par], writes=[k_yT])
                        psm, kpsm = [], []
                        for nb in range(2):
                            pm, kpm = bankF()
                            for c in range(16):
                                op("pe", MM(pm, yT[:, c, :], Wout[:, c, nb * 512:(nb + 1) * 512], c == 0, c == 15),
                                   reads=[k_yT, k_wout], writes=[kpm])
                            psm.append(pm)
                            kpsm.append(kpm)
                        mixer_epilogue(l, t, psm, kpsm, src_d, k_src)
                        SK(False)

            if MULTI:
                run_phase(True)
                op("act", A_(dtot[:], totsum[:, 0:32], AF.Exp), reads=[k_tot], writes=[k_dtot])
                P.barrier()
                arena_release(mark)
                exchange("l%d" % l, S[:], k_S, 2048, dtot[:], k_dtot, 32,
                         lambda a: a.rearrange("p (h q) -> p h q", q=64), lambda d_: d_.unsqueeze(2).broadcast_to([128, 32, 64]))
                op("act", A_(Sbf[:], S[:], AF.Copy), reads=[k_S], writes=[k_Sbf])
                P.barrier()
                arena_release(mark)
            run_phase(False)

        def hgrn_layer(l, src_d, k_src):
            j = l // 2
            Win = regA[:, 0:8 * 4096].rearrange("p (k n) -> p k n", k=8)
            Wout = regB[:, 0:8 * 1024].rearrange("p (k n) -> p k n", k=8)
            k_win, k_wout = Tk("win"), Tk("wout")
            wv = hg_w_in[j].rearrange("(k p) n -> p k n", p=128)
            for kc in range(8):
                for (a, b) in ((0, 2048), (2048, 4096)):
                    dma("pool", d_w, Win[:, kc, a:b], wv[:, kc, a:b], writes=[k_win])
            wo = hg_w_out[j].rearrange("(k p) n -> p k n", p=128)
            for kc in range(8):
                dma("pool", d_w2, Wout[:, kc, :], wo[:, kc, :], writes=[k_wout])
            par, _ = tmp("hg_par", [128, 8 * 5 + 128], F32, bufs=1)
            k_par = Tk("hpar")
            lb0, lb1, lbv, oml, noml = [par[:, i * 8:(i + 1) * 8] for i in range(5)]
            nwb = par[:, 40:168]
            dma("sp", d_par, lb0, hg_lb[0], writes=[k_par])
            dma("sp", d_par, lb1, hg_lb[1], writes=[k_par])
            dma("sp", d_par, nwb, hg_nw[j:j + 1, :].broadcast_to([128, 128]), writes=[k_par])
            if j == 0:
                op("dve", lambda e: e.memset(lbv, 0.0), writes=[k_par])
            else:
                op("dve", TT(lbv, lb1, lb0, ALU.subtract), reads=[k_par], writes=[k_par])
                op("act", A_(lbv, lbv, AF.Sigmoid), reads=[k_par], writes=[k_par])
            op("dve", TS(oml, lbv, -1.0, 1.0, ALU.mult, ALU.add), reads=[k_par], writes=[k_par])
            op("dve", TS(noml, oml, -1.0, None, ALU.mult), reads=[k_par], writes=[k_par])
            load_ln(l, 0)
            S, _ = tmp("hg_S", [128, 8, 128], F32, bufs=1)
            Sbf = [tmp("hg_Sbf", [128, 8, 128], BF16, bufs=2)[0] for i in range(2)]
            k_S, k_Sbf = Tk("hS"), [Tk("hSbf0"), Tk("hSbf1")]
            op("dve", lambda e: e.memset(S[:], 0.0), writes=[k_S])
            op("dve", lambda e: e.memset(Sbf[0][:], 0.0), writes=[k_Sbf[0]])

            dtot, _ = tmp("hg_dtot", [128, 8], F32, bufs=1)
            k_dtot = Tk("hdtot")
            mark = arena_mark()

            def run_phase(phaseA):
                cur["phaseA"] = phaseA
                if phaseA:
                    op("dve", lambda e: e.memset(S[:], 0.0), writes=[k_S])
                    op("dve", lambda e: e.memset(dtot[:], 1.0), writes=[k_dtot])
                for blk in range(NB):
                    XTb, k_XTb = load_xblock(src_d, k_src, blk)
                    qt, k_qt = tmp("qt", [128, 8, BLK], BF16, bufs=1)
                    kt, k_kt = tmp("kt", [128, 8, BLK], BF16, bufs=1)
                    qc, k_qc = tmp("qc", [128, 8, BLK], BF16, bufs=1)
                    ketok, k_ketok = tmp("ketok", [128, TPB, 8, 128], BF16, bufs=1)
                    ebl, k_ebl = tmp("ebl", [128, 8, BLK // 64], F32, bufs=1)
                    for h in range(8):
                        SK(True)
                        psq, kpsq = bankF()
                        for kc in range(8):
                            op("pe", MM(psq[:, 0:BLK], Win[:, kc, h * 128:(h + 1) * 128], XTb[:, kc, :], kc == 0, kc == 7),
                               reads=[k_win, k_XTb], writes=[kpsq])
                        SK(False)
                        psf, kpsf = bankF()
                        for kc in range(8):
                            op("pe", MM(psf[:, 0:BLK], Win[:, kc, 1024 + h * 128:1024 + (h + 1) * 128], XTb[:, kc, :], kc == 0, kc == 7),
                               reads=[k_win, k_XTb], writes=[kpsf])
                        sg, k_sg = tmp("sg", [128, BLK], F32, bufs=1)
                        op("act", A_(sg[:], psf[:, 0:BLK], AF.Sigmoid), reads=[kpsf], writes=[k_sg])
                        ff, k_ff = tmp("ff", [128, BLK], F32, bufs=1)
                        op("dve", TS(ff[:], sg[:], oml[:, h:h + 1], lbv[:, h:h + 1], ALU.mult, ALU.add), reads=[k_sg, k_par], writes=[k_ff])
                        op("act", A_(ff[:], ff[:], AF.Ln), reads=[k_ff], writes=[k_ff])
                        bb, k_bb = tmp("bb", [128, BLK], F32, bufs=1)
                        op("dve", (lambda o, d0, d1: lambda e: e.tensor_tensor_scan(out=o, data0=d0, data1=d1, initial=0.0,
                                                                                     op0=ALU.mult, op1=ALU.add))(bb[:], rst[:, 0:BLK], ff[:]),
                           reads=[k_ff, k_cst], writes=[k_bb])
                        kk, k_kk = tmp("kk", [128, BLK], F32, bufs=1)
                        op("dve", TS(kk[:], sg[:], noml[:, h:h + 1], oml[:, h:h + 1], ALU.mult, ALU.add), reads=[k_sg, k_par], writes=[k_kk])
                        eb, k_eb = tmp("eb", [128, BLK], F32, bufs=1)
                        op("act", A_(eb[:], bb[:], AF.Exp), reads=[k_bb], writes=[k_eb])
                        SK(True)
                        op("dve", TT(qt[:, h, :], psq[:, 0:BLK], eb[:], ALU.mult), reads=[kpsq, k_eb], writes=[k_qt])
                        SK(False)
                        op("act", A_(ebl[:, h, :], eb[:, 63:BLK:64], AF.Copy), reads=[k_eb], writes=[k_ebl])
                        SK(True)
                        nbm, k_nbm = tmp("nbm", [128, BLK // 64], F32, bufs=1)
                        op("dve", TS(nbm[:], bb[:, 31:BLK:64], -1.0, None, ALU.mult), reads=[k_bb], writes=[k_nbm])
                        ebc, k_ebc = tmp("ebc", [128, BLK], F32, bufs=1)
                        enb, k_enb = tmp("enb", [128, BLK], F32, bufs=1)
                        for c in range(BLK // 64):
                            csl = slice(c * 64, (c + 1) * 64)
                            op("act", A_(ebc[:, csl], bb[:, csl], AF.Exp, bias=nbm[:, c:c + 1]), reads=[k_bb, k_nbm], writes=[k_ebc])
                            op("act", A_(enb[:, csl], bb[:, csl], AF.Exp, scale=-1.0, bias=bb[:, c * 64 + 31:c * 64 + 32]),
                               reads=[k_bb], writes=[k_enb])
                        op("dve", TT(qc[:, h, :], psq[:, 0:BLK], ebc[:], ALU.mult), reads=[kpsq, k_ebc], writes=[k_qc])
                        op("dve", TT(kt[:, h, :], kk[:], enb[:], ALU.mult), reads=[k_kk, k_enb], writes=[k_kt])
                        SK(False)
                        ee, k_ee = tmp("ee", [128, BLK], F32, bufs=1)
                        for c in range(BLK // 64):
                            op("act", A_(ee[:, c * 64:(c + 1) * 64], bb[:, c * 64:(c + 1) * 64], AF.Exp, scale=-1.0,
                                         bias=bb[:, c * 64 + 63:c * 64 + 64]), reads=[k_bb], writes=[k_ee])
                        ke, k_ke = tmp("ke", [128, BLK], BF16)
                        op("dve", TT(ke[:], kk[:], ee[:], ALU.mult), reads=[k_kk, k_ee], writes=[k_ke])
                        pb, kpb = bankB()
                        for ti in range(TPB):
                            op("pe", TR(pb[:, ti * 128:(ti + 1) * 128], ke[:, ti * 128:(ti + 1) * 128], identB[:]),
                               reads=[k_ke, k_idb], writes=[kpb])
                        op("act", A_(ketok[:, :, h, :], pb[:, 0:TPB * 128].rearrange("p (a k) -> p a k", a=TPB), AF.Copy),
                           reads=[kpb], writes=[k_ketok])
                    for ti in range(TPB):
                        t = blk * TPB + ti
                        cs = slice(ti * 128, (ti + 1) * 128)
                        vb, k_vb = tmp("vb", [128, 1024], BF16, bufs=1)
                        sgt, k_sgt = tmp("sgt", [128, 1024], F32, bufs=1)
                        for nb in range(2):
                            ps, kps = bankF()
                            for kc in range(8):
                                op("pe", MM(ps, XTb[:, kc, cs], Win[:, kc, 2048 + nb * 512:2048 + (nb + 1) * 512], kc == 0, kc == 7),
                                   reads=[k_win, k_XTb], writes=[kps])
                            op("act", A_(vb[:, nb * 512:(nb + 1) * 512], ps, AF.Copy), reads=[kps], writes=[k_vb])
                            SK(True)
                            ps, kps = bankF()
                            for kc in range(8):
                                op("pe", MM(ps, XTb[:, kc, cs], Win[:, kc, 3072 + nb * 512:3072 + (nb + 1) * 512], kc == 0, kc == 7),
                                   reads=[k_win, k_XTb], writes=[kps])
                            op("act", A_(sgt[:, nb * 512:(nb + 1) * 512], ps, AF.Silu), reads=[kps], writes=[k_sgt])
                            SK(False)
                        SK(True)
                        scm, k_scm = tmp("scm", [128, 8, 128], BF16, bufs=1)
                        for hf in range(2):
                            ps, kps = bankF()
                            for hh in range(4):
                                h = hf * 4 + hh
                                op("pe", MM(ps[:, hh * 128:(hh + 1) * 128], kt[:, h, cs], qc[:, h, cs]), reads=[k_kt, k_qc], writes=[kps])
                            scl, k_scl = tmp("scl", [128, 512], F32, bufs=1)
                            op("dve", TS(scl[:], ps, -1e30, 1e30, ALU.max, ALU.min), reads=[kps], writes=[k_scl])
                            op("dve", TT(scm[:, hf * 4:(hf + 1) * 4, :], scl[:].rearrange("p (h l) -> p h l", h=4),
                                         maskBD.unsqueeze(1).broadcast_to([128, 4, 128]), ALU.mult),
                               reads=[k_scl, k_cst], writes=[k_scm])

                        SK(False)

                        def state_update(half, dst):
                            rs = slice(half * 64, (half + 1) * 64)
                            c = ti * 2 + half
                            if phaseA:
                                op("dve", TT(dtot[:], dtot[:], ebl[:, :, c], ALU.mult), reads=[k_dtot, k_ebl], writes=[k_dtot])
                            for hf in range(2):
                                ps, kps = bankF()
                                for hh in range(4):
                                    h = hf * 4 + hh
                                    op("pe", MM(ps[:, hh * 128:(hh + 1) * 128], ketok[rs, ti, h, :], vb[rs, h * 128:(h + 1) * 128]),
                                       reads=[k_ketok, k_vb], writes=[kps])
                                hs_ = slice(hf * 4, (hf + 1) * 4)
                                op("dve", TT(S[:, hs_, :], S[:, hs_, :], ebl[:, hs_, c:c + 1].broadcast_to([128, 4, 128]), ALU.mult),
                                   reads=[k_S, k_ebl], writes=[k_S])
                                op("dve", TT(S[:, hs_, :], S[:, hs_, :], ps.rearrange("p (h v) -> p h v", h=4), ALU.add),
                                   reads=[k_S, kps], writes=[k_S])
                            op("act", A_(Sbf[dst][:], S[:], AF.Copy), reads=[k_S], writes=[k_Sbf[dst]])

                        state_update(0, 1)
                        SK(True)
                        pso = []
                        for hf in range(2):
                            ps, kps = bankF()
                            for hh in range(4):
                                h = hf * 4 + hh
                                o_ = ps[:, hh * 128:(hh + 1) * 128]
                                op("pe", MM(o_, scm[:, h, :], vb[:, h * 128:(h + 1) * 128], True, False), reads=[k_scm, k_vb], writes=[kps])
                                op("pe", MM(o_[0:64, :], qt[:, h, ti * 128:ti * 128 + 64], Sbf[0][:, h, :], False, False),
                                   reads=[k_qt, k_Sbf[0]], writes=[kps])
                                op("pe", MM(o_[64:128, :], qt[:, h, ti * 128 + 64:ti * 128 + 128], Sbf[1][:, h, :], False, True),
                                   reads=[k_qt, k_Sbf[1]], writes=[kps])
                            pso.append((ps, kps))
                        SK(False)
                        state_update(1, 0)
                        SK(True)
                        on, k_on = tmp("on", [128, 8, 128], F32, bufs=1)
                        ssq, k_ssq = tmp("ssq", [128, 16])
                        junk, k_junk = tmp("junk", [128, 128])
                        for hf in range(2):
                            ps, kps = pso[hf]
                            for hh in range(4):
                                h = hf * 4 + hh
                                op("act", A_(junk[:], ps[:, hh * 128:(hh + 1) * 128], AF.Square, accum_out=ssq[:, h:h + 1]),
                                   reads=[kps], writes=[k_junk, k_ssq])
                        op("dve", TS(ssq[:, 8:16], ssq[:, 0:8], 1.0 / 128, EPS, ALU.mult, ALU.add), reads=[k_ssq], writes=[k_ssq])
                        op("act", A_(ssq[:, 8:16], ssq[:, 8:16], AF.Sqrt), reads=[k_ssq], writes=[k_ssq])
                        op("dve", (lambda o, i_: lambda e: e.reciprocal(out=o, in_=i_))(ssq[:, 8:16], ssq[:, 8:16]), reads=[k_ssq], writes=[k_ssq])
                        for hf in range(2):
                            ps, kps = pso[hf]
                            hs_ = slice(hf * 4, (hf + 1) * 4)
                            op("dve", TT(on[:, hs_, :], ps.rearrange("p (h v) -> p h v", h=4),
                                         ssq[:, 8 + hf * 4:12 + hf * 4].unsqueeze(2).broadcast_to([128, 4, 128]), ALU.mult),
                               reads=[kps, k_ssq], writes=[k_on])
                        op("dve", TT(on[:], on[:], nwb.unsqueeze(1).broadcast_to([128, 8, 128]), ALU.mult), reads=[k_on, k_par], writes=[k_on])
                        onb, k_onb = tmp("onb", [128, 1024], BF16, bufs=1)
                        op("dve", TT(onb[:], on[:].rearrange("p h v -> p (h v)"), sgt[:], ALU.mult), reads=[k_on, k_sgt], writes=[k_onb])
                        onT, k_onT = tmp("onT", [128, 8, 128], BF16, bufs=1)
                        pb, kpb = bankB()
                        for c in range(8):
                            op("pe", TR(pb[:, c * 128:(c + 1) * 128], onb[:, c * 128:(c + 1) * 128], identB[:]), reads=[k_onb, k_idb], writes=[kpb])
                        op("act", A_(onT[:], pb.rearrange("p (c t) -> p c t", c=8), AF.Copy), reads=[kpb], writes=[k_onT])
                        psm, kpsm = [], []
                        for nb in range(2):
                            pm, kpm = bankF()
                            for c in range(8):
                                op("pe", MM(pm, onT[:, c, :], Wout[:, c, nb * 512:(nb + 1) * 512], c == 0, c == 7),
                                   reads=[k_onT, k_wout], writes=[kpm])
                            psm.append(pm)
                            kpsm.append(kpm)
                        mixer_epilogue(l, t, psm, kpsm, src_d, k_src)
                        SK(False)

            if MULTI:
                run_phase(True)
                P.barrier()
                arena_release(mark)
                exchange("l%d" % l, S[:].rearrange("p h v -> p (h v)"), k_S, 1024, dtot[:], k_dtot, 8,
                         lambda a: a.rearrange("p (h v) -> p h v", v=128), lambda d_: d_.unsqueeze(2).broadcast_to([128, 8, 128]))
                op("act", A_(Sbf[0][:], S[:], AF.Copy), reads=[k_S], writes=[k_Sbf[0]])
                P.barrier()
                arena_release(mark)
            run_phase(False)

        def mlp_ple_segment(l, seg, dst_d, k_dst, last):
            Xacc = regA[:, 0:16 * 2048].bitcast(F32).rearrange("p (t d) -> p t d", t=16)
            kX = [Tk("Xacc%d" % i) for i in range(16)]
            XTh["XT"] = tmp("XT", [128, 8, 2048], BF16, bufs=1)[0]
            XT = XTh["XT"]
            for tl in range(16):
                t = seg * 16 + tl
                dma("sp", d_h[tl % 2], Xacc[:, tl, :], hs_d[t * 128:(t + 1) * 128, :], reads=[k_hs[t]], writes=[kX[tl]])
                hb, k_hbb = tmp("ln_hb", [128, D], BF16)
                op("act", A_(hb[:], Xacc[:, tl, :], AF.Copy, scale=1.0 / ALPHA), reads=[kX[tl]], writes=[k_hbb])
                to_XT(hb, k_hbb, tl)
            load_ln(l, 1)
            wbuf = [regB[:, i * 8192:(i + 1) * 8192] for i in range(2)]
            k_wb = [Tk("mwb0"), Tk("mwb1")]
            w1v = w1_d[l].rearrange("(k p) n -> p k n", p=128)
            w2v = w2_d[l].rearrange("(k p) n -> p k n", p=128)
            for e8 in range(8):
                wb = wbuf[e8 % 2]
                kwb = k_wb[e8 % 2]
                w1e = wb[:, 0:4096].rearrange("p (k n) -> p k n", k=8)
                w2e = wb[:, 4096:8192].rearrange("p (k n) -> p k n", k=4)
                for kc in range(8):
                    dma("pool", d_mw[e8 % 2], w1e[:, kc, :], w1v[:, kc, e8 * 512:(e8 + 1) * 512], writes=[kwb])
                for fc in range(4):
                    dma("pool", d_mw[e8 % 2], w2e[:, fc, :], w2v[:, e8 * 4 + fc, :], writes=[kwb])
                for tb in range(4):
                    hid, k_hid = tmp("hid", [128, 4, 512], BF16, bufs=2)
                    for fc in range(4):
                        ps, kps = bankF()
                        for kc in range(8):
                            op("pe", MM(ps, w1e[:, kc, fc * 128:(fc + 1) * 128], XT[:, kc, tb * 512:(tb + 1) * 512], kc == 0, kc == 7),
                               reads=[kwb] + kXT[tb * 4:tb * 4 + 4], writes=[kps])
                        rl, k_rl = tmp("rl", [128, 512])
                        op("act", A_(rl[:], ps, AF.Relu), reads=[kps], writes=[k_rl])
                        op("dve", TT(hid[:, fc, :], rl[:], rl[:], ALU.mult), reads=[k_rl], writes=[k_hid])
                    for ti in range(4):
                        tl = tb * 4 + ti
                        for nb in range(2):
                            ps, kps = bankF()
                            for fc in range(4):
                                op("pe", MM(ps, hid[:, fc, ti * 128:(ti + 1) * 128], w2e[:, fc, nb * 512:(nb + 1) * 512], fc == 0, fc == 3),
                                   reads=[k_hid, kwb], writes=[kps])
                            xa = Xacc[:, tl, nb * 512:(nb + 1) * 512]
                            op("dve", TT(xa, xa, ps, ALU.add), reads=[kX[tl], kps], writes=[kX[tl]])
            Wg = regB[:, 0:8192].rearrange("p (k n) -> p k n", k=8)
            Wp = regB[:, 8192:8192 + 2048].rearrange("p (k n) -> p k n", k=2)
            wgv = wg_d[l].rearrange("(k p) n -> p k n", p=128)
            wpv = wp_d[l].rearrange("(k p) n -> p k n", p=128)
            for kc in range(8):
                dma("pool", d_pl, Wg[:, kc, :], wgv[:, kc, :], writes=[k_wb[0]])
            for kc in range(2):
                dma("pool", d_pl, Wp[:, kc, :], wpv[:, kc, :], writes=[k_wb[1]])
            for tl in range(16):
                t = seg * 16 + tl
                hb, k_hbb = layer_norm(Xacc[:, tl, :], kX[tl], Xacc[:, tl, :], kX[tl])
                to_XT(hb, k_hbb, tl)
                pt, k_pt = tmp("pt", [128, 256])
                dma("sp", d_p[tl % 2], pt[:], p_d[l, t * 128:(t + 1) * 128, :], writes=[k_pt])
                ptb, k_ptb = tmp("ptb", [128, 256], BF16)
                op("dve", CP(ptb[:], pt[:]), reads=[k_pt], writes=[k_ptb])
                pb, kpb = bankB()
                for c in range(2):
                    op("pe", TR(pb[:, c * 128:(c + 1) * 128], ptb[:, c * 128:(c + 1) * 128], identB[:]), reads=[k_ptb, k_idb], writes=[kpb])
                pT, k_pT = tmp("pT", [128, 2, 128], BF16)
                op("act", A_(pT[:], pb[:, 0:256].rearrange("p (c t) -> p c t", c=2), AF.Copy), reads=[kpb], writes=[k_pT])
                xn, k_xn = tmp("xn", [128, D])
                for nb in range(2):
                    psg, kpsg = bankF()
                    for kc in range(8):
                        op("pe", MM(psg, XT[:, kc, tl * 128:(tl + 1) * 128], Wg[:, kc, nb * 512:(nb + 1) * 512], kc == 0, kc == 7),
                           reads=[kXT[tl], k_wb[0]], writes=[kpsg])
                    psp, kpsp = bankF()
                    for kc in range(2):
                        op("pe", MM(psp, pT[:, kc, :], Wp[:, kc, nb * 512:(nb + 1) * 512], kc == 0, kc == 1),
                           reads=[k_pT, k_wb[1]], writes=[kpsp])
                    sgm, k_sgm = tmp("sgm", [128, 512])
                    op("act", A_(sgm[:], psg, AF.Sigmoid), reads=[kpsg], writes=[k_sgm])
                    op("dve", TT(sgm[:], sgm[:], psp, ALU.mult), reads=[k_sgm, kpsp], writes=[k_sgm])
                    op("dve", TT(xn[:, nb * 512:(nb + 1) * 512], sgm[:], Xacc[:, tl, nb * 512:(nb + 1) * 512], ALU.add),
                       reads=[k_sgm, kX[tl]], writes=[k_xn])
                dma("sp", d_st, dst_d[t * 128:(t + 1) * 128, :], xn[:], reads=[k_xn], writes=[k_dst[t]])
                if MULTI and tl == 15 and (l + 1) in layers and (l + 1) % 2 == 0:
                    xnb, k_xnb = tmp("ln_hb", [128, D], BF16)
                    op("act", A_(xnb[:], xn[:], AF.Copy), reads=[k_xn], writes=[k_xnb])
                    pb, kpb = bankB()
                    for c in range(8):
                        op("pe", TR(pb[:, c * 128:(c + 1) * 128], xnb[:, c * 128:(c + 1) * 128], identB[:]), reads=[k_xnb, k_idb], writes=[kpb])
                    xl, k_xl = tmp("xl", [128, 24], F32, bufs=1)
                    op("act", A_(xl[:].rearrange("p (c t) -> p c t", c=8), pb.rearrange("p (c t) -> p c t", c=8)[:, :, 125:128], AF.Copy),
                       reads=[kpb], writes=[k_xl])
                    halo_exchange("l%d" % l, xl[:], k_xl)

        src_d, k_src = x_d, None
        k_out = [Tk("out%d" % i) for i in range(NT)]
        for li, l in enumerate(layers):
            last = li == len(layers) - 1
            if l % 2 == 0:
                ssd_layer(l, src_d, k_src)
            else:
                hgrn_layer(l, src_d, k_src)
            P.barrier()
            arena_reset()
            dst_d, k_dst = (out_d, k_out) if last else (xs_d, k_xs)
            for seg in range(NSEG):
                mlp_ple_segment(l, seg, dst_d, k_dst, last)
                P.barrier()
                arena_reset()
            src_d, k_src = xs_d, k_xs
        P.barrier()
        block = es.enter_context(nc.Block())
        P.replay(block)
    return nc


def make_cst():
    c = np.zeros((128, 5 * 128 + 512), np.float32)
    i = np.arange(128)
    c[:, 0:128] = np.eye(128)
    c[:, 128:256] = (i[:, None] <= i[None, :])
    c[:, 256:384] = (i[:, None] > i[None, :])
    c[:, 384:512] = 1.0
    c[:, 512:640] = (i[:, None] <= i[None, :]) & ((i[:, None] // 64) == (i[None, :] // 64))
    r = np.ones(512, np.float32)
    r[0::64] = 0.0
    c[:, 640:1152] = r[None, :]
    return c


def prep_weights(inp):
    f = lambda a: np.ascontiguousarray(np.asarray(a, dtype=np.float32))
    w = {}
    for k in ["ssd_w_in", "ssd_dt_bias", "ssd_a_log", "ssd_d", "ssd_w_out", "hgrn_w_in", "hgrn_norm_w", "hgrn_w_out",
              "ln_g", "ln_b", "mlp_w1", "mlp_w2", "ple_w_proj", "ple_w_gate"]:
        w[k] = f(inp[k])
    cw = f(inp["ssd_conv_w"])
    w["ssd_cw"] = f(cw.reshape(2, 4, 24, 128).transpose(0, 3, 2, 1).reshape(2, 128, 96))
    w["ssd_cb"] = f(f(inp["ssd_conv_b"]).reshape(2, 24, 128).transpose(0, 2, 1))
    w["ssd_nw"] = f(f(inp["ssd_norm_w"]).reshape(2, 16, 128).transpose(0, 2, 1))
    w["hgrn_lb"] = f(f(inp["hgrn_lower_bounds"]).reshape(2, 8, 128).transpose(0, 2, 1))
    w["cst"] = make_cst()
    return w


def run(inp, T, layers, ncores=8):
    w = prep_weights(inp)
    x = np.asarray(inp["x"], np.float32)
    p = np.asarray(inp["p"], np.float32)
    nc = build(T, layers)
    in_maps = []
    for c in range(ncores):
        b = c if c < 2 else 0
        m = dict(w)
        m["x"] = np.ascontiguousarray(x[b, :T])
        m["p"] = np.ascontiguousarray(p[:, b, :T])
        m["msk"] = np.zeros((128, 18), np.float32)
        m["xhT"] = np.zeros((128, 24), np.float32)
        in_maps.append(m)
    res = run_bass_kernel_spmd(nc, in_maps, core_ids=list(range(ncores)))
    return np.stack([res.results[0]["out"], res.results[1]["out"]], axis=0)


def run_multi(inp, layers, nseq=4, trace=False):
    w = prep_weights(inp)
    x = np.asarray(inp["x"], np.float32)
    p = np.asarray(inp["p"], np.float32)
    TS_ = 2048
    nc = build(TS_, layers, MULTI=True)
    in_maps = []
    for c in range(8):
        b, r = c // 4, c % 4
        m = dict(w)
        m["x"] = np.ascontiguousarray(x[b, r * TS_:(r + 1) * TS_])
        m["p"] = np.ascontiguousarray(p[:, b, r * TS_:(r + 1) * TS_])
        msk = np.zeros((128, 18), np.float32)
        slot = lambda bb, rr: bb * 3 + rr
        if r < 3:
            msk[:, slot(b, r)] = 1.0
        for rr in range(r):
            msk[:, 6 + slot(b, rr)] = 1.0
        if r > 0:
            msk[:, 12 + slot(b, r - 1)] = 1.0
        m["msk"] = msk
        xh = np.zeros((128, 24), np.float32)
        if r > 0:
            prev = x[b, r * TS_ - 3:r * TS_, :]
            xh = np.ascontiguousarray(prev.reshape(3, 8, 128).transpose(2, 1, 0).reshape(128, 24))
        m["xhT"] = xh
        in_maps.append(m)
    res = run_bass_kernel_spmd(nc, in_maps, core_ids=list(range(8)))
    out = np.zeros((2, 4 * TS_, D), np.float32)
    for c in range(8):
        out[c // 4, (c % 4) * TS_:(c % 4 + 1) * TS_] = res.results[c]["out"]
    return out


def kernel(**inputs):
    return run_multi(inputs, [0, 1, 2, 3]).astype(np.float32)
```

```python
import numpy as np
from contextlib import ExitStack
import concourse.bass as bass
import concourse.mybir as mybir
from concourse.bass_utils import run_bass_kernel_spmd

F32, BF16 = mybir.dt.float32, mybir.dt.bfloat16
AF = mybir.ActivationFunctionType
ALU = mybir.AluOpType

D = 1024
DEPTH = 4
ALPHA = (2.0 * DEPTH) ** 0.25
EPS = 1e-5
SAME_ENGINE_SYNC = True


class Tk:
    __slots__ = ("name", "w", "r")

    def __init__(self, name=""):
        self.name = name
        self.w = {}
        self.r = {}


class Eng:
    def __init__(self, name, sem):
        self.name = name
        self.sem = sem
        self.n = 0
        self.waited = {}
        self.ops = []


class Prog:
    def __init__(self, nc, es):
        self.nc, self.es = nc, es
        self.E = {}
        for name in ["pe", "act", "dve", "pool", "sp"]:
            self.E[name] = Eng(name, es.enter_context(nc.semaphore("s_" + name)))
        self.dsems = []
        self.skip = False
        self.dset = set()

    def dsem(self, name):
        d = Eng("d_" + name, self.es.enter_context(self.nc.semaphore("d_" + name)))
        self.dsems.append(d)
        self.dset.add(d)
        return d

    def _waits(self, e, reads, writes, skip_same):
        need = {}
        for t in reads:
            for k, v in t.w.items():
                if need.get(k, 0) < v:
                    need[k] = v
        for t in writes:
            for k, v in t.w.items():
                if need.get(k, 0) < v:
                    need[k] = v
            for k, v in t.r.items():
                if need.get(k, 0) < v:
                    need[k] = v
        for k, v in need.items():
            if k in self.dset:
                v = k.n
            if k is e and skip_same:
                continue
            if e.waited.get(k, 0) >= v:
                continue
            e.waited[k] = v
            e.ops.append(("w", k.sem, v))

    def op(self, en, fn, reads=(), writes=()):
        if self.skip:
            return
        e = self.E[en]
        self._waits(e, reads, writes, en == "pe" or not SAME_ENGINE_SYNC)
        e.n += 1
        e.ops.append(("i", fn, e.sem, 1))
        for t in reads:
            t.r[e] = e.n
        for t in writes:
            t.w = {e: e.n}
            t.r = {}

    def cc(self, d, fn, reads=(), writes=()):
        q = self.E["pool"]
        self._waits(q, reads, writes, False)
        d.n += 1
        q.ops.append(("i", fn, d.sem, 1))
        for t in reads:
            t.r[d] = d.n
        for t in writes:
            t.w = {d: d.n}
            t.r = {}

    def dma(self, qn, d, out, in_, reads=(), writes=()):
        if self.skip:
            return
        q = self.E[qn]
        self._waits(q, reads, writes, False)
        d.n += 16
        q.ops.append(("i", lambda eng: eng.dma_start(out=out, in_=in_), d.sem, 16))
        for t in reads:
            t.r[d] = d.n
        for t in writes:
            t.w[d] = d.n
            t.r = {}

    def barrier(self):
        allk = list(self.E.values()) + self.dsems
        for e in self.E.values():
            for k in allk:
                if k is e or k.n == 0:
                    continue
                if e.waited.get(k, 0) < k.n:
                    e.waited[k] = k.n
                    e.ops.append(("w", k.sem, k.n))

    def replay(self, block):
        def run(e):
            def f(eng):
                for o in e.ops:
                    if o[0] == "w":
                        eng.wait_ge(o[1], o[2])
                    else:
                        o[1](eng).then_inc(o[2], o[3])
            return f
        block.tensor(run(self.E["pe"]))
        block.scalar(run(self.E["act"]))
        block.vector(run(self.E["dve"]))
        block.gpsimd(run(self.E["pool"]))
        block.sync(run(self.E["sp"]))


def A_(out, in_, func, **kw):
    return lambda e: e.activation(out=out, in_=in_, func=func, **kw)


def TT(out, in0, in1, op):
    return lambda e: e.tensor_tensor(out=out, in0=in0, in1=in1, op=op)


def TS(out, in0, s1, s2, op0, op1=None):
    if op1 is None:
        return lambda e: e.tensor_scalar(out=out, in0=in0, scalar1=s1, scalar2=None, op0=op0)
    return lambda e: e.tensor_scalar(out=out, in0=in0, scalar1=s1, scalar2=s2, op0=op0, op1=op1)


def STT(out, in0, sc, in1, op0, op1):
    return lambda e: e.scalar_tensor_tensor(out=out, in0=in0, scalar=sc, in1=in1, op0=op0, op1=op1)


def CP(out, in_):
    return lambda e: e.tensor_copy(out=out, in_=in_)


def MM(out, lhsT, rhs, start=True, stop=True):
    return lambda e: e.matmul(out, lhsT=lhsT, rhs=rhs, start=start, stop=stop)


def TR(out, in_, ident):
    return lambda e: e.transpose(out, in_, ident)


def build(T, layers, MULTI=False):
    NSEG = T // 2048
    NT = T // 128
    BLK = 256
    TPB = BLK // 128
    NB = T // BLK
    nc = bass.Bass("TRN2", target_bir_lowering=False)

    def din(name, shape):
        return nc.dram_tensor(name, list(shape), F32, kind="ExternalInput").ap()

    x_d = din("x", [T, D])
    p_d = din("p", [DEPTH, T, 256])
    ssd_w_in = din("ssd_w_in", [2, D, 5152])
    ssd_cw = din("ssd_cw", [2, 128, 24 * 4])
    ssd_cb = din("ssd_cb", [2, 128, 24])
    ssd_dtb = din("ssd_dt_bias", [2, 32])
    ssd_alog = din("ssd_a_log", [2, 32])
    ssd_dd = din("ssd_d", [2, 32])
    ssd_nw = din("ssd_nw", [2, 128, 16])
    ssd_w_out = din("ssd_w_out", [2, 2048, D])
    hg_w_in = din("hgrn_w_in", [2, D, 4096])
    hg_lb = din("hgrn_lb", [2, 128, 8])
    hg_nw = din("hgrn_norm_w", [2, 128])
    hg_w_out = din("hgrn_w_out", [2, D, D])
    ln_g = din("ln_g", [DEPTH, 2, D])
    ln_b = din("ln_b", [DEPTH, 2, D])
    w1_d = din("mlp_w1", [DEPTH, D, 4096])
    w2_d = din("mlp_w2", [DEPTH, 4096, D])
    wp_d = din("ple_w_proj", [DEPTH, 256, D])
    wg_d = din("ple_w_gate", [DEPTH, D, D])
    cst_d = din("cst", [128, 5 * 128 + 512])
    msk_d = din("msk", [128, 18])
    xhT_d = din("xhT", [128, 24])
    out_d = nc.dram_tensor("out", [T, D], F32, kind="ExternalOutput").ap()
    xs_d = nc.dram_tensor("xscr", [T, D], F32).ap()
    hs_d = nc.dram_tensor("hscr", [T, D], F32).ap()

    es = ExitStack()
    with es:
        P = Prog(nc, es)
        op, dma = P.op, P.dma

        def sb(name, shape, dt=F32):
            return es.enter_context(nc.sbuf_tensor(name, list(shape), dt))

        cst = sb("cst_sb", [128, 5 * 128 + 512])
        k_cst = Tk("cst")
        d_cst = P.dsem("cst")
        dma("sp", d_cst, cst[:], cst_d, writes=[k_cst])
        identF = cst[:, 0:128]
        triU = cst[:, 128:256]
        SU = cst[:, 256:384]
        ones = cst[:, 384:512]
        maskBD = cst[:, 512:640]
        rst = cst[:, 640:1152]
        msk = sb("msk_sb", [128, 18])
        xh_sb = sb("xh_sb", [128, 24])
        k_msk, k_xh = Tk("msk"), Tk("xh")
        dma("sp", d_cst, msk[:], msk_d, writes=[k_msk])
        dma("sp", d_cst, xh_sb[:], xhT_d, writes=[k_xh])
        oh6, pr6, sel6 = msk[:, 0:6], msk[:, 6:12], msk[:, 12:18]
        identB = sb("identB", [128, 128], BF16)
        k_idb = Tk()
        op("dve", CP(identB[:], identF), reads=[k_cst], writes=[k_idb])

        psF = es.enter_context(nc.psum_tensor("psF", [128, 6, 512], F32))
        psB = es.enter_context(nc.psum_tensor("psB", [128, 2, 1024], BF16))
        kF = [Tk("psF%d" % i) for i in range(6)]
        kB = [Tk("psB%d" % i) for i in range(2)]
        rot = {"F": 0, "B": 0}

        def bankF():
            i = rot["F"] % 6
            rot["F"] += 1
            return psF[:, i, :], kF[i]

        def bankB():
            i = rot["B"] % 2
            rot["B"] += 1
            return psB[:, i, :], kB[i]

        ARENA = 80 * 1024
        arena = sb("arena", [128, ARENA // 2], BF16)
        pools = {}
        ar = {"off": 0}

        def carve(shape, dt):
            n = 1
            for d_ in shape[1:]:
                n *= d_
            nb = n * (4 if dt == F32 else 2)
            nb = (nb + 31) // 32 * 32
            off = ar["off"]
            assert off + nb <= ARENA, ("arena overflow", off, nb)
            ar["off"] = off + nb
            ar["hw"] = max(ar.get("hw", 0), off + nb)
            a = arena[:, off // 2:(off + nb) // 2]
            if dt == F32:
                a = a.bitcast(F32)
            a = a[:, 0:n]
            if len(shape) == 3:
                a = a.rearrange("p (a b) -> p a b", a=shape[1])
            elif len(shape) == 4:
                a = a.rearrange("p (a b c) -> p a b c", a=shape[1], b=shape[2])
            return a

        def arena_reset():
            print("arena high-water", ar.get("hw", 0))
            ar["hw"] = 0
            pools.clear()
            ar["off"] = 0

        def arena_mark():
            return (ar["off"], set(pools.keys()))

        def arena_release(m):
            for k in list(pools.keys()):
                if k not in m[1]:
                    del pools[k]
            ar["off"] = m[0]

        cur = {"phaseA": False}

        def SK(b):
            P.skip = bool(b) and cur["phaseA"]

        class _T:
            def __init__(self, a):
                self.a = a

            def __getitem__(self, key):
                return self.a[key]

        def tmp(name, shape, dt=F32, bufs=2):
            if name not in pools:
                pools[name] = [[(_T(carve(shape, dt)), Tk(name)) for i in range(bufs)], 0]
            pl = pools[name]
            t, k = pl[0][pl[1] % len(pl[0])]
            pl[1] += 1
            return t, k

        regA = sb("regA", [128, 8 * 5152], BF16)
        regB = sb("regB", [128, 16 * 1024], BF16)
        kXT = [Tk("XT%d" % i) for i in range(16)]
        XTh = {}
        d_w = P.dsem("w")
        d_w2 = P.dsem("w2")
        d_x = [P.dsem("x%d" % i) for i in range(2)]
        d_st = P.dsem("st")
        d_par = P.dsem("par")
        d_mw = [P.dsem("mw%d" % i) for i in range(2)]
        d_pl = P.dsem("pl")
        d_h = [P.dsem("h%d" % i) for i in range(2)]
        d_p = [P.dsem("p%d" % i) for i in range(2)]
        k_xs = [Tk("xs%d" % i) for i in range(NT)]
        k_hs = [Tk("hs%d" % i) for i in range(NT)]

        gb = sb("gb", [128, 2, D])
        k_gb = Tk("gb")

        def load_ln(l, j):
            dma("sp", d_par, gb[:, 0, :], ln_g[l, j:j + 1, :].broadcast_to([128, D]), writes=[k_gb])
            dma("sp", d_par, gb[:, 1, :], ln_b[l, j:j + 1, :].broadcast_to([128, D]), writes=[k_gb])

        def layer_norm(src, k_src, dst_f32, k_dst, want_bf=True):
            st, k_st = tmp("ln_st", [128, 2, 6])
            for i in range(2):
                op("dve", (lambda o, i_: lambda e: e.bn_stats(out=o, in_=i_))(st[:, i, :], src[:, i * 512:(i + 1) * 512]),
                   reads=[k_src], writes=[k_st])
            mv, k_mv = tmp("ln_mv", [128, 4])
            op("dve", (lambda o, i_: lambda e: e.bn_aggr(out=o, in_=i_))(mv[:, 0:2], st[:].rearrange("p a b -> p (a b)")),
               reads=[k_st], writes=[k_mv])
            op("dve", TS(mv[:, 2:3], mv[:, 1:2], EPS, None, ALU.add), reads=[k_mv], writes=[k_mv])
            op("act", A_(mv[:, 2:3], mv[:, 2:3], AF.Sqrt), reads=[k_mv], writes=[k_mv])
            op("dve", (lambda o, i_: lambda e: e.reciprocal(out=o, in_=i_))(mv[:, 2:3], mv[:, 2:3]), reads=[k_mv], writes=[k_mv])
            op("dve", TS(mv[:, 3:4], mv[:, 0:1], mv[:, 2:3], -1.0, ALU.mult, ALU.mult), reads=[k_mv], writes=[k_mv])
            op("act", A_(dst_f32, src, AF.Identity, scale=mv[:, 2:3], bias=mv[:, 3:4]), reads=[k_src, k_mv], writes=[k_dst])
            op("dve", TT(dst_f32, dst_f32, gb[:, 0, :], ALU.mult), reads=[k_dst, k_gb], writes=[k_dst])
            op("dve", TT(dst_f32, dst_f32, gb[:, 1, :], ALU.add), reads=[k_dst, k_gb], writes=[k_dst])
            if not want_bf:
                return None, None
            hb, k_hbb = tmp("ln_hb", [128, D], BF16)
            op("act", A_(hb[:], dst_f32, AF.Copy), reads=[k_dst], writes=[k_hbb])
            return hb, k_hbb

        def to_XT(hb, k_hbb, tl):
            pb, kpb = bankB()
            for c in range(8):
                op("pe", TR(pb[:, c * 128:(c + 1) * 128], hb[:, c * 128:(c + 1) * 128], identB[:]),
                   reads=[k_hbb, k_idb], writes=[kpb])
            op("act", A_(XTh["XT"][:, :, tl * 128:(tl + 1) * 128], pb.rearrange("p (c t) -> p c t", c=8), AF.Copy),
               reads=[kpb], writes=[kXT[tl]])

        def mixer_epilogue(l, t, psm, kpsm, xsrc, k_xsrc):
            xt, k_xt = tmp("ep_x", [128, D], F32, bufs=1)
            dma("sp", d_x[t % 2], xt[:], xsrc[t * 128:(t + 1) * 128, :], reads=[k_xsrc[t]] if k_xsrc else [], writes=[k_xt])
            for nb in range(2):
                sl = slice(nb * 512, (nb + 1) * 512)
                op("dve", STT(xt[:, sl], xt[:, sl], ALPHA, psm[nb], ALU.mult, ALU.add),
                   reads=[k_xt, kpsm[nb]], writes=[k_xt])
            layer_norm(xt[:], k_xt, xt[:], k_xt, want_bf=False)
            op("act", A_(xt[:], xt[:], AF.Copy, scale=ALPHA), reads=[k_xt], writes=[k_xt])
            dma("sp", d_st, hs_d[t * 128:(t + 1) * 128, :], xt[:], reads=[k_xt], writes=[k_hs[t]])

        d_xc = [P.dsem("xc0"), P.dsem("xc1")]
        d_cc = P.dsem("cc")

        def allreduce(bin_, bout, k_bin, k_bout):
            P.cc(d_cc, lambda g: g.collective_compute("AllReduce", ALU.add, replica_groups=[list(range(8))],
                                                      ins=[bin_.ap().opt()], outs=[bout.ap().opt()]),
                 reads=[k_bin], writes=[k_bout])

        def exchange(tag, S2d, k_S, W_S, Dt, k_D, W_D, viewS, bcD):
            W = W_S + W_D
            bin_ = nc.dram_tensor("xin_" + tag, [6 * 128, W], F32)
            bout = nc.dram_tensor("xout_" + tag, [6 * 128, W], F32)
            k_bin, k_bout = Tk("bin"), Tk("bout")
            for j in range(6):
                stg, k_stg = tmp("xstg", [128, W], F32, bufs=2)
                op("dve", TS(stg[:, 0:W_S], S2d, oh6[:, j:j + 1], None, ALU.mult), reads=[k_S, k_msk], writes=[k_stg])
                op("dve", TS(stg[:, W_S:W], Dt, oh6[:, j:j + 1], None, ALU.mult), reads=[k_D, k_msk], writes=[k_stg])
                dma("sp", d_xc[0], bin_.ap()[j * 128:(j + 1) * 128, :], stg[:], reads=[k_stg], writes=[k_bin])
            allreduce(bin_, bout, k_bin, k_bout)
            op("dve", lambda e: e.memset(S2d, 0.0), reads=[k_bin], writes=[k_S])
            for j in range(6):
                stg, k_stg = tmp("xstg", [128, W], F32, bufs=2)
                dma("pool", d_xc[1], stg[:], bout.ap()[j * 128:(j + 1) * 128, :], reads=[k_bout], writes=[k_stg])
                de, k_de = tmp("xde", [128, W_D], F32, bufs=2)
                op("dve", TS(de[:], stg[:, W_S:W], 1.0, pr6[:, j:j + 1], ALU.subtract, ALU.mult), reads=[k_stg, k_msk], writes=[k_de])
                op("dve", TS(de[:], de[:], 1.0, None, ALU.add), reads=[k_de], writes=[k_de])
                op("dve", TT(viewS(S2d), viewS(S2d), bcD(de[:]), ALU.mult), reads=[k_S, k_de], writes=[k_S])
                op("dve", STT(S2d, stg[:, 0:W_S], pr6[:, j:j + 1], S2d, ALU.mult, ALU.add), reads=[k_stg, k_S, k_msk], writes=[k_S])

        def halo_exchange(tag, xl, k_xl):
            bin_ = nc.dram_tensor("hin_" + tag, [6 * 128, 24], F32)
            bout = nc.dram_tensor("hout_" + tag, [6 * 128, 24], F32)
            k_bin, k_bout = Tk("hbin"), Tk("hbout")
            stg, k_stg = tmp("hstg", [128, 6, 24], F32, bufs=1)
            for j in range(6):
                op("dve", TS(stg[:, j, :], xl, oh6[:, j:j + 1], None, ALU.mult), reads=[k_xl, k_msk], writes=[k_stg])
            dma("sp", d_xc[0], bin_.ap().rearrange("(j p) w -> p j w", p=128), stg[:], reads=[k_stg], writes=[k_bin])
            allreduce(bin_, bout, k_bin, k_bout)
            stg2, k_stg2 = tmp("hstg2", [128, 6, 24], F32, bufs=1)
            dma("pool", d_xc[1], stg2[:], bout.ap().rearrange("(j p) w -> p j w", p=128), reads=[k_bout], writes=[k_stg2])
            op("dve", lambda e: e.memset(xh_sb[:], 0.0), writes=[k_xh])
            for j in range(6):
                op("dve", STT(xh_sb[:], stg2[:, j, :], sel6[:, j:j + 1], xh_sb[:], ALU.mult, ALU.add),
                   reads=[k_stg2, k_xh, k_msk], writes=[k_xh])

        def load_xblock(src_d, k_src_tiles, blk):
            XTb, k_XTb = tmp("XTb", [128, 8, BLK], BF16, bufs=1)
            for a in range(TPB):
                t = blk * TPB + a
                xt, k_xt = tmp("ep_x", [128, D], F32, bufs=1)
                dma("sp", d_x[t % 2], xt[:], src_d[t * 128:(t + 1) * 128, :],
                    reads=[k_src_tiles[t]] if k_src_tiles else [], writes=[k_xt])
                xb, k_xb = tmp("xb", [128, D], BF16, bufs=1)
                op("dve", CP(xb[:], xt[:]), reads=[k_xt], writes=[k_xb])
                pb, kpb = bankB()
                for c in range(8):
                    op("pe", TR(pb[:, c * 128:(c + 1) * 128], xb[:, c * 128:(c + 1) * 128], identB[:]),
                       reads=[k_xb, k_idb], writes=[kpb])
                op("act", A_(XTb[:, :, a * 128:(a + 1) * 128], pb.rearrange("p (c t) -> p c t", c=8), AF.Copy),
                   reads=[kpb], writes=[k_XTb])
            return XTb, k_XTb

        def ssd_layer(l, src_d, k_src):
            j = l // 2
            Win = regA[:].rearrange("p (k n) -> p k n", k=8)
            Wout = regB[:].rearrange("p (k n) -> p k n", k=16)
            k_win, k_wout = Tk("win"), Tk("wout")
            wv = ssd_w_in[j].rearrange("(k p) n -> p k n", p=128)
            for kc in range(8):
                for (a, b) in ((0, 2048), (2048, 4096), (4096, 5152)):
                    dma("pool", d_w, Win[:, kc, a:b], wv[:, kc, a:b], writes=[k_win])
            wo = ssd_w_out[j].rearrange("(k p) n -> p k n", p=128)
            for kc in range(16):
                dma("pool", d_w2, Wout[:, kc, :], wo[:, kc, :], writes=[k_wout])
            par, _ = tmp("ssd_par", [128, 24 * 4 + 24 + 16 + 32 * 3], F32, bufs=1)
            k_par = Tk("par")
            cw = par[:, 0:96].rearrange("p (c k) -> p c k", k=4)
            cb = par[:, 96:120]
            nw = par[:, 120:136]
            dtb = par[:, 136:168]
            an = par[:, 168:200]
            dd = par[:, 200:232]
            dma("sp", d_par, par[:, 0:96], ssd_cw[j], writes=[k_par])
            dma("sp", d_par, cb, ssd_cb[j], writes=[k_par])
            dma("sp", d_par, nw, ssd_nw[j], writes=[k_par])
            dma("sp", d_par, dtb, ssd_dtb[j:j + 1, :].broadcast_to([128, 32]), writes=[k_par])
            dma("sp", d_par, an, ssd_alog[j:j + 1, :].broadcast_to([128, 32]), writes=[k_par])
            dma("sp", d_par, dd, ssd_dd[j:j + 1, :].broadcast_to([128, 32]), writes=[k_par])
            op("act", A_(an, an, AF.Exp), reads=[k_par], writes=[k_par])
            op("dve", TS(an, an, -1.0, None, ALU.mult), reads=[k_par], writes=[k_par])
            load_ln(l, 0)
            S, _ = tmp("ssd_S", [128, 2048], F32, bufs=1)
            Sbf, _ = tmp("ssd_Sbf", [128, 2048], BF16, bufs=1)
            halo, _ = tmp("ssd_halo", [128, 24, 3], F32, bufs=1)
            k_S, k_Sbf, k_halo = Tk("S"), Tk("Sbf"), Tk("halo")
            op("dve", lambda e: e.memset(S[:], 0.0), writes=[k_S])
            op("dve", lambda e: e.memset(Sbf[:], 0.0), writes=[k_Sbf])

            halo0, _ = tmp("ssd_halo0", [128, 24, 3], F32, bufs=1)
            totsum, _ = tmp("ssd_totsum", [128, 64], F32, bufs=1)
            k_tot = Tk("tot")
            dtot, _ = tmp("ssd_dtot", [128, 32], F32, bufs=1)
            k_dtot = Tk("dtot")
            k_h0 = Tk("halo0")
            if MULTI:
                xhb, k_xhb = tmp("xhb", [128, 8, 3], BF16, bufs=1)
                op("dve", CP(xhb[:], xh_sb[:].rearrange("p (k t) -> p k t", k=8)), reads=[k_xh], writes=[k_xhb])
                psh, kpsh = bankF()
                for cc in range(24):
                    for kc in range(8):
                        op("pe", MM(psh[:, cc * 3:(cc + 1) * 3], Win[:, kc, 2048 + cc * 128:2048 + (cc + 1) * 128], xhb[:, kc, :], kc == 0, kc == 7),
                           reads=[k_win, k_xhb], writes=[kpsh])
                op("act", A_(halo0[:].rearrange("p c t -> p (c t)"), psh[:, 0:72], AF.Copy), reads=[kpsh], writes=[k_h0])
            else:
                op("dve", lambda e: e.memset(halo0[:], 0.0), writes=[k_h0])
            mark = arena_mark()

            def run_phase(phaseA):
                cur["phaseA"] = phaseA
                op("act", A_(halo[:], halo0[:], AF.Copy), reads=[k_h0], writes=[k_halo])
                if phaseA:
                    op("dve", lambda e: e.memset(S[:], 0.0), writes=[k_S])
                    op("dve", lambda e: e.memset(totsum[:], 0.0), writes=[k_tot])
                for blk in range(NB):
                    XTb, k_XTb = load_xblock(src_d, k_src, blk)
                    xbcT, k_xbcT = tmp("xbcT", [128, 24, BLK], BF16, bufs=1)
                    pend_silu = []
                    for cc in range(24):
                        SK(cc >= 20)
                        ps, kps = bankF()
                        for kc in range(8):
                            op("pe", MM(ps[:, 0:BLK], Win[:, kc, 2048 + cc * 128:2048 + (cc + 1) * 128], XTb[:, kc, :], kc == 0, kc == 7),
                               reads=[k_win, k_XTb], writes=[kps])
                        u, k_u = tmp("u", [128, BLK + 3])
                        op("act", A_(u[:, 0:3], halo[:, cc, :], AF.Copy), reads=[k_halo], writes=[k_u])
                        op("act", A_(u[:, 3:BLK + 3], ps[:, 0:BLK], AF.Copy), reads=[kps], writes=[k_u])
                        op("act", A_(halo[:, cc, :], u[:, BLK:BLK + 3], AF.Copy), reads=[k_u], writes=[k_halo])
                        acc, k_acc = tmp("acc", [128, BLK])
                        op("dve", TS(acc[:], u[:, 0:BLK], cw[:, cc, 0:1], cb[:, cc:cc + 1], ALU.mult, ALU.add),
                           reads=[k_u, k_par], writes=[k_acc])
                        for k in range(1, 4):
                            op("dve", STT(acc[:], u[:, k:k + BLK], cw[:, cc, k:k + 1], acc[:], ALU.mult, ALU.add),
                               reads=[k_u, k_par, k_acc], writes=[k_acc])
                        pend_silu.append((cc, acc, k_acc))
                        if len(pend_silu) > 1:
                            c0, a0, ka0 = pend_silu.pop(0)
                            SK(c0 >= 20)
                            op("act", A_(xbcT[:, c0, :], a0[:], AF.Silu), reads=[ka0], writes=[k_xbcT])
                            SK(cc >= 20)
                    for (c0, a0, ka0) in pend_silu:
                        SK(c0 >= 20)
                        op("act", A_(xbcT[:, c0, :], a0[:], AF.Silu), reads=[ka0], writes=[k_xbcT])
                    SK(False)
                    for ti in range(TPB):
                        t = blk * TPB + ti
                        cs = slice(ti * 128, (ti + 1) * 128)
                        ps, kps = bankF()
                        for kc in range(8):
                            op("pe", MM(ps[:, 0:32], XTb[:, kc, cs], Win[:, kc, 5120:5152], kc == 0, kc == 7),
                               reads=[k_win, k_XTb], writes=[kps])
                        sm, k_sm = tmp("sm", [128, 8, 32])
                        dt_, da, acs, dif, ea, dte, cd, f2 = [sm[:, i, :] for i in range(8)]
                        op("dve", TT(dt_, ps[:, 0:32], dtb, ALU.add), reads=[kps, k_par], writes=[k_sm])
                        op("act", A_(dt_, dt_, AF.Exp), reads=[k_sm], writes=[k_sm])
                        op("act", A_(dt_, dt_, AF.Ln, bias=1.0), reads=[k_sm], writes=[k_sm])
                        op("dve", TT(da, dt_, an, ALU.mult), reads=[k_sm, k_par], writes=[k_sm])
                        ps2, kps2 = bankF()
                        op("pe", MM(ps2[:, 0:32], triU, da), reads=[k_cst, k_sm], writes=[kps2])
                        op("pe", MM(ps2[:, 32:64], ones, da), reads=[k_cst, k_sm], writes=[kps2])
                        op("act", A_(acs, ps2[:, 0:32], AF.Copy), reads=[kps2], writes=[k_sm])
                        op("dve", TT(dif, ps2[:, 32:64], acs, ALU.subtract), reads=[kps2, k_sm], writes=[k_sm])
                        op("act", A_(ea, acs, AF.Exp), reads=[k_sm], writes=[k_sm])
                        op("act", A_(dte, dif, AF.Exp), reads=[k_sm], writes=[k_sm])
                        op("act", A_(cd, ps2[:, 32:64], AF.Exp), reads=[kps2], writes=[k_sm])
                        if phaseA:
                            op("dve", TT(totsum[:, 0:32], totsum[:, 0:32], ps2[:, 32:64], ALU.add), reads=[k_tot, kps2], writes=[k_tot])
                        op("dve", TT(f2, dt_, dte, ALU.mult), reads=[k_sm], writes=[k_sm])
                        op("dve", (lambda o, i_: lambda e: e.reciprocal(out=o, in_=i_))(dif, dt_), reads=[k_sm], writes=[k_sm])
                        op("dve", TT(dif, dif, dd, ALU.mult), reads=[k_sm, k_par], writes=[k_sm])
                        xdt, k_xdt = tmp("xdt", [128, 2048], BF16, bufs=1)
                        xw, k_xw = tmp("xw", [128, 2048], BF16, bufs=1)
                        for hf in range(2):
                            pb, kpb = bankB()
                            for c in range(8):
                                op("pe", TR(pb[:, c * 128:(c + 1) * 128], xbcT[:, hf * 8 + c, cs], identB[:]),
                                   reads=[k_xbcT, k_idb], writes=[kpb])
                            sl = slice(hf * 1024, (hf + 1) * 1024)
                            pv = pb.rearrange("p (h q) -> p h q", q=64)
                            hsl = slice(hf * 16, (hf + 1) * 16)
                            SK(True)
                            op("dve", TT(xdt[:, sl].rearrange("p (h q) -> p h q", q=64), pv,
                                         dt_[:, hsl].unsqueeze(2).broadcast_to([128, 16, 64]), ALU.mult),
                               reads=[kpb, k_sm], writes=[k_xdt])
                            SK(False)
                            op("dve", TT(xw[:, sl].rearrange("p (h q) -> p h q", q=64), pv,
                                         f2[:, hsl].unsqueeze(2).broadcast_to([128, 16, 64]), ALU.mult),
                               reads=[kpb, k_sm], writes=[k_xw])
                        btok, k_btok = tmp("btok", [128, 512], BF16, bufs=1)
                        pb, kpb = bankB()
                        for g in range(4):
                            op("pe", TR(pb[:, g * 128:(g + 1) * 128], xbcT[:, 16 + g, cs], identB[:]),
                               reads=[k_xbcT, k_idb], writes=[kpb])
                        op("act", A_(btok[:], pb[:, 0:512], AF.Copy), reads=[kpb], writes=[k_btok])
                        SK(True)
                        ps3, kps3 = bankF()
                        for g in range(4):
                            op("pe", MM(ps3[:, g * 128:(g + 1) * 128], xbcT[:, 16 + g, cs], xbcT[:, 20 + g, cs]),
                               reads=[k_xbcT], writes=[kps3])
                        cbm, k_cbm = tmp("cbm", [128, 4, 128], F32, bufs=1)
                        op("dve", TT(cbm[:], ps3.rearrange("p (g l) -> p g l", g=4),
                                     triU.unsqueeze(1).broadcast_to([128, 4, 128]), ALU.mult),
                           reads=[kps3, k_cst], writes=[k_cbm])
                        def make_MT(g_):
                          MT, k_MT = tmp("MT", [128, 8, 128], BF16, bufs=2)
                          for q in (2 * g_, 2 * g_ + 1):
                            R, k_R = tmp("R", [128, 4, 128], F32, bufs=1)
                            op("dve", TT(R[:], triU.unsqueeze(1).broadcast_to([128, 4, 128]),
                                         da[:, q * 4:(q + 1) * 4].unsqueeze(2).broadcast_to([128, 4, 128]), ALU.mult),
                               reads=[k_cst, k_sm], writes=[k_R])
                            ps4, kps4 = bankF()
                            op("pe", MM(ps4, SU, R[:].rearrange("p h l -> p (h l)")), reads=[k_cst, k_R], writes=[kps4])
                            Ex, k_Ex = tmp("Ex", [128, 4, 128], F32, bufs=1)
                            op("act", A_(Ex[:].rearrange("p h l -> p (h l)"), ps4, AF.Exp), reads=[kps4], writes=[k_Ex])
                            op("dve", TT(MT[:, (q % 2) * 4:(q % 2 + 1) * 4, :], Ex[:],
                                         cbm[:, g_:g_ + 1, :].broadcast_to([128, 4, 128]), ALU.mult),
                               reads=[k_Ex, k_cbm], writes=[k_MT])
                          return MT, k_MT
                        yn, k_yn = tmp("yn", [128, 2048], BF16, bufs=1)
                        ss, k_ss = tmp("ss", [128, 8])
                        for g in range(4):
                            gs = slice(g * 512, (g + 1) * 512)
                            if g == 0:
                                nxtMT = make_MT(0)
                            MT, k_MT = nxtMT
                            if g < 3:
                                nxtMT = make_MT(g + 1)
                            psD, kpsD = bankF()
                            for r in range(8):
                                h = g * 8 + r
                                op("pe", MM(psD[:, r * 64:(r + 1) * 64], MT[:, r, :], xdt[:, h * 64:(h + 1) * 64]),
                                   reads=[k_MT, k_xdt], writes=[kpsD])
                            psO, kpsO = bankF()
                            op("pe", MM(psO, xbcT[:, 20 + g, cs], Sbf[:, gs]), reads=[k_xbcT, k_Sbf], writes=[kpsO])
                            psZ, kpsZ = bankF()
                            for kc in range(8):
                                op("pe", MM(psZ, XTb[:, kc, cs], Win[:, kc, gs], kc == 0, kc == 7),
                                   reads=[k_win, k_XTb], writes=[kpsZ])
                            y, k_y = tmp("y", [128, 512], F32, bufs=1)
                            op("dve", TT(y[:].rearrange("p (h q) -> p h q", q=64), psO.rearrange("p (h q) -> p h q", q=64),
                                         ea[:, g * 8:(g + 1) * 8].unsqueeze(2).broadcast_to([128, 8, 64]), ALU.mult),
                               reads=[kpsO, k_sm], writes=[k_y])
                            op("dve", TT(y[:], y[:], psD, ALU.add), reads=[k_y, kpsD], writes=[k_y])
                            y2, k_y2 = tmp("y2", [128, 512], F32, bufs=1)
                            op("dve", TT(y2[:].rearrange("p (h q) -> p h q", q=64), xdt[:, gs].rearrange("p (h q) -> p h q", q=64),
                                         dif[:, g * 8:(g + 1) * 8].unsqueeze(2).broadcast_to([128, 8, 64]), ALU.mult),
                               reads=[k_xdt, k_sm], writes=[k_y2])
                            op("dve", TT(y[:], y[:], y2[:], ALU.add), reads=[k_y, k_y2], writes=[k_y])
                            sz, k_sz = tmp("sz", [128, 512], F32, bufs=1)
                            op("act", A_(sz[:], psZ, AF.Tanh, scale=0.5), reads=[kpsZ], writes=[k_sz])
                            op("dve", STT(sz[:], sz[:], 1.0, psZ, ALU.add, ALU.mult), reads=[k_sz, kpsZ], writes=[k_sz])
                            op("dve", TT(y[:], y[:], sz[:], ALU.mult), reads=[k_y, k_sz], writes=[k_y])
                            op("act", A_(y2[:], y[:], AF.Square, accum_out=ss[:, g:g + 1]), reads=[k_y], writes=[k_y2, k_ss])
                            op("dve", TS(ss[:, 4 + g:5 + g], ss[:, g:g + 1], 1.0 / 512, 4.0 * EPS, ALU.mult, ALU.add), reads=[k_ss], writes=[k_ss])
                            op("act", A_(ss[:, 4 + g:5 + g], ss[:, 4 + g:5 + g], AF.Sqrt), reads=[k_ss], writes=[k_ss])
                            op("dve", (lambda o, i_: lambda e: e.reciprocal(out=o, in_=i_))(ss[:, 4 + g:5 + g], ss[:, 4 + g:5 + g]),
                               reads=[k_ss], writes=[k_ss])
                            op("dve", TS(yn[:, gs], y[:], ss[:, 4 + g:5 + g], None, ALU.mult), reads=[k_y, k_ss], writes=[k_yn])
                        SK(False)
                        for g in range(4):
                            gs = slice(g * 512, (g + 1) * 512)
                            psU, kpsU = bankF()
                            op("pe", MM(psU, btok[:, g * 128:(g + 1) * 128], xw[:, gs]), reads=[k_btok, k_xw], writes=[kpsU])
                            op("dve", TT(S[:, gs].rearrange("p (h q) -> p h q", q=64), S[:, gs].rearrange("p (h q) -> p h q", q=64),
                                         cd[:, g * 8:(g + 1) * 8].unsqueeze(2).broadcast_to([128, 8, 64]), ALU.mult),
                               reads=[k_S, k_sm], writes=[k_S])
                            op("dve", TT(S[:, gs], S[:, gs], psU, ALU.add), reads=[k_S, kpsU], writes=[k_S])
                        SK(True)
                        op("act", A_(Sbf[:], S[:], AF.Copy), reads=[k_S], writes=[k_Sbf])
                        yT, k_yT = tmp("yT", [128, 16, 128], BF16, bufs=1)
                        for hf in range(2):
                            pb, kpb = bankB()
                            for c in range(8):
                                op("pe", TR(pb[:, c * 128:(c + 1) * 128], yn[:, (hf * 8 + c) * 128:(hf * 8 + c + 1) * 128], identB[:]),
                                   reads=[k_yn, k_idb], writes=[kpb])
                            op("dve", TT(yT[:, hf * 8:(hf + 1) * 8, :], pb.rearrange("p (c t) -> p c t", c=8),
                                         nw[:, hf * 8:(hf + 1) * 8].unsqueeze(2).broadcast_to([128, 8, 128]), ALU.mult),
                               reads=[kpb, k_par], writes=[k_yT])
                        psm, kpsm = [], []
                        for nb in range(2):
                            pm, kpm = bankF()
                            for c in range(16):
                                op("pe", MM(pm, yT[:, c, :], Wout[:, c, nb * 512:(nb + 1) * 512], c == 0, c == 15),
                                   reads=[k_yT, k_wout], writes=[kpm])
                            psm.append(pm)
                            kpsm.append(kpm)
                        mixer_epilogue(l, t, psm, kpsm, src_d, k_src)
                        SK(False)

            if MULTI:
                run_phase(True)
                op("act", A_(dtot[:], totsum[:, 0:32], AF.Exp), reads=[k_tot], writes=[k_dtot])
                P.barrier()
                arena_release(mark)
                exchange("l%d" % l, S[:], k_S, 2048, dtot[:], k_dtot, 32,
                         lambda a: a.rearrange("p (h q) -> p h q", q=64), lambda d_: d_.unsqueeze(2).broadcast_to([128, 32, 64]))
                op("act", A_(Sbf[:], S[:], AF.Copy), reads=[k_S], writes=[k_Sbf])
                P.barrier()
                arena_release(mark)
            run_phase(False)

        def hgrn_layer(l, src_d, k_src):
            j = l // 2
            Win = regA[:, 0:8 * 4096].rearrange("p (k n) -> p k n", k=8)
            Wout = regB[:, 0:8 * 1024].rearrange("p (k n) -> p k n", k=8)
            k_win, k_wout = Tk("win"), Tk("wout")
            wv = hg_w_in[j].rearrange("(k p) n -> p k n", p=128)
            for kc in range(8):
                for (a, b) in ((0, 2048), (2048, 4096)):
                    dma("pool", d_w, Win[:, kc, a:b], wv[:, kc, a:b], writes=[k_win])
            wo = hg_w_out[j].rearrange("(k p) n -> p k n", p=128)
            for kc in range(8):
                dma("pool", d_w2, Wout[:, kc, :], wo[:, kc, :], writes=[k_wout])
            par, _ = tmp("hg_par", [128, 8 * 8 + 128], F32, bufs=1)
            k_par = Tk("hpar")
            lb0, lb1, lbv, oml, noml, hc0, hc1, hnc1 = [par[:, i * 8:(i + 1) * 8] for i in range(8)]
            nwb = par[:, 64:192]
            dma("sp", d_par, lb0, hg_lb[0], writes=[k_par])
            dma("sp", d_par, lb1, hg_lb[1], writes=[k_par])
            dma("sp", d_par, nwb, hg_nw[j:j + 1, :].broadcast_to([128, 128]), writes=[k_par])
            if j == 0:
                op("dve", lambda e: e.memset(lbv, 0.0), writes=[k_par])
            else:
                op("dve", TT(lbv, lb1, lb0, ALU.subtract), reads=[k_par], writes=[k_par])
                op("act", A_(lbv, lbv, AF.Sigmoid), reads=[k_par], writes=[k_par])
            op("dve", TS(oml, lbv, -1.0, 1.0, ALU.mult, ALU.add), reads=[k_par], writes=[k_par])
            op("dve", TS(noml, oml, -1.0, None, ALU.mult), reads=[k_par], writes=[k_par])
            op("dve", TS(hc1, oml, 0.5, None, ALU.mult), reads=[k_par], writes=[k_par])
            op("dve", TT(hc0, lbv, hc1, ALU.add), reads=[k_par], writes=[k_par])
            op("dve", TS(hnc1, hc1, -1.0, None, ALU.mult), reads=[k_par], writes=[k_par])
            op("dve", TS(nwb, nwb, 0.5, None, ALU.mult), reads=[k_par], writes=[k_par])
            load_ln(l, 0)
            S, _ = tmp("hg_S", [128, 8, 128], F32, bufs=1)
            Sbf = [tmp("hg_Sbf", [128, 8, 128], BF16, bufs=2)[0] for i in range(2)]
            k_S, k_Sbf = Tk("hS"), [Tk("hSbf0"), Tk("hSbf1")]
            op("dve", lambda e: e.memset(S[:], 0.0), writes=[k_S])
            op("dve", lambda e: e.memset(Sbf[0][:], 0.0), writes=[k_Sbf[0]])

            dtot, _ = tmp("hg_dtot", [128, 8], F32, bufs=1)
            k_dtot = Tk("hdtot")
            mark = arena_mark()

            def run_phase(phaseA):
                cur["phaseA"] = phaseA
                if phaseA:
                    op("dve", lambda e: e.memset(S[:], 0.0), writes=[k_S])
                    op("dve", lambda e: e.memset(dtot[:], 1.0), writes=[k_dtot])
                for blk in range(NB):
                    XTb, k_XTb = load_xblock(src_d, k_src, blk)
                    qt, k_qt = tmp("qt", [128, 8, BLK], BF16, bufs=1)
                    kt, k_kt = tmp("kt", [128, 8, BLK], BF16, bufs=1)
                    qc, k_qc = tmp("qc", [128, 8, BLK], BF16, bufs=1)
                    ketok, k_ketok = tmp("ketok", [128, TPB, 8, 128], BF16, bufs=1)
                    ebl, k_ebl = tmp("ebl", [128, 8, BLK // 64], F32, bufs=1)
                    ffa, _ = tmp("ffa", [128, 8, BLK], F32, bufs=1)
                    kka, _ = tmp("kka", [128, 8, BLK], BF16, bufs=1)
                    k_ffh = [Tk("ffh%d" % h) for h in range(8)]
                    k_kkh = [Tk("kkh%d" % h) for h in range(8)]
                    for h in range(8):
                        psf, kpsf = bankF()
                        for kc in range(8):
                            op("pe", MM(psf[:, 0:BLK], Win[:, kc, 1024 + h * 128:1024 + (h + 1) * 128], XTb[:, kc, :], kc == 0, kc == 7),
                               reads=[k_win, k_XTb], writes=[kpsf])
                        sg, k_sg = tmp("sg", [128, BLK], F32, bufs=2)
                        op("act", A_(sg[:], psf[:, 0:BLK], AF.Tanh, scale=0.5), reads=[kpsf], writes=[k_sg])
                        op("dve", TS(ffa[:, h, :], sg[:], hc1[:, h:h + 1], hc0[:, h:h + 1], ALU.mult, ALU.add), reads=[k_sg, k_par], writes=[k_ffh[h]])
                        op("dve", TS(kka[:, h, :], sg[:], hnc1[:, h:h + 1], hc1[:, h:h + 1], ALU.mult, ALU.add), reads=[k_sg, k_par], writes=[k_kkh[h]])
                    k_lnf = Tk("lnf")
                    op("act", A_(ffa[:].rearrange("p h t -> p (h t)"), ffa[:].rearrange("p h t -> p (h t)"), AF.Ln), reads=k_ffh, writes=[k_lnf])
                    for h in range(8):
                        SK(True)
                        psq, kpsq = bankF()
                        for kc in range(8):
                            op("pe", MM(psq[:, 0:BLK], Win[:, kc, h * 128:(h + 1) * 128], XTb[:, kc, :], kc == 0, kc == 7),
                               reads=[k_win, k_XTb], writes=[kpsq])
                        SK(False)
                        k_kk = k_kkh[h]
                        bb, k_bb = tmp("bb", [128, BLK], F32, bufs=2)
                        op("dve", (lambda o, d0, d1: lambda e: e.tensor_tensor_scan(out=o, data0=d0, data1=d1, initial=0.0,
                                                                                     op0=ALU.mult, op1=ALU.add))(bb[:], rst[:, 0:BLK], ffa[:, h, :]),
                           reads=[k_lnf, k_cst], writes=[k_bb])
                        eb, k_eb = tmp("eb", [128, BLK], F32, bufs=1)
                        op("act", A_(eb[:], bb[:], AF.Exp), reads=[k_bb], writes=[k_eb])
                        SK(True)
                        op("dve", TT(qt[:, h, :], psq[:, 0:BLK], eb[:], ALU.mult), reads=[kpsq, k_eb], writes=[k_qt])
                        SK(False)
                        op("act", A_(ebl[:, h, :], eb[:, 63:BLK:64], AF.Copy), reads=[k_eb], writes=[k_ebl])
                        SK(True)
                        bcn, k_bcn = tmp("bcn", [128, BLK], F32, bufs=1)
                        op("dve", TT(bcn[:].rearrange("p (c t) -> p c t", t=64), bb[:].rearrange("p (c t) -> p c t", t=64),
                                     bb[:, 31:BLK:64].unsqueeze(2).broadcast_to([128, BLK // 64, 64]), ALU.subtract),
                           reads=[k_bb], writes=[k_bcn])
                        ebc, k_ebc = tmp("ebc", [128, BLK], F32, bufs=1)
                        enb, k_enb = tmp("enb", [128, BLK], F32, bufs=1)
                        op("act", A_(ebc[:], bcn[:], AF.Exp), reads=[k_bcn], writes=[k_ebc])
                        op("act", A_(enb[:], bcn[:], AF.Exp, scale=-1.0), reads=[k_bcn], writes=[k_enb])
                        op("dve", TT(qc[:, h, :], psq[:, 0:BLK], ebc[:], ALU.mult), reads=[kpsq, k_ebc], writes=[k_qc])
                        op("dve", TT(kt[:, h, :], kka[:, h, :], enb[:], ALU.mult), reads=[k_kk, k_enb], writes=[k_kt])
                        SK(False)
                        ee, k_ee = tmp("ee", [128, BLK], F32, bufs=1)
                        op("dve", TT(ee[:].rearrange("p (c t) -> p c t", t=64), bb[:].rearrange("p (c t) -> p c t", t=64),
                                     bb[:, 63:BLK:64].unsqueeze(2).broadcast_to([128, BLK // 64, 64]), ALU.subtract),
                           reads=[k_bb], writes=[k_ee])
                        op("act", A_(ee[:], ee[:], AF.Exp, scale=-1.0), reads=[k_ee], writes=[k_ee])
                        ke, k_ke = tmp("ke", [128, BLK], BF16)
                        op("dve", TT(ke[:], kka[:, h, :], ee[:], ALU.mult), reads=[k_kk, k_ee], writes=[k_ke])
                        pb, kpb = bankB()
                        for ti in range(TPB):
                            op("pe", TR(pb[:, ti * 128:(ti + 1) * 128], ke[:, ti * 128:(ti + 1) * 128], identB[:]),
                               reads=[k_ke, k_idb], writes=[kpb])
                        op("act", A_(ketok[:, :, h, :], pb[:, 0:TPB * 128].rearrange("p (a k) -> p a k", a=TPB), AF.Copy),
                           reads=[kpb], writes=[k_ketok])
                    for ti in range(TPB):
                        t = blk * TPB + ti
                        cs = slice(ti * 128, (ti + 1) * 128)
                        vb, k_vb = tmp("vb", [128, 1024], BF16, bufs=1)
                        sgt, k_sgt = tmp("sgt", [128, 1024], F32, bufs=1)
                        for nb in range(2):
                            ps, kps = bankF()
                            for kc in range(8):
                                op("pe", MM(ps, XTb[:, kc, cs], Win[:, kc, 2048 + nb * 512:2048 + (nb + 1) * 512], kc == 0, kc == 7),
                                   reads=[k_win, k_XTb], writes=[kps])
                            op("act", A_(vb[:, nb * 512:(nb + 1) * 512], ps, AF.Copy), reads=[kps], writes=[k_vb])
                            SK(True)
                            ps, kps = bankF()
                            for kc in range(8):
                                op("pe", MM(ps, XTb[:, kc, cs], Win[:, kc, 3072 + nb * 512:3072 + (nb + 1) * 512], kc == 0, kc == 7),
                                   reads=[k_win, k_XTb], writes=[kps])
                            op("act", A_(sgt[:, nb * 512:(nb + 1) * 512], ps, AF.Tanh, scale=0.5), reads=[kps], writes=[k_sgt])
                            op("dve", STT(sgt[:, nb * 512:(nb + 1) * 512], sgt[:, nb * 512:(nb + 1) * 512], 1.0, ps, ALU.add, ALU.mult),
                               reads=[k_sgt, kps], writes=[k_sgt])
                            SK(False)
                        SK(True)
                        scm, k_scm = tmp("scm", [128, 8, 128], BF16, bufs=1)
                        for hf in range(2):
                            ps, kps = bankF()
                            for hh in range(4):
                                h = hf * 4 + hh
                                op("pe", MM(ps[:, hh * 128:(hh + 1) * 128], kt[:, h, cs], qc[:, h, cs]), reads=[k_kt, k_qc], writes=[kps])
                            scl, k_scl = tmp("scl", [128, 512], F32, bufs=1)
                            op("dve", TS(scl[:], ps, -1e30, 1e30, ALU.max, ALU.min), reads=[kps], writes=[k_scl])
                            op("dve", TT(scm[:, hf * 4:(hf + 1) * 4, :], scl[:].rearrange("p (h l) -> p h l", h=4),
                                         maskBD.unsqueeze(1).broadcast_to([128, 4, 128]), ALU.mult),
                               reads=[k_scl, k_cst], writes=[k_scm])

                        SK(False)

                        def state_update(half, dst):
                            rs = slice(half * 64, (half + 1) * 64)
                            c = ti * 2 + half
                            if phaseA:
                                op("dve", TT(dtot[:], dtot[:], ebl[:, :, c], ALU.mult), reads=[k_dtot, k_ebl], writes=[k_dtot])
                            for hf in range(2):
                                ps, kps = bankF()
                                for hh in range(4):
                                    h = hf * 4 + hh
                                    op("pe", MM(ps[:, hh * 128:(hh + 1) * 128], ketok[rs, ti, h, :], vb[rs, h * 128:(h + 1) * 128]),
                                       reads=[k_ketok, k_vb], writes=[kps])
                                hs_ = slice(hf * 4, (hf + 1) * 4)
                                op("dve", TT(S[:, hs_, :], S[:, hs_, :], ebl[:, hs_, c:c + 1].broadcast_to([128, 4, 128]), ALU.mult),
                                   reads=[k_S, k_ebl], writes=[k_S])
                                op("dve", TT(S[:, hs_, :], S[:, hs_, :], ps.rearrange("p (h v) -> p h v", h=4), ALU.add),
                                   reads=[k_S, kps], writes=[k_S])
                            op("act", A_(Sbf[dst][:], S[:], AF.Copy), reads=[k_S], writes=[k_Sbf[dst]])

                        state_update(0, 1)
                        SK(True)
                        pso = []
                        for hf in range(2):
                            ps, kps = bankF()
                            for hh in range(4):
                                h = hf * 4 + hh
                                o_ = ps[:, hh * 128:(hh + 1) * 128]
                                op("pe", MM(o_, scm[:, h, :], vb[:, h * 128:(h + 1) * 128], True, False), reads=[k_scm, k_vb], writes=[kps])
                                op("pe", MM(o_[0:64, :], qt[:, h, ti * 128:ti * 128 + 64], Sbf[0][:, h, :], False, False),
                                   reads=[k_qt, k_Sbf[0]], writes=[kps])
                                op("pe", MM(o_[64:128, :], qt[:, h, ti * 128 + 64:ti * 128 + 128], Sbf[1][:, h, :], False, True),
                                   reads=[k_qt, k_Sbf[1]], writes=[kps])
                            pso.append((ps, kps))
                        SK(False)
                        state_update(1, 0)
                        SK(True)
                        on, k_on = tmp("on", [128, 8, 128], F32, bufs=1)
                        ssq, k_ssq = tmp("ssq", [128, 16])
                        junk, k_junk = tmp("junk", [128, 128])
                        for hf in range(2):
                            ps, kps = pso[hf]
                            for hh in range(4):
                                h = hf * 4 + hh
                                op("act", A_(junk[:], ps[:, hh * 128:(hh + 1) * 128], AF.Square, accum_out=ssq[:, h:h + 1]),
                                   reads=[kps], writes=[k_junk, k_ssq])
                        op("dve", TS(ssq[:, 8:16], ssq[:, 0:8], 1.0 / 128, EPS, ALU.mult, ALU.add), reads=[k_ssq], writes=[k_ssq])
                        op("act", A_(ssq[:, 8:16], ssq[:, 8:16], AF.Sqrt), reads=[k_ssq], writes=[k_ssq])
                        op("dve", (lambda o, i_: lambda e: e.reciprocal(out=o, in_=i_))(ssq[:, 8:16], ssq[:, 8:16]), reads=[k_ssq], writes=[k_ssq])
                        for hf in range(2):
                            ps, kps = pso[hf]
                            hs_ = slice(hf * 4, (hf + 1) * 4)
                            op("dve", TT(on[:, hs_, :], ps.rearrange("p (h v) -> p h v", h=4),
                                         ssq[:, 8 + hf * 4:12 + hf * 4].unsqueeze(2).broadcast_to([128, 4, 128]), ALU.mult),
                               reads=[kps, k_ssq], writes=[k_on])
                        op("dve", TT(on[:], on[:], nwb.unsqueeze(1).broadcast_to([128, 8, 128]), ALU.mult), reads=[k_on, k_par], writes=[k_on])
                        onb, k_onb = tmp("onb", [128, 1024], BF16, bufs=1)
                        op("dve", TT(onb[:], on[:].rearrange("p h v -> p (h v)"), sgt[:], ALU.mult), reads=[k_on, k_sgt], writes=[k_onb])
                        onT, k_onT = tmp("onT", [128, 8, 128], BF16, bufs=1)
                        pb, kpb = bankB()
                        for c in range(8):
                            op("pe", TR(pb[:, c * 128:(c + 1) * 128], onb[:, c * 128:(c + 1) * 128], identB[:]), reads=[k_onb, k_idb], writes=[kpb])
                        op("act", A_(onT[:], pb.rearrange("p (c t) -> p c t", c=8), AF.Copy), reads=[kpb], writes=[k_onT])
                        psm, kpsm = [], []
                        for nb in range(2):
                            pm, kpm = bankF()
                            for c in range(8):
                                op("pe", MM(pm, onT[:, c, :], Wout[:, c, nb * 512:(nb + 1) * 512], c == 0, c == 7),
                                   reads=[k_onT, k_wout], writes=[kpm])
                            psm.append(pm)
                            kpsm.append(kpm)
                        mixer_epilogue(l, t, psm, kpsm, src_d, k_src)
                        SK(False)

            if MULTI:
                run_phase(True)
                P.barrier()
                arena_release(mark)
                exchange("l%d" % l, S[:].rearrange("p h v -> p (h v)"), k_S, 1024, dtot[:], k_dtot, 8,
                         lambda a: a.rearrange("p (h v) -> p h v", v=128), lambda d_: d_.unsqueeze(2).broadcast_to([128, 8, 128]))
                op("act", A_(Sbf[0][:], S[:], AF.Copy), reads=[k_S], writes=[k_Sbf[0]])
                P.barrier()
                arena_release(mark)
            run_phase(False)

        def mlp_ple_segment(l, seg, dst_d, k_dst, last):
            Xacc = regA[:, 0:16 * 2048].bitcast(F32).rearrange("p (t d) -> p t d", t=16)
            kX = [Tk("Xacc%d" % i) for i in range(16)]
            XTh["XT"] = tmp("XT", [128, 8, 2048], BF16, bufs=1)[0]
            XT = XTh["XT"]
            for tl in range(16):
                t = seg * 16 + tl
                dma("sp", d_h[tl % 2], Xacc[:, tl, :], hs_d[t * 128:(t + 1) * 128, :], reads=[k_hs[t]], writes=[kX[tl]])
                hb, k_hbb = tmp("ln_hb", [128, D], BF16)
                op("act", A_(hb[:], Xacc[:, tl, :], AF.Copy, scale=1.0 / ALPHA), reads=[kX[tl]], writes=[k_hbb])
                to_XT(hb, k_hbb, tl)
            load_ln(l, 1)
            wbuf = [regB[:, i * 8192:(i + 1) * 8192] for i in range(2)]
            k_wb = [Tk("mwb0"), Tk("mwb1")]
            w1v = w1_d[l].rearrange("(k p) n -> p k n", p=128)
            w2v = w2_d[l].rearrange("(k p) n -> p k n", p=128)
            for e8 in range(8):
                wb = wbuf[e8 % 2]
                kwb = k_wb[e8 % 2]
                w1e = wb[:, 0:4096].rearrange("p (k n) -> p k n", k=8)
                w2e = wb[:, 4096:8192].rearrange("p (k n) -> p k n", k=4)
                for kc in range(8):
                    dma("pool", d_mw[e8 % 2], w1e[:, kc, :], w1v[:, kc, e8 * 512:(e8 + 1) * 512], writes=[kwb])
                for fc in range(4):
                    dma("pool", d_mw[e8 % 2], w2e[:, fc, :], w2v[:, e8 * 4 + fc, :], writes=[kwb])
                for tb in range(4):
                    hid, k_hid = tmp("hid", [128, 4, 512], BF16, bufs=2)
                    for fc in range(4):
                        ps, kps = bankF()
                        for kc in range(8):
                            op("pe", MM(ps, w1e[:, kc, fc * 128:(fc + 1) * 128], XT[:, kc, tb * 512:(tb + 1) * 512], kc == 0, kc == 7),
                               reads=[kwb] + kXT[tb * 4:tb * 4 + 4], writes=[kps])
                        rl, k_rl = tmp("rl", [128, 512])
                        op("act", A_(rl[:], ps, AF.Relu), reads=[kps], writes=[k_rl])
                        op("dve", TT(hid[:, fc, :], rl[:], rl[:], ALU.mult), reads=[k_rl], writes=[k_hid])
                    for ti in range(4):
                        tl = tb * 4 + ti
                        for nb in range(2):
                            ps, kps = bankF()
                            for fc in range(4):
                                op("pe", MM(ps, hid[:, fc, ti * 128:(ti + 1) * 128], w2e[:, fc, nb * 512:(nb + 1) * 512], fc == 0, fc == 3),
                                   reads=[k_hid, kwb], writes=[kps])
                            xa = Xacc[:, tl, nb * 512:(nb + 1) * 512]
                            op("dve", TT(xa, xa, ps, ALU.add), reads=[kX[tl], kps], writes=[kX[tl]])
            Wg = regB[:, 0:8192].rearrange("p (k n) -> p k n", k=8)
            Wp = regB[:, 8192:8192 + 2048].rearrange("p (k n) -> p k n", k=2)
            wgv = wg_d[l].rearrange("(k p) n -> p k n", p=128)
            wpv = wp_d[l].rearrange("(k p) n -> p k n", p=128)
            for kc in range(8):
                dma("pool", d_pl, Wg[:, kc, :], wgv[:, kc, :], writes=[k_wb[0]])
            for kc in range(2):
                dma("pool", d_pl, Wp[:, kc, :], wpv[:, kc, :], writes=[k_wb[1]])
            for tl in range(16):
                t = seg * 16 + tl
                hb, k_hbb = layer_norm(Xacc[:, tl, :], kX[tl], Xacc[:, tl, :], kX[tl])
                to_XT(hb, k_hbb, tl)
                pt, k_pt = tmp("pt", [128, 256])
                dma("sp", d_p[tl % 2], pt[:], p_d[l, t * 128:(t + 1) * 128, :], writes=[k_pt])
                ptb, k_ptb = tmp("ptb", [128, 256], BF16)
                op("dve", CP(ptb[:], pt[:]), reads=[k_pt], writes=[k_ptb])
                pb, kpb = bankB()
                for c in range(2):
                    op("pe", TR(pb[:, c * 128:(c + 1) * 128], ptb[:, c * 128:(c + 1) * 128], identB[:]), reads=[k_ptb, k_idb], writes=[kpb])
                pT, k_pT = tmp("pT", [128, 2, 128], BF16)
                op("act", A_(pT[:], pb[:, 0:256].rearrange("p (c t) -> p c t", c=2), AF.Copy), reads=[kpb], writes=[k_pT])
                xn, k_xn = tmp("xn", [128, D])
                for nb in range(2):
                    psg, kpsg = bankF()
                    for kc in range(8):
                        op("pe", MM(psg, XT[:, kc, tl * 128:(tl + 1) * 128], Wg[:, kc, nb * 512:(nb + 1) * 512], kc == 0, kc == 7),
                           reads=[kXT[tl], k_wb[0]], writes=[kpsg])
                    psp, kpsp = bankF()
                    for kc in range(2):
                        op("pe", MM(psp, pT[:, kc, :], Wp[:, kc, nb * 512:(nb + 1) * 512], kc == 0, kc == 1),
                           reads=[k_pT, k_wb[1]], writes=[kpsp])
                    sgm, k_sgm = tmp("sgm", [128, 512])
                    op("act", A_(sgm[:], psg, AF.Tanh, scale=0.5), reads=[kpsg], writes=[k_sgm])
                    op("dve", STT(sgm[:], sgm[:], 1.0, psp, ALU.add, ALU.mult), reads=[k_sgm, kpsp], writes=[k_sgm])
                    op("dve", STT(xn[:, nb * 512:(nb + 1) * 512], sgm[:], 0.5, Xacc[:, tl, nb * 512:(nb + 1) * 512], ALU.mult, ALU.add),
                       reads=[k_sgm, kX[tl]], writes=[k_xn])
                dma("sp", d_st, dst_d[t * 128:(t + 1) * 128, :], xn[:], reads=[k_xn], writes=[k_dst[t]])
                if MULTI and tl == 15 and (l + 1) in layers and (l + 1) % 2 == 0:
                    xnb, k_xnb = tmp("ln_hb", [128, D], BF16)
                    op("act", A_(xnb[:], xn[:], AF.Copy), reads=[k_xn], writes=[k_xnb])
                    pb, kpb = bankB()
                    for c in range(8):
                        op("pe", TR(pb[:, c * 128:(c + 1) * 128], xnb[:, c * 128:(c + 1) * 128], identB[:]), reads=[k_xnb, k_idb], writes=[kpb])
                    xl, k_xl = tmp("xl", [128, 24], F32, bufs=1)
                    op("act", A_(xl[:].rearrange("p (c t) -> p c t", c=8), pb.rearrange("p (c t) -> p c t", c=8)[:, :, 125:128], AF.Copy),
                       reads=[kpb], writes=[k_xl])
                    halo_exchange("l%d" % l, xl[:], k_xl)

        src_d, k_src = x_d, None
        k_out = [Tk("out%d" % i) for i in range(NT)]
        for li, l in enumerate(layers):
            last = li == len(layers) - 1
            if l % 2 == 0:
                ssd_layer(l, src_d, k_src)
            else:
                hgrn_layer(l, src_d, k_src)
            P.barrier()
            arena_reset()
            dst_d, k_dst = (out_d, k_out) if last else (xs_d, k_xs)
            for seg in range(NSEG):
                mlp_ple_segment(l, seg, dst_d, k_dst, last)
                P.barrier()
                arena_reset()
            src_d, k_src = xs_d, k_xs
        P.barrier()
        block = es.enter_context(nc.Block())
        P.replay(block)
    return nc


def make_cst():
    c = np.zeros((128, 5 * 128 + 512), np.float32)
    i = np.arange(128)
    c[:, 0:128] = np.eye(128)
    c[:, 128:256] = (i[:, None] <= i[None, :])
    c[:, 256:384] = (i[:, None] > i[None, :])
    c[:, 384:512] = 1.0
    c[:, 512:640] = (i[:, None] <= i[None, :]) & ((i[:, None] // 64) == (i[None, :] // 64))
    r = np.ones(512, np.float32)
    r[0::64] = 0.0
    c[:, 640:1152] = r[None, :]
    return c


def prep_weights(inp):
    f = lambda a: np.ascontiguousarray(np.asarray(a, dtype=np.float32))
    w = {}
    for k in ["ssd_w_in", "ssd_dt_bias", "ssd_a_log", "ssd_d", "ssd_w_out", "hgrn_w_in", "hgrn_norm_w", "hgrn_w_out",
              "ln_g", "ln_b", "mlp_w1", "mlp_w2", "ple_w_proj", "ple_w_gate"]:
        w[k] = f(inp[k])
    cw = f(inp["ssd_conv_w"])
    w["ssd_cw"] = f(cw.reshape(2, 4, 24, 128).transpose(0, 3, 2, 1).reshape(2, 128, 96))
    w["ssd_cb"] = f(f(inp["ssd_conv_b"]).reshape(2, 24, 128).transpose(0, 2, 1))
    w["ssd_nw"] = f(f(inp["ssd_norm_w"]).reshape(2, 16, 128).transpose(0, 2, 1))
    w["hgrn_lb"] = f(f(inp["hgrn_lower_bounds"]).reshape(2, 8, 128).transpose(0, 2, 1))
    w["cst"] = make_cst()
    return w


def run(inp, T, layers, ncores=8):
    w = prep_weights(inp)
    x = np.asarray(inp["x"], np.float32)
    p = np.asarray(inp["p"], np.float32)
    nc = build(T, layers)
    in_maps = []
    for c in range(ncores):
        b = c if c < 2 else 0
        m = dict(w)
        m["x"] = np.ascontiguousarray(x[b, :T])
        m["p"] = np.ascontiguousarray(p[:, b, :T])
        m["msk"] = np.zeros((128, 18), np.float32)
        m["xhT"] = np.zeros((128, 24), np.float32)
        in_maps.append(m)
    res = run_bass_kernel_spmd(nc, in_maps, core_ids=list(range(ncores)))
    return np.stack([res.results[0]["out"], res.results[1]["out"]], axis=0)


def run_multi(inp, layers, nseq=4, trace=False):
    w = prep_weights(inp)
    x = np.asarray(inp["x"], np.float32)
    p = np.asarray(inp["p"], np.float32)
    TS_ = 2048
    nc = build(TS_, layers, MULTI=True)
    in_maps = []
    for c in range(8):
        b, r = c // 4, c % 4
        m = dict(w)
        m["x"] = np.ascontiguousarray(x[b, r * TS_:(r + 1) * TS_])
        m["p"] = np.ascontiguousarray(p[:, b, r * TS_:(r + 1) * TS_])
        msk = np.zeros((128, 18), np.float32)
        slot = lambda bb, rr: bb * 3 + rr
        if r < 3:
            msk[:, slot(b, r)] = 1.0
        for rr in range(r):
            msk[:, 6 + slot(b, rr)] = 1.0
        if r > 0:
            msk[:, 12 + slot(b, r - 1)] = 1.0
        m["msk"] = msk
        xh = np.zeros((128, 24), np.float32)
        if r > 0:
            prev = x[b, r * TS_ - 3:r * TS_, :]
            xh = np.ascontiguousarray(prev.reshape(3, 8, 128).transpose(2, 1, 0).reshape(128, 24))
        m["xhT"] = xh
        in_maps.append(m)
    res = run_bass_kernel_spmd(nc, in_maps, core_ids=list(range(8)))
    out = np.zeros((2, 4 * TS_, D), np.float32)
    for c in range(8):
        out[c // 4, (c % 4) * TS_:(c % 4 + 1) * TS_] = res.results[c]["out"]
    return out


def kernel(**inputs):
    return run_multi(inputs, [0, 1, 2, 3]).astype(np.float32)
```

```python
import numpy as np
from contextlib import ExitStack
import concourse.bass as bass
import concourse.mybir as mybir
from concourse.bass_utils import run_bass_kernel_spmd

F32, BF16 = mybir.dt.float32, mybir.dt.bfloat16
AF = mybir.ActivationFunctionType
ALU = mybir.AluOpType

D = 1024
DEPTH = 4
ALPHA = (2.0 * DEPTH) ** 0.25
EPS = 1e-5
SAME_ENGINE_SYNC = True
USE_POW = True


class Tk:
    __slots__ = ("name", "w", "r")

    def __init__(self, name=""):
        self.name = name
        self.w = {}
        self.r = {}


class Eng:
    def __init__(self, name, sem):
        self.name = name
        self.sem = sem
        self.n = 0
        self.waited = {}
        self.ops = []


class Prog:
    def __init__(self, nc, es):
        self.nc, self.es = nc, es
        self.E = {}
        for name in ["pe", "act", "dve", "pool", "sp"]:
            self.E[name] = Eng(name, es.enter_context(nc.semaphore("s_" + name)))
        self.dsems = []
        self.skip = False
        self.rec = None
        self.dset = set()
        self.KD = 12
        self.dpool = {q: [self.dsem("%s%d" % (q, i)) for i in range(self.KD)] for q in ("sp", "pool")}
        self.dpi = {"sp": 0, "pool": 0}
        self.dset = set()

    def dsem(self, name):
        d = Eng("d_" + name, self.es.enter_context(self.nc.semaphore("d_" + name)))
        self.dsems.append(d)
        self.dset.add(d)
        return d

    def _waits(self, e, reads, writes, skip_same):
        need = {}
        for t in reads:
            for k, v in t.w.items():
                if need.get(k, 0) < v:
                    need[k] = v
        for t in writes:
            for k, v in t.w.items():
                if need.get(k, 0) < v:
                    need[k] = v
            for k, v in t.r.items():
                if need.get(k, 0) < v:
                    need[k] = v
        for k, v in need.items():
            if k in self.dset:
                v = k.n
            if k is e and skip_same:
                continue
            if e.waited.get(k, 0) >= v:
                continue
            e.waited[k] = v
            e.ops.append(("w", k.sem, v))

    def op(self, en, fn, reads=(), writes=()):
        if self.skip:
            return
        if self.rec is not None:
            self.rec.append(("op", (en, fn, reads, writes)))
            return
        e = self.E[en]
        self._waits(e, reads, writes, en == "pe" or not SAME_ENGINE_SYNC)
        e.n += 1
        e.ops.append(("i", fn, e.sem, 1))
        for t in reads:
            t.r[e] = e.n
        for t in writes:
            t.w = {e: e.n}
            t.r = {}

    def cc(self, d, fn, reads=(), writes=()):
        if self.rec is not None:
            self.rec.append(("cc", (d, fn, reads, writes)))
            return
        q = self.E["pool"]
        self._waits(q, reads, writes, False)
        d.n += 1
        q.ops.append(("i", fn, d.sem, 1))
        for t in reads:
            t.r[d] = d.n
        for t in writes:
            t.w = {d: d.n}
            t.r = {}

    def dma(self, qn, d, out, in_, reads=(), writes=()):
        if self.skip:
            return
        if self.rec is not None:
            self.rec.append(("dma", (qn, d, out, in_, reads, writes)))
            return
        q = self.E[qn]
        self._waits(q, reads, writes, False)
        d = self.dpool[qn][self.dpi[qn] % self.KD]
        self.dpi[qn] += 1
        if d.n > 0 and q.waited.get(d, 0) < d.n:
            q.waited[d] = d.n
            q.ops.append(("w", d.sem, d.n))
        d.n += 16
        q.ops.append(("i", lambda eng: eng.dma_start(out=out, in_=in_), d.sem, 16))
        for t in reads:
            t.r[d] = d.n
        for t in writes:
            t.w[d] = d.n
            t.r = {}

    def interleave(self, cur, bodies):
        recs = []
        for sid, body in enumerate(bodies):
            cur["sid"] = sid
            self.rec = []
            body()
            recs.append(self.rec)
            self.rec = None
        cur["sid"] = None
        idx = [0] * len(recs)
        live = True
        while live:
            live = False
            for i, r in enumerate(recs):
                if idx[i] < len(r):
                    kind, args = r[idx[i]]
                    idx[i] += 1
                    live = True
                    getattr(self, kind)(*args)

    def barrier(self):
        allk = list(self.E.values()) + self.dsems
        for e in self.E.values():
            for k in allk:
                if k is e or k.n == 0:
                    continue
                if e.waited.get(k, 0) < k.n:
                    e.waited[k] = k.n
                    e.ops.append(("w", k.sem, k.n))

    def replay(self, block):
        def run(e):
            def f(eng):
                for o in e.ops:
                    if o[0] == "w":
                        eng.wait_ge(o[1], o[2])
                    else:
                        o[1](eng).then_inc(o[2], o[3])
            return f
        block.tensor(run(self.E["pe"]))
        block.scalar(run(self.E["act"]))
        block.vector(run(self.E["dve"]))
        block.gpsimd(run(self.E["pool"]))
        block.sync(run(self.E["sp"]))


def A_(out, in_, func, **kw):
    return lambda e: e.activation(out=out, in_=in_, func=func, **kw)


def TT(out, in0, in1, op):
    return lambda e: e.tensor_tensor(out=out, in0=in0, in1=in1, op=op)


def TS(out, in0, s1, s2, op0, op1=None):
    if op1 is None:
        return lambda e: e.tensor_scalar(out=out, in0=in0, scalar1=s1, scalar2=None, op0=op0)
    return lambda e: e.tensor_scalar(out=out, in0=in0, scalar1=s1, scalar2=s2, op0=op0, op1=op1)


def STT(out, in0, sc, in1, op0, op1):
    return lambda e: e.scalar_tensor_tensor(out=out, in0=in0, scalar=sc, in1=in1, op0=op0, op1=op1)


def CP(out, in_):
    return lambda e: e.tensor_copy(out=out, in_=in_)


def MM(out, lhsT, rhs, start=True, stop=True):
    return lambda e: e.matmul(out, lhsT=lhsT, rhs=rhs, start=start, stop=stop)


def TR(out, in_, ident):
    return lambda e: e.transpose(out, in_, ident)


def build(T, layers, MULTI=False):
    NSEG = T // 2048
    NT = T // 128
    BLK = 256
    TPB = BLK // 128
    NB = T // BLK
    nc = bass.Bass("TRN2", target_bir_lowering=False)

    def din(name, shape):
        return nc.dram_tensor(name, list(shape), F32, kind="ExternalInput").ap()

    x_d = din("x", [T, D])
    p_d = din("p", [DEPTH, T, 256])
    ssd_w_in = din("ssd_w_in", [2, D, 5152])
    ssd_cw = din("ssd_cw", [2, 128, 24 * 4])
    ssd_cb = din("ssd_cb", [2, 128, 24])
    ssd_dtb = din("ssd_dt_bias", [2, 32])
    ssd_alog = din("ssd_a_log", [2, 32])
    ssd_dd = din("ssd_d", [2, 32])
    ssd_nw = din("ssd_nw", [2, 128, 16])
    ssd_w_out = din("ssd_w_out", [2, 2048, D])
    hg_w_in = din("hgrn_w_in", [2, D, 4096])
    hg_lb = din("hgrn_lb", [2, 128, 8])
    hg_nw = din("hgrn_norm_w", [2, 128])
    hg_w_out = din("hgrn_w_out", [2, D, D])
    ln_g = din("ln_g", [DEPTH, 2, D])
    ln_b = din("ln_b", [DEPTH, 2, D])
    w1_d = din("mlp_w1", [DEPTH, D, 4096])
    w2_d = din("mlp_w2", [DEPTH, 4096, D])
    wp_d = din("ple_w_proj", [DEPTH, 256, D])
    wg_d = din("ple_w_gate", [DEPTH, D, D])
    cst_d = din("cst", [128, 5 * 128 + 512])
    msk_d = din("msk", [128, 18])
    xhT_d = din("xhT", [128, 24])
    out_d = nc.dram_tensor("out", [T, D], F32, kind="ExternalOutput").ap()
    xs_d = nc.dram_tensor("xscr", [T, D], F32).ap()
    hs_d = nc.dram_tensor("hscr", [T, D], F32).ap()

    es = ExitStack()
    with es:
        P = Prog(nc, es)
        op, dma = P.op, P.dma

        def sb(name, shape, dt=F32):
            return es.enter_context(nc.sbuf_tensor(name, list(shape), dt))

        cst = sb("cst_sb", [128, 5 * 128 + 512])
        k_cst = Tk("cst")
        d_cst = P.dsem("cst")
        dma("sp", d_cst, cst[:], cst_d, writes=[k_cst])
        identF = cst[:, 0:128]
        triU = cst[:, 128:256]
        SU = cst[:, 256:384]
        ones = cst[:, 384:512]
        maskBD = cst[:, 512:640]
        rst = cst[:, 640:1152]
        msk = sb("msk_sb", [128, 18])
        xh_sb = sb("xh_sb", [128, 24])
        k_msk, k_xh = Tk("msk"), Tk("xh")
        dma("sp", d_cst, msk[:], msk_d, writes=[k_msk])
        dma("sp", d_cst, xh_sb[:], xhT_d, writes=[k_xh])
        oh6, pr6, sel6 = msk[:, 0:6], msk[:, 6:12], msk[:, 12:18]
        identB = sb("identB", [128, 128], BF16)
        k_idb = Tk()
        op("dve", CP(identB[:], identF), reads=[k_cst], writes=[k_idb])
        SUb = sb("SUb", [128, 128], BF16)
        op("dve", CP(SUb[:], SU), reads=[k_cst], writes=[k_idb])

        negh = sb("negh", [128, 8])
        k_negh = Tk("negh")
        op("dve", lambda e: e.memset(negh[:], -0.5), writes=[k_negh])

        def rsqrt_(x, n, k_x):
            if USE_POW:
                op("pool", TT(x, x, negh[:, 0:n], ALU.pow), reads=[k_x, k_negh], writes=[k_x])
            else:
                op("act", A_(x, x, AF.Sqrt), reads=[k_x], writes=[k_x])
                op("dve", (lambda o, i_: lambda e: e.reciprocal(out=o, in_=i_))(x, x), reads=[k_x], writes=[k_x])

        psF = es.enter_context(nc.psum_tensor("psF", [128, 6, 512], F32))
        psB = es.enter_context(nc.psum_tensor("psB", [128, 2, 1024], BF16))
        kF = [Tk("psF%d" % i) for i in range(6)]
        kB = [Tk("psB%d" % i) for i in range(2)]
        rot = {"F": 0, "B": 0}

        def bankF():
            sid = cur["sid"]
            if sid is None:
                i = rot["F"] % 6
                rot["F"] += 1
            else:
                key = "F%d" % sid
                i = 3 * sid + rot.get(key, 0) % 3
                rot[key] = rot.get(key, 0) + 1
            return psF[:, i, :], kF[i]

        def bankB():
            sid = cur["sid"]
            if sid is None:
                i = rot["B"] % 2
                rot["B"] += 1
            else:
                i = sid
            return psB[:, i, :], kB[i]

        cur = {"phaseA": False, "sid": None}

        ARENA = 80 * 1024
        arena = sb("arena", [128, ARENA // 2], BF16)
        pools = {}
        ar = {"off": 0}

        def carve(shape, dt):
            n = 1
            for d_ in shape[1:]:
                n *= d_
            nb = n * (4 if dt == F32 else 2)
            nb = (nb + 31) // 32 * 32
            off = ar["off"]
            assert off + nb <= ARENA, ("arena overflow", off, nb)
            ar["off"] = off + nb
            ar["hw"] = max(ar.get("hw", 0), off + nb)
            a = arena[:, off // 2:(off + nb) // 2]
            if dt == F32:
                a = a.bitcast(F32)
            a = a[:, 0:n]
            if len(shape) == 3:
                a = a.rearrange("p (a b) -> p a b", a=shape[1])
            elif len(shape) == 4:
                a = a.rearrange("p (a b c) -> p a b c", a=shape[1], b=shape[2])
            return a

        def arena_reset():
            print("arena high-water", ar.get("hw", 0))
            ar["hw"] = 0
            pools.clear()
            ar["off"] = 0

        def arena_mark():
            return (ar["off"], set(pools.keys()))

        def arena_release(m):
            for k in list(pools.keys()):
                if k not in m[1]:
                    del pools[k]
            ar["off"] = m[0]

        def SK(b):
            P.skip = bool(b) and cur["phaseA"]

        class _T:
            def __init__(self, a):
                self.a = a

            def __getitem__(self, key):
                return self.a[key]

        def tmp(name, shape, dt=F32, bufs=2):
            if cur["sid"] is not None:
                name = "%s_s%d" % (name, cur["sid"])
                bufs = 1
            if name not in pools:
                pools[name] = [[(_T(carve(shape, dt)), Tk(name)) for i in range(bufs)], 0]
            pl = pools[name]
            t, k = pl[0][pl[1] % len(pl[0])]
            pl[1] += 1
            return t, k

        regA = sb("regA", [128, 8 * 5152], BF16)
        regB = sb("regB", [128, 16 * 1024], BF16)
        kXT = [Tk("XT%d" % i) for i in range(16)]
        XTh = {}
        d_w = P.dsem("w")
        d_w2 = P.dsem("w2")
        d_x = [P.dsem("x%d" % i) for i in range(2)]
        d_st = P.dsem("st")
        d_par = P.dsem("par")
        d_mw = [P.dsem("mw%d" % i) for i in range(2)]
        d_pl = P.dsem("pl")
        d_h = [P.dsem("h%d" % i) for i in range(2)]
        d_p = [P.dsem("p%d" % i) for i in range(2)]
        k_xs = [Tk("xs%d" % i) for i in range(NT)]
        k_hs = [Tk("hs%d" % i) for i in range(NT)]

        gb = sb("gb", [128, 2, D])
        k_gb = Tk("gb")

        def load_ln(l, j):
            dma("sp", d_par, gb[:, 0, :], ln_g[l, j:j + 1, :].broadcast_to([128, D]), writes=[k_gb])
            dma("sp", d_par, gb[:, 1, :], ln_b[l, j:j + 1, :].broadcast_to([128, D]), writes=[k_gb])

        def layer_norm(src, k_src, dst_f32, k_dst, want_bf=True):
            st, k_st = tmp("ln_st", [128, 2, 6])
            for i in range(2):
                op("dve", (lambda o, i_: lambda e: e.bn_stats(out=o, in_=i_))(st[:, i, :], src[:, i * 512:(i + 1) * 512]),
                   reads=[k_src], writes=[k_st])
            mv, k_mv = tmp("ln_mv", [128, 4])
            op("dve", (lambda o, i_: lambda e: e.bn_aggr(out=o, in_=i_))(mv[:, 0:2], st[:].rearrange("p a b -> p (a b)")),
               reads=[k_st], writes=[k_mv])
            op("dve", TS(mv[:, 2:3], mv[:, 1:2], EPS, None, ALU.add), reads=[k_mv], writes=[k_mv])
            rsqrt_(mv[:, 2:3], 1, k_mv)
            op("dve", TS(mv[:, 3:4], mv[:, 0:1], mv[:, 2:3], -1.0, ALU.mult, ALU.mult), reads=[k_mv], writes=[k_mv])
            op("act", A_(dst_f32, src, AF.Identity, scale=mv[:, 2:3], bias=mv[:, 3:4]), reads=[k_src, k_mv], writes=[k_dst])
            op("dve", TT(dst_f32, dst_f32, gb[:, 0, :], ALU.mult), reads=[k_dst, k_gb], writes=[k_dst])
            op("dve", TT(dst_f32, dst_f32, gb[:, 1, :], ALU.add), reads=[k_dst, k_gb], writes=[k_dst])
            if not want_bf:
                return None, None
            hb, k_hbb = tmp("ln_hb", [128, D], BF16)
            op("act", A_(hb[:], dst_f32, AF.Copy), reads=[k_dst], writes=[k_hbb])
            return hb, k_hbb

        def to_XT(hb, k_hbb, tl):
            pb, kpb = bankB()
            for c in range(8):
                op("pe", TR(pb[:, c * 128:(c + 1) * 128], hb[:, c * 128:(c + 1) * 128], identB[:]),
                   reads=[k_hbb, k_idb], writes=[kpb])
            op("act", A_(XTh["XT"][:, :, tl * 128:(tl + 1) * 128], pb.rearrange("p (c t) -> p c t", c=8), AF.Copy),
               reads=[kpb], writes=[kXT[tl]])

        def mixer_epilogue(l, t, psm, kpsm, xsrc, k_xsrc):
            xt, k_xt = tmp("ep_x", [128, D], F32, bufs=1)
            dma("sp", d_x[t % 2], xt[:], xsrc[t * 128:(t + 1) * 128, :], reads=[k_xsrc[t]] if k_xsrc else [], writes=[k_xt])
            for nb in range(2):
                sl = slice(nb * 512, (nb + 1) * 512)
                op("dve", STT(xt[:, sl], xt[:, sl], ALPHA, psm[nb], ALU.mult, ALU.add),
                   reads=[k_xt, kpsm[nb]], writes=[k_xt])
            layer_norm(xt[:], k_xt, xt[:], k_xt, want_bf=False)
            op("act", A_(xt[:], xt[:], AF.Copy, scale=ALPHA), reads=[k_xt], writes=[k_xt])
            dma("sp", d_st, hs_d[t * 128:(t + 1) * 128, :], xt[:], reads=[k_xt], writes=[k_hs[t]])

        d_xc = [P.dsem("xc0"), P.dsem("xc1")]
        d_cc = P.dsem("cc")

        def allreduce(bin_, bout, k_bin, k_bout):
            P.cc(d_cc, lambda g: g.collective_compute("AllReduce", ALU.add, replica_groups=[list(range(8))],
                                                      ins=[bin_.ap().opt()], outs=[bout.ap().opt()]),
                 reads=[k_bin], writes=[k_bout])

        def exchange(tag, S2d, k_S, W_S, Dt, k_D, W_D, viewS, bcD):
            W = W_S + W_D
            bin_ = nc.dram_tensor("xin_" + tag, [6 * 128, W], F32)
            bout = nc.dram_tensor("xout_" + tag, [6 * 128, W], F32)
            k_bin, k_bout = Tk("bin"), Tk("bout")
            for j in range(6):
                stg, k_stg = tmp("xstg", [128, W], F32, bufs=2)
                op("dve", TS(stg[:, 0:W_S], S2d, oh6[:, j:j + 1], None, ALU.mult), reads=[k_S, k_msk], writes=[k_stg])
                op("dve", TS(stg[:, W_S:W], Dt, oh6[:, j:j + 1], None, ALU.mult), reads=[k_D, k_msk], writes=[k_stg])
                dma("sp", d_xc[0], bin_.ap()[j * 128:(j + 1) * 128, :], stg[:], reads=[k_stg], writes=[k_bin])
            allreduce(bin_, bout, k_bin, k_bout)
            op("dve", lambda e: e.memset(S2d, 0.0), reads=[k_bin], writes=[k_S])
            for j in range(6):
                stg, k_stg = tmp("xstg", [128, W], F32, bufs=2)
                dma("pool", d_xc[1], stg[:], bout.ap()[j * 128:(j + 1) * 128, :], reads=[k_bout], writes=[k_stg])
                de, k_de = tmp("xde", [128, W_D], F32, bufs=2)
                op("dve", TS(de[:], stg[:, W_S:W], 1.0, pr6[:, j:j + 1], ALU.subtract, ALU.mult), reads=[k_stg, k_msk], writes=[k_de])
                op("dve", TS(de[:], de[:], 1.0, None, ALU.add), reads=[k_de], writes=[k_de])
                op("dve", TT(viewS(S2d), viewS(S2d), bcD(de[:]), ALU.mult), reads=[k_S, k_de], writes=[k_S])
                op("dve", STT(S2d, stg[:, 0:W_S], pr6[:, j:j + 1], S2d, ALU.mult, ALU.add), reads=[k_stg, k_S, k_msk], writes=[k_S])

        def halo_exchange(tag, xl, k_xl):
            bin_ = nc.dram_tensor("hin_" + tag, [6 * 128, 24], F32)
            bout = nc.dram_tensor("hout_" + tag, [6 * 128, 24], F32)
            k_bin, k_bout = Tk("hbin"), Tk("hbout")
            stg, k_stg = tmp("hstg", [128, 6, 24], F32, bufs=1)
            for j in range(6):
                op("dve", TS(stg[:, j, :], xl, oh6[:, j:j + 1], None, ALU.mult), reads=[k_xl, k_msk], writes=[k_stg])
            dma("sp", d_xc[0], bin_.ap().rearrange("(j p) w -> p j w", p=128), stg[:], reads=[k_stg], writes=[k_bin])
            allreduce(bin_, bout, k_bin, k_bout)
            stg2, k_stg2 = tmp("hstg2", [128, 6, 24], F32, bufs=1)
            dma("pool", d_xc[1], stg2[:], bout.ap().rearrange("(j p) w -> p j w", p=128), reads=[k_bout], writes=[k_stg2])
            op("dve", lambda e: e.memset(xh_sb[:], 0.0), writes=[k_xh])
            for j in range(6):
                op("dve", STT(xh_sb[:], stg2[:, j, :], sel6[:, j:j + 1], xh_sb[:], ALU.mult, ALU.add),
                   reads=[k_stg2, k_xh, k_msk], writes=[k_xh])

        def load_xblock(src_d, k_src_tiles, blk):
            XTb, k_XTb = tmp("XTb", [128, 8, BLK], BF16, bufs=1)
            for a in range(TPB):
                t = blk * TPB + a
                xt, k_xt = tmp("ep_x", [128, D], F32, bufs=1)
                dma("sp", d_x[t % 2], xt[:], src_d[t * 128:(t + 1) * 128, :],
                    reads=[k_src_tiles[t]] if k_src_tiles else [], writes=[k_xt])
                xb, k_xb = tmp("xb", [128, D], BF16, bufs=1)
                op("dve", CP(xb[:], xt[:]), reads=[k_xt], writes=[k_xb])
                pb, kpb = bankB()
                for c in range(8):
                    op("pe", TR(pb[:, c * 128:(c + 1) * 128], xb[:, c * 128:(c + 1) * 128], identB[:]),
                       reads=[k_xb, k_idb], writes=[kpb])
                op("act", A_(XTb[:, :, a * 128:(a + 1) * 128], pb.rearrange("p (c t) -> p c t", c=8), AF.Copy),
                   reads=[kpb], writes=[k_XTb])
            return XTb, k_XTb

        def ssd_layer(l, src_d, k_src):
            j = l // 2
            Win = regA[:].rearrange("p (k n) -> p k n", k=8)
            Wout = regB[:].rearrange("p (k n) -> p k n", k=16)
            k_win, k_wout = Tk("win"), Tk("wout")
            wv = ssd_w_in[j].rearrange("(k p) n -> p k n", p=128)
            for kc in range(8):
                for (a, b) in ((0, 2048), (2048, 4096), (4096, 5152)):
                    dma("pool", d_w, Win[:, kc, a:b], wv[:, kc, a:b], writes=[k_win])
            wo = ssd_w_out[j].rearrange("(k p) n -> p k n", p=128)
            for kc in range(16):
                dma("pool", d_w2, Wout[:, kc, :], wo[:, kc, :], writes=[k_wout])
            par, _ = tmp("ssd_par", [128, 24 * 4 + 24 + 16 + 32 * 3], F32, bufs=1)
            k_par = Tk("par")
            cw = par[:, 0:96].rearrange("p (c k) -> p c k", k=4)
            cb = par[:, 96:120]
            nw = par[:, 120:136]
            dtb = par[:, 136:168]
            an = par[:, 168:200]
            dd = par[:, 200:232]
            dma("sp", d_par, par[:, 0:96], ssd_cw[j], writes=[k_par])
            dma("sp", d_par, cb, ssd_cb[j], writes=[k_par])
            dma("sp", d_par, nw, ssd_nw[j], writes=[k_par])
            dma("sp", d_par, dtb, ssd_dtb[j:j + 1, :].broadcast_to([128, 32]), writes=[k_par])
            dma("sp", d_par, an, ssd_alog[j:j + 1, :].broadcast_to([128, 32]), writes=[k_par])
            dma("sp", d_par, dd, ssd_dd[j:j + 1, :].broadcast_to([128, 32]), writes=[k_par])
            op("act", A_(an, an, AF.Exp), reads=[k_par], writes=[k_par])
            op("dve", TS(an, an, -1.0, None, ALU.mult), reads=[k_par], writes=[k_par])
            load_ln(l, 0)
            S, _ = tmp("ssd_S", [128, 2048], F32, bufs=1)
            Sbf, _ = tmp("ssd_Sbf", [128, 2048], BF16, bufs=1)
            halo, _ = tmp("ssd_halo", [128, 24, 3], F32, bufs=1)
            k_S, k_Sbf, k_halo = Tk("S"), Tk("Sbf"), Tk("halo")
            op("dve", lambda e: e.memset(S[:], 0.0), writes=[k_S])
            op("dve", lambda e: e.memset(Sbf[:], 0.0), writes=[k_Sbf])

            halo0, _ = tmp("ssd_halo0", [128, 24, 3], F32, bufs=1)
            totsum, _ = tmp("ssd_totsum", [128, 64], F32, bufs=1)
            k_tot = Tk("tot")
            dtot, _ = tmp("ssd_dtot", [128, 32], F32, bufs=1)
            k_dtot = Tk("dtot")
            k_h0 = Tk("halo0")
            if MULTI:
                xhb, k_xhb = tmp("xhb", [128, 8, 3], BF16, bufs=1)
                op("dve", CP(xhb[:], xh_sb[:].rearrange("p (k t) -> p k t", k=8)), reads=[k_xh], writes=[k_xhb])
                psh, kpsh = bankF()
                for cc in range(24):
                    for kc in range(8):
                        op("pe", MM(psh[:, cc * 3:(cc + 1) * 3], Win[:, kc, 2048 + cc * 128:2048 + (cc + 1) * 128], xhb[:, kc, :], kc == 0, kc == 7),
                           reads=[k_win, k_xhb], writes=[kpsh])
                op("act", A_(halo0[:].rearrange("p c t -> p (c t)"), psh[:, 0:72], AF.Copy), reads=[kpsh], writes=[k_h0])
            else:
                op("dve", lambda e: e.memset(halo0[:], 0.0), writes=[k_h0])
            mark = arena_mark()

            def run_phase(phaseA):
                cur["phaseA"] = phaseA
                op("act", A_(halo[:], halo0[:], AF.Copy), reads=[k_h0], writes=[k_halo])
                if phaseA:
                    op("dve", lambda e: e.memset(S[:], 0.0), writes=[k_S])
                    op("dve", lambda e: e.memset(totsum[:], 0.0), writes=[k_tot])
                for blk in range(NB):
                    XTb, k_XTb = load_xblock(src_d, k_src, blk)
                    xbcT, k_xbcT = tmp("xbcT", [128, 24, BLK], BF16, bufs=1)
                    pend_silu = []
                    for cc in range(24):
                        SK(cc >= 20)
                        ps, kps = bankF()
                        for kc in range(8):
                            op("pe", MM(ps[:, 0:BLK], Win[:, kc, 2048 + cc * 128:2048 + (cc + 1) * 128], XTb[:, kc, :], kc == 0, kc == 7),
                               reads=[k_win, k_XTb], writes=[kps])
                        u, k_u = tmp("u", [128, BLK + 3])
                        op("act", A_(u[:, 0:3], halo[:, cc, :], AF.Copy), reads=[k_halo], writes=[k_u])
                        op("act", A_(u[:, 3:BLK + 3], ps[:, 0:BLK], AF.Copy), reads=[kps], writes=[k_u])
                        op("act", A_(halo[:, cc, :], u[:, BLK:BLK + 3], AF.Copy), reads=[k_u], writes=[k_halo])
                        acc, k_acc = tmp("acc", [128, BLK])
                        op("dve", TS(acc[:], u[:, 0:BLK], cw[:, cc, 0:1], cb[:, cc:cc + 1], ALU.mult, ALU.add),
                           reads=[k_u, k_par], writes=[k_acc])
                        for k in range(1, 4):
                            op("dve", STT(acc[:], u[:, k:k + BLK], cw[:, cc, k:k + 1], acc[:], ALU.mult, ALU.add),
                               reads=[k_u, k_par, k_acc], writes=[k_acc])
                        pend_silu.append((cc, acc, k_acc))
                        if len(pend_silu) > 1:
                            c0, a0, ka0 = pend_silu.pop(0)
                            SK(c0 >= 20)
                            op("act", A_(xbcT[:, c0, :], a0[:], AF.Silu), reads=[ka0], writes=[k_xbcT])
                            SK(cc >= 20)
                    for (c0, a0, ka0) in pend_silu:
                        SK(c0 >= 20)
                        op("act", A_(xbcT[:, c0, :], a0[:], AF.Silu), reads=[ka0], writes=[k_xbcT])
                    SK(False)
                    for ti in range(TPB):
                        t = blk * TPB + ti
                        cs = slice(ti * 128, (ti + 1) * 128)
                        ps, kps = bankF()
                        for kc in range(8):
                            op("pe", MM(ps[:, 0:32], XTb[:, kc, cs], Win[:, kc, 5120:5152], kc == 0, kc == 7),
                               reads=[k_win, k_XTb], writes=[kps])
                        sm, k_sm = tmp("sm", [128, 8, 32])
                        dt_, da, acs, dif, ea, dte, cd, f2 = [sm[:, i, :] for i in range(8)]
                        op("dve", TT(dt_, ps[:, 0:32], dtb, ALU.add), reads=[kps, k_par], writes=[k_sm])
                        op("act", A_(dt_, dt_, AF.Exp), reads=[k_sm], writes=[k_sm])
                        op("act", A_(dt_, dt_, AF.Ln, bias=1.0), reads=[k_sm], writes=[k_sm])
                        op("dve", TT(da, dt_, an, ALU.mult), reads=[k_sm, k_par], writes=[k_sm])
                        ps2, kps2 = bankF()
                        op("pe", MM(ps2[:, 0:32], triU, da), reads=[k_cst, k_sm], writes=[kps2])
                        op("pe", MM(ps2[:, 32:64], ones, da), reads=[k_cst, k_sm], writes=[kps2])
                        op("act", A_(acs, ps2[:, 0:32], AF.Copy), reads=[kps2], writes=[k_sm])
                        op("dve", TT(dif, ps2[:, 32:64], acs, ALU.subtract), reads=[kps2, k_sm], writes=[k_sm])
                        op("act", A_(ea, acs, AF.Exp), reads=[k_sm], writes=[k_sm])
                        op("act", A_(dte, dif, AF.Exp), reads=[k_sm], writes=[k_sm])
                        op("act", A_(cd, ps2[:, 32:64], AF.Exp), reads=[kps2], writes=[k_sm])
                        if phaseA:
                            op("dve", TT(totsum[:, 0:32], totsum[:, 0:32], ps2[:, 32:64], ALU.add), reads=[k_tot, kps2], writes=[k_tot])
                        op("dve", TT(f2, dt_, dte, ALU.mult), reads=[k_sm], writes=[k_sm])
                        op("dve", (lambda o, i_: lambda e: e.reciprocal(out=o, in_=i_))(dif, dt_), reads=[k_sm], writes=[k_sm])
                        op("dve", TT(dif, dif, dd, ALU.mult), reads=[k_sm, k_par], writes=[k_sm])
                        xdt, k_xdt = tmp("xdt", [128, 2048], BF16, bufs=1)
                        xw, k_xw = tmp("xw", [128, 2048], BF16, bufs=1)
                        for hf in range(2):
                            pb, kpb = bankB()
                            for c in range(8):
                                op("pe", TR(pb[:, c * 128:(c + 1) * 128], xbcT[:, hf * 8 + c, cs], identB[:]),
                                   reads=[k_xbcT, k_idb], writes=[kpb])
                            sl = slice(hf * 1024, (hf + 1) * 1024)
                            pv = pb.rearrange("p (h q) -> p h q", q=64)
                            hsl = slice(hf * 16, (hf + 1) * 16)
                            SK(True)
                            op("dve", TT(xdt[:, sl].rearrange("p (h q) -> p h q", q=64), pv,
                                         dt_[:, hsl].unsqueeze(2).broadcast_to([128, 16, 64]), ALU.mult),
                               reads=[kpb, k_sm], writes=[k_xdt])
                            SK(False)
                            op("dve", TT(xw[:, sl].rearrange("p (h q) -> p h q", q=64), pv,
                                         f2[:, hsl].unsqueeze(2).broadcast_to([128, 16, 64]), ALU.mult),
                               reads=[kpb, k_sm], writes=[k_xw])
                        btok, k_btok = tmp("btok", [128, 512], BF16, bufs=1)
                        pb, kpb = bankB()
                        for g in range(4):
                            op("pe", TR(pb[:, g * 128:(g + 1) * 128], xbcT[:, 16 + g, cs], identB[:]),
                               reads=[k_xbcT, k_idb], writes=[kpb])
                        op("act", A_(btok[:], pb[:, 0:512], AF.Copy), reads=[kpb], writes=[k_btok])
                        SK(True)
                        ps3, kps3 = bankF()
                        for g in range(4):
                            op("pe", MM(ps3[:, g * 128:(g + 1) * 128], xbcT[:, 16 + g, cs], xbcT[:, 20 + g, cs]),
                               reads=[k_xbcT], writes=[kps3])
                        cbm, k_cbm = tmp("cbm", [128, 4, 128], F32, bufs=1)
                        op("dve", TT(cbm[:], ps3.rearrange("p (g l) -> p g l", g=4),
                                     triU.unsqueeze(1).broadcast_to([128, 4, 128]), ALU.mult),
                           reads=[kps3, k_cst], writes=[k_cbm])
                        def make_MT(g_):
                          MT, k_MT = tmp("MT", [128, 8, 128], BF16, bufs=2)
                          for q in (2 * g_, 2 * g_ + 1):
                            R, k_R = tmp("R", [128, 4, 128], BF16, bufs=2)
                            op("dve", TT(R[:], triU.unsqueeze(1).broadcast_to([128, 4, 128]),
                                         da[:, q * 4:(q + 1) * 4].unsqueeze(2).broadcast_to([128, 4, 128]), ALU.mult),
                               reads=[k_cst, k_sm], writes=[k_R])
                            ps4, kps4 = bankF()
                            op("pe", MM(ps4, SUb[:], R[:].rearrange("p h l -> p (h l)")), reads=[k_idb, k_R], writes=[kps4])
                            Ex, k_Ex = tmp("Ex", [128, 4, 128], BF16, bufs=2)
                            op("act", A_(Ex[:].rearrange("p h l -> p (h l)"), ps4, AF.Exp), reads=[kps4], writes=[k_Ex])
                            op("dve", TT(MT[:, (q % 2) * 4:(q % 2 + 1) * 4, :], Ex[:],
                                         cbm[:, g_:g_ + 1, :].broadcast_to([128, 4, 128]), ALU.mult),
                               reads=[k_Ex, k_cbm], writes=[k_MT])
                          return MT, k_MT
                        yn, k_yn = tmp("yn", [128, 2048], BF16, bufs=1)
                        ss, k_ss = tmp("ss", [128, 8])
                        for g in range(4):
                            gs = slice(g * 512, (g + 1) * 512)
                            if g == 0:
                                nxtMT = make_MT(0)
                            MT, k_MT = nxtMT
                            if g < 3:
                                nxtMT = make_MT(g + 1)
                            psD, kpsD = bankF()
                            for r in range(8):
                                h = g * 8 + r
                                op("pe", MM(psD[:, r * 64:(r + 1) * 64], MT[:, r, :], xdt[:, h * 64:(h + 1) * 64]),
                                   reads=[k_MT, k_xdt], writes=[kpsD])
                            psO, kpsO = bankF()
                            op("pe", MM(psO, xbcT[:, 20 + g, cs], Sbf[:, gs]), reads=[k_xbcT, k_Sbf], writes=[kpsO])
                            psZ, kpsZ = bankF()
                            for kc in range(8):
                                op("pe", MM(psZ, XTb[:, kc, cs], Win[:, kc, gs], kc == 0, kc == 7),
                                   reads=[k_win, k_XTb], writes=[kpsZ])
                            y, k_y = tmp("y", [128, 512], F32, bufs=1)
                            op("dve", TT(y[:].rearrange("p (h q) -> p h q", q=64), psO.rearrange("p (h q) -> p h q", q=64),
                                         ea[:, g * 8:(g + 1) * 8].unsqueeze(2).broadcast_to([128, 8, 64]), ALU.mult),
                               reads=[kpsO, k_sm], writes=[k_y])
                            op("dve", TT(y[:], y[:], psD, ALU.add), reads=[k_y, kpsD], writes=[k_y])
                            y2, k_y2 = tmp("y2", [128, 512], F32, bufs=1)
                            op("dve", TT(y2[:].rearrange("p (h q) -> p h q", q=64), xdt[:, gs].rearrange("p (h q) -> p h q", q=64),
                                         dif[:, g * 8:(g + 1) * 8].unsqueeze(2).broadcast_to([128, 8, 64]), ALU.mult),
                               reads=[k_xdt, k_sm], writes=[k_y2])
                            op("dve", TT(y[:], y[:], y2[:], ALU.add), reads=[k_y, k_y2], writes=[k_y])
                            sz, k_sz = tmp("sz", [128, 512], F32, bufs=1)
                            op("act", A_(sz[:], psZ, AF.Tanh, scale=0.5), reads=[kpsZ], writes=[k_sz])
                            op("dve", STT(sz[:], sz[:], 1.0, psZ, ALU.add, ALU.mult), reads=[k_sz, kpsZ], writes=[k_sz])
                            op("dve", TT(y[:], y[:], sz[:], ALU.mult), reads=[k_y, k_sz], writes=[k_y])
                            op("act", A_(y2[:], y[:], AF.Square, accum_out=ss[:, g:g + 1]), reads=[k_y], writes=[k_y2, k_ss])
                            op("act", A_(yn[:, gs], y[:], AF.Copy), reads=[k_y], writes=[k_yn])
                        op("dve", TS(ss[:, 4:8], ss[:, 0:4], 1.0 / 512, 4.0 * EPS, ALU.mult, ALU.add), reads=[k_ss], writes=[k_ss])
                        rsqrt_(ss[:, 4:8], 4, k_ss)
                        for g in range(4):
                            gs = slice(g * 512, (g + 1) * 512)
                            op("dve", TS(yn[:, gs], yn[:, gs], ss[:, 4 + g:5 + g], None, ALU.mult), reads=[k_yn, k_ss], writes=[k_yn])
                        SK(False)
                        for g in range(4):
                            gs = slice(g * 512, (g + 1) * 512)
                            psU, kpsU = bankF()
                            op("pe", MM(psU, btok[:, g * 128:(g + 1) * 128], xw[:, gs]), reads=[k_btok, k_xw], writes=[kpsU])
                            op("dve", TT(S[:, gs].rearrange("p (h q) -> p h q", q=64), S[:, gs].rearrange("p (h q) -> p h q", q=64),
                                         cd[:, g * 8:(g + 1) * 8].unsqueeze(2).broadcast_to([128, 8, 64]), ALU.mult),
                               reads=[k_S, k_sm], writes=[k_S])
                            op("dve", TT(S[:, gs], S[:, gs], psU, ALU.add), reads=[k_S, kpsU], writes=[k_S])
                        SK(True)
                        op("act", A_(Sbf[:], S[:], AF.Copy), reads=[k_S], writes=[k_Sbf])
                        yT, k_yT = tmp("yT", [128, 16, 128], BF16, bufs=1)
                        for hf in range(2):
                            pb, kpb = bankB()
                            for c in range(8):
                                op("pe", TR(pb[:, c * 128:(c + 1) * 128], yn[:, (hf * 8 + c) * 128:(hf * 8 + c + 1) * 128], identB[:]),
                                   reads=[k_yn, k_idb], writes=[kpb])
                            op("dve", TT(yT[:, hf * 8:(hf + 1) * 8, :], pb.rearrange("p (c t) -> p c t", c=8),
                                         nw[:, hf * 8:(hf + 1) * 8].unsqueeze(2).broadcast_to([128, 8, 128]), ALU.mult),
                               reads=[kpb, k_par], writes=[k_yT])
                        psm, kpsm = [], []
                        for nb in range(2):
                            pm, kpm = bankF()
                            for c in range(16):
                                op("pe", MM(pm, yT[:, c, :], Wout[:, c, nb * 512:(nb + 1) * 512], c == 0, c == 15),
                                   reads=[k_yT, k_wout], writes=[kpm])
                            psm.append(pm)
                            kpsm.append(kpm)
                        mixer_epilogue(l, t, psm, kpsm, src_d, k_src)
                        SK(False)

            if MULTI:
                run_phase(True)
                op("act", A_(dtot[:], totsum[:, 0:32], AF.Exp), reads=[k_tot], writes=[k_dtot])
                P.barrier()
                arena_release(mark)
                exchange("l%d" % l, S[:], k_S, 2048, dtot[:], k_dtot, 32,
                         lambda a: a.rearrange("p (h q) -> p h q", q=64), lambda d_: d_.unsqueeze(2).broadcast_to([128, 32, 64]))
                op("act", A_(Sbf[:], S[:], AF.Copy), reads=[k_S], writes=[k_Sbf])
                P.barrier()
                arena_release(mark)
            run_phase(False)

        def hgrn_layer(l, src_d, k_src):
            j = l // 2
            Win = regA[:, 0:8 * 4096].rearrange("p (k n) -> p k n", k=8)
            Wout = regB[:, 0:8 * 1024].rearrange("p (k n) -> p k n", k=8)
            k_win, k_wout = Tk("win"), Tk("wout")
            wv = hg_w_in[j].rearrange("(k p) n -> p k n", p=128)
            for kc in range(8):
                for (a, b) in ((0, 2048), (2048, 4096)):
                    dma("pool", d_w, Win[:, kc, a:b], wv[:, kc, a:b], writes=[k_win])
            wo = hg_w_out[j].rearrange("(k p) n -> p k n", p=128)
            for kc in range(8):
                dma("pool", d_w2, Wout[:, kc, :], wo[:, kc, :], writes=[k_wout])
            par, _ = tmp("hg_par", [128, 8 * 8 + 128], F32, bufs=1)
            k_par = Tk("hpar")
            lb0, lb1, lbv, oml, noml, hc0, hc1, hnc1 = [par[:, i * 8:(i + 1) * 8] for i in range(8)]
            nwb = par[:, 64:192]
            dma("sp", d_par, lb0, hg_lb[0], writes=[k_par])
            dma("sp", d_par, lb1, hg_lb[1], writes=[k_par])
            dma("sp", d_par, nwb, hg_nw[j:j + 1, :].broadcast_to([128, 128]), writes=[k_par])
            if j == 0:
                op("dve", lambda e: e.memset(lbv, 0.0), writes=[k_par])
            else:
                op("dve", TT(lbv, lb1, lb0, ALU.subtract), reads=[k_par], writes=[k_par])
                op("act", A_(lbv, lbv, AF.Sigmoid), reads=[k_par], writes=[k_par])
            op("dve", TS(oml, lbv, -1.0, 1.0, ALU.mult, ALU.add), reads=[k_par], writes=[k_par])
            op("dve", TS(noml, oml, -1.0, None, ALU.mult), reads=[k_par], writes=[k_par])
            op("dve", TS(hc1, oml, 0.5, None, ALU.mult), reads=[k_par], writes=[k_par])
            op("dve", TT(hc0, lbv, hc1, ALU.add), reads=[k_par], writes=[k_par])
            op("dve", TS(hnc1, hc1, -1.0, None, ALU.mult), reads=[k_par], writes=[k_par])
            op("dve", TS(nwb, nwb, 0.5, None, ALU.mult), reads=[k_par], writes=[k_par])
            load_ln(l, 0)
            S, _ = tmp("hg_S", [128, 8, 128], F32, bufs=1)
            Sbf = [tmp("hg_Sbf", [128, 8, 128], BF16, bufs=2)[0] for i in range(2)]
            k_S, k_Sbf = Tk("hS"), [Tk("hSbf0"), Tk("hSbf1")]
            op("dve", lambda e: e.memset(S[:], 0.0), writes=[k_S])
            op("dve", lambda e: e.memset(Sbf[0][:], 0.0), writes=[k_Sbf[0]])

            dtot, _ = tmp("hg_dtot", [128, 8], F32, bufs=1)
            k_dtot = Tk("hdtot")
            mark = arena_mark()

            def run_phase(phaseA):
                cur["phaseA"] = phaseA
                if phaseA:
                    op("dve", lambda e: e.memset(S[:], 0.0), writes=[k_S])
                    op("dve", lambda e: e.memset(dtot[:], 1.0), writes=[k_dtot])
                for blk in range(NB):
                    XTb, k_XTb = load_xblock(src_d, k_src, blk)
                    qt, k_qt = tmp("qt", [128, 8, BLK], BF16, bufs=1)
                    kt, k_kt = tmp("kt", [128, 8, BLK], BF16, bufs=1)
                    qc, k_qc = tmp("qc", [128, 8, BLK], BF16, bufs=1)
                    ketok, k_ketok = tmp("ketok", [128, TPB, 8, 128], BF16, bufs=1)
                    ebl, k_ebl = tmp("ebl", [128, 8, BLK // 64], F32, bufs=1)
                    k_qt = [Tk("qt%d" % h) for h in range(8)]
                    k_kt = [Tk("kt%d" % h) for h in range(8)]
                    k_qc = [Tk("qc%d" % h) for h in range(8)]
                    k_ketok = [Tk("ketok%d" % h) for h in range(8)]
                    k_ebl = [Tk("ebl%d" % h) for h in range(8)]
                    ffa, _ = tmp("ffa", [128, 8, BLK], F32, bufs=1)
                    kka, _ = tmp("kka", [128, 8, BLK], BF16, bufs=1)
                    k_ffh = [Tk("ffh%d" % h) for h in range(8)]
                    k_kkh = [Tk("kkh%d" % h) for h in range(8)]
                    for h in range(8):
                        psf, kpsf = bankF()
                        for kc in range(8):
                            op("pe", MM(psf[:, 0:BLK], Win[:, kc, 1024 + h * 128:1024 + (h + 1) * 128], XTb[:, kc, :], kc == 0, kc == 7),
                               reads=[k_win, k_XTb], writes=[kpsf])
                        sg, k_sg = tmp("sg", [128, BLK], F32, bufs=2)
                        op("act", A_(sg[:], psf[:, 0:BLK], AF.Tanh, scale=0.5), reads=[kpsf], writes=[k_sg])
                        op("dve", TS(ffa[:, h, :], sg[:], hc1[:, h:h + 1], hc0[:, h:h + 1], ALU.mult, ALU.add), reads=[k_sg, k_par], writes=[k_ffh[h]])
                        op("dve", TS(kka[:, h, :], sg[:], hnc1[:, h:h + 1], hc1[:, h:h + 1], ALU.mult, ALU.add), reads=[k_sg, k_par], writes=[k_kkh[h]])
                    k_lnf = Tk("lnf")
                    op("act", A_(ffa[:].rearrange("p h t -> p (h t)"), ffa[:].rearrange("p h t -> p (h t)"), AF.Ln), reads=k_ffh, writes=[k_lnf])
                    def head3(h):
                        SK(True)
                        psq, kpsq = bankF()
                        for kc in range(8):
                            op("pe", MM(psq[:, 0:BLK], Win[:, kc, h * 128:(h + 1) * 128], XTb[:, kc, :], kc == 0, kc == 7),
                               reads=[k_win, k_XTb], writes=[kpsq])
                        SK(False)
                        k_kk = k_kkh[h]
                        bb, k_bb = tmp("bb", [128, BLK], F32, bufs=2)
                        op("dve", (lambda o, d0, d1: lambda e: e.tensor_tensor_scan(out=o, data0=d0, data1=d1, initial=0.0,
                                                                                     op0=ALU.mult, op1=ALU.add))(bb[:], rst[:, 0:BLK], ffa[:, h, :]),
                           reads=[k_lnf, k_cst], writes=[k_bb])
                        eb, k_eb = tmp("eb", [128, BLK], F32, bufs=1)
                        op("act", A_(eb[:], bb[:], AF.Exp), reads=[k_bb], writes=[k_eb])
                        SK(True)
                        op("dve", TT(qt[:, h, :], psq[:, 0:BLK], eb[:], ALU.mult), reads=[kpsq, k_eb], writes=[k_qt[h]])
                        SK(False)
                        op("act", A_(ebl[:, h, :], eb[:, 63:BLK:64], AF.Copy), reads=[k_eb], writes=[k_ebl[h]])
                        SK(True)
                        bcn, k_bcn = tmp("bcn", [128, BLK], F32, bufs=1)
                        op("dve", TT(bcn[:].rearrange("p (c t) -> p c t", t=64), bb[:].rearrange("p (c t) -> p c t", t=64),
                                     bb[:, 31:BLK:64].unsqueeze(2).broadcast_to([128, BLK // 64, 64]), ALU.subtract),
                           reads=[k_bb], writes=[k_bcn])
                        enb, k_enb = tmp("enb", [128, BLK], F32, bufs=1)
                        op("act", A_(enb[:], bcn[:], AF.Exp, scale=-1.0), reads=[k_bcn], writes=[k_enb])
                        ebc, k_ebc = bcn, k_bcn
                        op("act", A_(ebc[:], bcn[:], AF.Exp), reads=[k_bcn], writes=[k_ebc])
                        op("dve", TT(qc[:, h, :], psq[:, 0:BLK], ebc[:], ALU.mult), reads=[kpsq, k_ebc], writes=[k_qc[h]])
                        op("dve", TT(kt[:, h, :], kka[:, h, :], enb[:], ALU.mult), reads=[k_kk, k_enb], writes=[k_kt[h]])
                        SK(False)
                        ee, k_ee = tmp("ee", [128, BLK], F32, bufs=1)
                        op("dve", TT(ee[:].rearrange("p (c t) -> p c t", t=64), bb[:].rearrange("p (c t) -> p c t", t=64),
                                     bb[:, 63:BLK:64].unsqueeze(2).broadcast_to([128, BLK // 64, 64]), ALU.subtract),
                           reads=[k_bb], writes=[k_ee])
                        op("act", A_(ee[:], ee[:], AF.Exp, scale=-1.0), reads=[k_ee], writes=[k_ee])
                        ke, k_ke = tmp("ke", [128, BLK], BF16)
                        op("dve", TT(ke[:], kka[:, h, :], ee[:], ALU.mult), reads=[k_kk, k_ee], writes=[k_ke])
                        pb, kpb = bankB()
                        for ti in range(TPB):
                            op("pe", TR(pb[:, ti * 128:(ti + 1) * 128], ke[:, ti * 128:(ti + 1) * 128], identB[:]),
                               reads=[k_ke, k_idb], writes=[kpb])
                        op("act", A_(ketok[:, :, h, :], pb[:, 0:TPB * 128].rearrange("p (a k) -> p a k", a=TPB), AF.Copy),
                           reads=[kpb], writes=[k_ketok[h]])

                    P.interleave(cur, [(lambda s_: (lambda: [head3(h) for h in range(s_, 8, 2)]))(s_) for s_ in range(2)])
                    for ti in range(TPB):
                        t = blk * TPB + ti
                        cs = slice(ti * 128, (ti + 1) * 128)
                        vb, k_vb = tmp("vb", [128, 1024], BF16, bufs=1)
                        sgt, k_sgt = tmp("sgt", [128, 1024], BF16, bufs=1)
                        for nb in range(2):
                            ps, kps = bankF()
                            for kc in range(8):
                                op("pe", MM(ps, XTb[:, kc, cs], Win[:, kc, 2048 + nb * 512:2048 + (nb + 1) * 512], kc == 0, kc == 7),
                                   reads=[k_win, k_XTb], writes=[kps])
                            op("act", A_(vb[:, nb * 512:(nb + 1) * 512], ps, AF.Copy), reads=[kps], writes=[k_vb])
                            SK(True)
                            ps, kps = bankF()
                            for kc in range(8):
                                op("pe", MM(ps, XTb[:, kc, cs], Win[:, kc, 3072 + nb * 512:3072 + (nb + 1) * 512], kc == 0, kc == 7),
                                   reads=[k_win, k_XTb], writes=[kps])
                            op("act", A_(sgt[:, nb * 512:(nb + 1) * 512], ps, AF.Tanh, scale=0.5), reads=[kps], writes=[k_sgt])
                            op("dve", STT(sgt[:, nb * 512:(nb + 1) * 512], sgt[:, nb * 512:(nb + 1) * 512], 1.0, ps, ALU.add, ALU.mult),
                               reads=[k_sgt, kps], writes=[k_sgt])
                            SK(False)
                        SK(True)
                        scm, k_scm = tmp("scm", [128, 8, 128], BF16, bufs=1)
                        for hf in range(2):
                            ps, kps = bankF()
                            for hh in range(4):
                                h = hf * 4 + hh
                                op("pe", MM(ps[:, hh * 128:(hh + 1) * 128], kt[:, h, cs], qc[:, h, cs]), reads=[k_kt[h], k_qc[h]], writes=[kps])
                            scl, k_scl = tmp("scl", [128, 512], F32, bufs=1)
                            op("dve", TS(scl[:], ps, -1e30, 1e30, ALU.max, ALU.min), reads=[kps], writes=[k_scl])
                            op("dve", TT(scm[:, hf * 4:(hf + 1) * 4, :], scl[:].rearrange("p (h l) -> p h l", h=4),
                                         maskBD.unsqueeze(1).broadcast_to([128, 4, 128]), ALU.mult),
                               reads=[k_scl, k_cst], writes=[k_scm])

                        SK(False)

                        def state_update(half, dst):
                            rs = slice(half * 64, (half + 1) * 64)
                            c = ti * 2 + half
                            if phaseA:
                                op("dve", TT(dtot[:], dtot[:], ebl[:, :, c], ALU.mult), reads=[k_dtot] + k_ebl, writes=[k_dtot])
                            for hf in range(2):
                                ps, kps = bankF()
                                for hh in range(4):
                                    h = hf * 4 + hh
                                    op("pe", MM(ps[:, hh * 128:(hh + 1) * 128], ketok[rs, ti, h, :], vb[rs, h * 128:(h + 1) * 128]),
                                       reads=[k_ketok[h], k_vb], writes=[kps])
                                hs_ = slice(hf * 4, (hf + 1) * 4)
                                op("dve", TT(S[:, hs_, :], S[:, hs_, :], ebl[:, hs_, c:c + 1].broadcast_to([128, 4, 128]), ALU.mult),
                                   reads=[k_S] + k_ebl[hf * 4:hf * 4 + 4], writes=[k_S])
                                op("dve", TT(S[:, hs_, :], S[:, hs_, :], ps.rearrange("p (h v) -> p h v", h=4), ALU.add),
                                   reads=[k_S, kps], writes=[k_S])
                            op("act", A_(Sbf[dst][:], S[:], AF.Copy), reads=[k_S], writes=[k_Sbf[dst]])

                        state_update(0, 1)
                        SK(True)
                        pso = []
                        for hf in range(2):
                            ps, kps = bankF()
                            for hh in range(4):
                                h = hf * 4 + hh
                                o_ = ps[:, hh * 128:(hh + 1) * 128]
                                op("pe", MM(o_, scm[:, h, :], vb[:, h * 128:(h + 1) * 128], True, False), reads=[k_scm, k_vb], writes=[kps])
                                op("pe", MM(o_[0:64, :], qt[:, h, ti * 128:ti * 128 + 64], Sbf[0][:, h, :], False, False),
                                   reads=[k_qt[h], k_Sbf[0]], writes=[kps])
                                op("pe", MM(o_[64:128, :], qt[:, h, ti * 128 + 64:ti * 128 + 128], Sbf[1][:, h, :], False, True),
                                   reads=[k_qt[h], k_Sbf[1]], writes=[kps])
                            pso.append((ps, kps))
                        SK(False)
                        state_update(1, 0)
                        SK(True)
                        on, k_on = tmp("on", [128, 8, 128], F32, bufs=1)
                        ssq, k_ssq = tmp("ssq", [128, 16])
                        junk, k_junk = tmp("junk", [128, 128])
                        for hf in range(2):
                            ps, kps = pso[hf]
                            for hh in range(4):
                                h = hf * 4 + hh
                                op("act", A_(junk[:], ps[:, hh * 128:(hh + 1) * 128], AF.Square, accum_out=ssq[:, h:h + 1]),
                                   reads=[kps], writes=[k_junk, k_ssq])
                        op("dve", TS(ssq[:, 8:16], ssq[:, 0:8], 1.0 / 128, EPS, ALU.mult, ALU.add), reads=[k_ssq], writes=[k_ssq])
                        rsqrt_(ssq[:, 8:16], 8, k_ssq)
                        for hf in range(2):
                            ps, kps = pso[hf]
                            hs_ = slice(hf * 4, (hf + 1) * 4)
                            op("dve", TT(on[:, hs_, :], ps.rearrange("p (h v) -> p h v", h=4),
                                         ssq[:, 8 + hf * 4:12 + hf * 4].unsqueeze(2).broadcast_to([128, 4, 128]), ALU.mult),
                               reads=[kps, k_ssq], writes=[k_on])
                        op("dve", TT(on[:], on[:], nwb.unsqueeze(1).broadcast_to([128, 8, 128]), ALU.mult), reads=[k_on, k_par], writes=[k_on])
                        onb, k_onb = tmp("onb", [128, 1024], BF16, bufs=1)
                        op("dve", TT(onb[:], on[:].rearrange("p h v -> p (h v)"), sgt[:], ALU.mult), reads=[k_on, k_sgt], writes=[k_onb])
                        onT, k_onT = tmp("onT", [128, 8, 128], BF16, bufs=1)
                        pb, kpb = bankB()
                        for c in range(8):
                            op("pe", TR(pb[:, c * 128:(c + 1) * 128], onb[:, c * 128:(c + 1) * 128], identB[:]), reads=[k_onb, k_idb], writes=[kpb])
                        op("act", A_(onT[:], pb.rearrange("p (c t) -> p c t", c=8), AF.Copy), reads=[kpb], writes=[k_onT])
                        psm, kpsm = [], []
                        for nb in range(2):
                            pm, kpm = bankF()
                            for c in range(8):
                                op("pe", MM(pm, onT[:, c, :], Wout[:, c, nb * 512:(nb + 1) * 512], c == 0, c == 7),
                                   reads=[k_onT, k_wout], writes=[kpm])
                            psm.append(pm)
                            kpsm.append(kpm)
                        mixer_epilogue(l, t, psm, kpsm, src_d, k_src)
                        SK(False)

            if MULTI:
                run_phase(True)
                P.barrier()
                arena_release(mark)
                exchange("l%d" % l, S[:].rearrange("p h v -> p (h v)"), k_S, 1024, dtot[:], k_dtot, 8,
                         lambda a: a.rearrange("p (h v) -> p h v", v=128), lambda d_: d_.unsqueeze(2).broadcast_to([128, 8, 128]))
                op("act", A_(Sbf[0][:], S[:], AF.Copy), reads=[k_S], writes=[k_Sbf[0]])
                P.barrier()
                arena_release(mark)
            run_phase(False)

        def mlp_ple_segment(l, seg, dst_d, k_dst, last):
            Xacc = regA[:, 0:16 * 2048].bitcast(F32).rearrange("p (t d) -> p t d", t=16)
            kX = [Tk("Xacc%d" % i) for i in range(16)]
            XTh["XT"] = tmp("XT", [128, 8, 2048], BF16, bufs=1)[0]
            XT = XTh["XT"]
            def ld_tile(tl):
                t = seg * 16 + tl
                dma("sp", d_h[tl % 2], Xacc[:, tl, :], hs_d[t * 128:(t + 1) * 128, :], reads=[k_hs[t]], writes=[kX[tl]])
                hb, k_hbb = tmp("ln_hb", [128, D], BF16)
                op("act", A_(hb[:], Xacc[:, tl, :], AF.Copy, scale=1.0 / ALPHA), reads=[kX[tl]], writes=[k_hbb])
                to_XT(hb, k_hbb, tl)

            P.interleave(cur, [(lambda s_: (lambda: [ld_tile(tl) for tl in range(s_, 16, 2)]))(s_) for s_ in range(2)])
            load_ln(l, 1)
            wbuf = [regB[:, i * 8192:(i + 1) * 8192] for i in range(2)]
            k_wb = [Tk("mwb0"), Tk("mwb1")]
            w1v = w1_d[l].rearrange("(k p) n -> p k n", p=128)
            w2v = w2_d[l].rearrange("(k p) n -> p k n", p=128)
            for e8 in range(8):
                wb = wbuf[e8 % 2]
                kwb = k_wb[e8 % 2]
                w1e = wb[:, 0:4096].rearrange("p (k n) -> p k n", k=8)
                w2e = wb[:, 4096:8192].rearrange("p (k n) -> p k n", k=4)
                for kc in range(8):
                    dma("pool", d_mw[e8 % 2], w1e[:, kc, :], w1v[:, kc, e8 * 512:(e8 + 1) * 512], writes=[kwb])
                for fc in range(4):
                    dma("pool", d_mw[e8 % 2], w2e[:, fc, :], w2v[:, e8 * 4 + fc, :], writes=[kwb])
                for tb in range(4):
                    hid, k_hid = tmp("hid", [128, 4, 512], BF16, bufs=2)
                    for fc in range(4):
                        ps, kps = bankF()
                        for kc in range(8):
                            op("pe", MM(ps, w1e[:, kc, fc * 128:(fc + 1) * 128], XT[:, kc, tb * 512:(tb + 1) * 512], kc == 0, kc == 7),
                               reads=[kwb] + kXT[tb * 4:tb * 4 + 4], writes=[kps])
                        rl, k_rl = tmp("rl", [128, 512])
                        op("act", A_(rl[:], ps, AF.Relu), reads=[kps], writes=[k_rl])
                        op("dve", TT(hid[:, fc, :], rl[:], rl[:], ALU.mult), reads=[k_rl], writes=[k_hid])
                    for ti in range(4):
                        tl = tb * 4 + ti
                        for nb in range(2):
                            ps, kps = bankF()
                            for fc in range(4):
                                op("pe", MM(ps, hid[:, fc, ti * 128:(ti + 1) * 128], w2e[:, fc, nb * 512:(nb + 1) * 512], fc == 0, fc == 3),
                                   reads=[k_hid, kwb], writes=[kps])
                            xa = Xacc[:, tl, nb * 512:(nb + 1) * 512]
                            op("dve", TT(xa, xa, ps, ALU.add), reads=[kX[tl], kps], writes=[kX[tl]])
            Wg = regB[:, 0:8192].rearrange("p (k n) -> p k n", k=8)
            Wp = regB[:, 8192:8192 + 2048].rearrange("p (k n) -> p k n", k=2)
            wgv = wg_d[l].rearrange("(k p) n -> p k n", p=128)
            wpv = wp_d[l].rearrange("(k p) n -> p k n", p=128)
            for kc in range(8):
                dma("pool", d_pl, Wg[:, kc, :], wgv[:, kc, :], writes=[k_wb[0]])
            for kc in range(2):
                dma("pool", d_pl, Wp[:, kc, :], wpv[:, kc, :], writes=[k_wb[1]])
            def ple_tile(tl):
                t = seg * 16 + tl
                hb, k_hbb = layer_norm(Xacc[:, tl, :], kX[tl], Xacc[:, tl, :], kX[tl])
                to_XT(hb, k_hbb, tl)
                pt, k_pt = tmp("pt", [128, 256])
                dma("sp", d_p[tl % 2], pt[:], p_d[l, t * 128:(t + 1) * 128, :], writes=[k_pt])
                ptb, k_ptb = tmp("ptb", [128, 256], BF16)
                op("dve", CP(ptb[:], pt[:]), reads=[k_pt], writes=[k_ptb])
                pb, kpb = bankB()
                for c in range(2):
                    op("pe", TR(pb[:, c * 128:(c + 1) * 128], ptb[:, c * 128:(c + 1) * 128], identB[:]), reads=[k_ptb, k_idb], writes=[kpb])
                pT, k_pT = tmp("pT", [128, 2, 128], BF16)
                op("act", A_(pT[:], pb[:, 0:256].rearrange("p (c t) -> p c t", c=2), AF.Copy), reads=[kpb], writes=[k_pT])
                xn, k_xn = tmp("xn", [128, D])
                for nb in range(2):
                    psg, kpsg = bankF()
                    for kc in range(8):
                        op("pe", MM(psg, XT[:, kc, tl * 128:(tl + 1) * 128], Wg[:, kc, nb * 512:(nb + 1) * 512], kc == 0, kc == 7),
                           reads=[kXT[tl], k_wb[0]], writes=[kpsg])
                    psp, kpsp = bankF()
                    for kc in range(2):
                        op("pe", MM(psp, pT[:, kc, :], Wp[:, kc, nb * 512:(nb + 1) * 512], kc == 0, kc == 1),
                           reads=[k_pT, k_wb[1]], writes=[kpsp])
                    sgm, k_sgm = tmp("sgm", [128, 512])
                    op("act", A_(sgm[:], psg, AF.Tanh, scale=0.5), reads=[kpsg], writes=[k_sgm])
                    op("dve", STT(sgm[:], sgm[:], 1.0, psp, ALU.add, ALU.mult), reads=[k_sgm, kpsp], writes=[k_sgm])
                    op("dve", STT(xn[:, nb * 512:(nb + 1) * 512], sgm[:], 0.5, Xacc[:, tl, nb * 512:(nb + 1) * 512], ALU.mult, ALU.add),
                       reads=[k_sgm, kX[tl]], writes=[k_xn])
                dma("sp", d_st, dst_d[t * 128:(t + 1) * 128, :], xn[:], reads=[k_xn], writes=[k_dst[t]])
                if MULTI and tl == 15 and (l + 1) in layers and (l + 1) % 2 == 0:
                    xnb, k_xnb = tmp("ln_hb", [128, D], BF16)
                    op("act", A_(xnb[:], xn[:], AF.Copy), reads=[k_xn], writes=[k_xnb])
                    pb, kpb = bankB()
                    for c in range(8):
                        op("pe", TR(pb[:, c * 128:(c + 1) * 128], xnb[:, c * 128:(c + 1) * 128], identB[:]), reads=[k_xnb, k_idb], writes=[kpb])
                    xl, k_xl = tmp("xl", [128, 24], F32, bufs=1)
                    op("act", A_(xl[:].rearrange("p (c t) -> p c t", c=8), pb.rearrange("p (c t) -> p c t", c=8)[:, :, 125:128], AF.Copy),
                       reads=[kpb], writes=[k_xl])
                    halo_exchange("l%d" % l, xl[:], k_xl)

            P.interleave(cur, [(lambda s_: (lambda: [ple_tile(tl) for tl in range(s_, 16, 2)]))(s_) for s_ in range(2)])

        src_d, k_src = x_d, None
        k_out = [Tk("out%d" % i) for i in range(NT)]
        for li, l in enumerate(layers):
            last = li == len(layers) - 1
            if l % 2 == 0:
                ssd_layer(l, src_d, k_src)
            else:
                hgrn_layer(l, src_d, k_src)
            P.barrier()
            arena_reset()
            dst_d, k_dst = (out_d, k_out) if last else (xs_d, k_xs)
            for seg in range(NSEG):
                mlp_ple_segment(l, seg, dst_d, k_dst, last)
                P.barrier()
                arena_reset()
            src_d, k_src = xs_d, k_xs
        P.barrier()
        block = es.enter_context(nc.Block())
        P.replay(block)
    return nc


def make_cst():
    c = np.zeros((128, 5 * 128 + 512), np.float32)
    i = np.arange(128)
    c[:, 0:128] = np.eye(128)
    c[:, 128:256] = (i[:, None] <= i[None, :])
    c[:, 256:384] = (i[:, None] > i[None, :])
    c[:, 384:512] = 1.0
    c[:, 512:640] = (i[:, None] <= i[None, :]) & ((i[:, None] // 64) == (i[None, :] // 64))
    r = np.ones(512, np.float32)
    r[0::64] = 0.0
    c[:, 640:1152] = r[None, :]
    return c


def prep_weights(inp):
    f = lambda a: np.ascontiguousarray(np.asarray(a, dtype=np.float32))
    w = {}
    for k in ["ssd_w_in", "ssd_dt_bias", "ssd_a_log", "ssd_d", "ssd_w_out", "hgrn_w_in", "hgrn_norm_w", "hgrn_w_out",
              "ln_g", "ln_b", "mlp_w1", "mlp_w2", "ple_w_proj", "ple_w_gate"]:
        w[k] = f(inp[k])
    cw = f(inp["ssd_conv_w"])
    w["ssd_cw"] = f(cw.reshape(2, 4, 24, 128).transpose(0, 3, 2, 1).reshape(2, 128, 96))
    w["ssd_cb"] = f(f(inp["ssd_conv_b"]).reshape(2, 24, 128).transpose(0, 2, 1))
    w["ssd_nw"] = f(f(inp["ssd_norm_w"]).reshape(2, 16, 128).transpose(0, 2, 1))
    w["hgrn_lb"] = f(f(inp["hgrn_lower_bounds"]).reshape(2, 8, 128).transpose(0, 2, 1))
    w["cst"] = make_cst()
    return w


def run(inp, T, layers, ncores=8):
    w = prep_weights(inp)
    x = np.asarray(inp["x"], np.float32)
    p = np.asarray(inp["p"], np.float32)
    nc = build(T, layers)
    in_maps = []
    for c in range(ncores):
        b = c if c < 2 else 0
        m = dict(w)
        m["x"] = np.ascontiguousarray(x[b, :T])
        m["p"] = np.ascontiguousarray(p[:, b, :T])
        m["msk"] = np.zeros((128, 18), np.float32)
        m["xhT"] = np.zeros((128, 24), np.float32)
        in_maps.append(m)
    res = run_bass_kernel_spmd(nc, in_maps, core_ids=list(range(ncores)))
    return np.stack([res.results[0]["out"], res.results[1]["out"]], axis=0)


def run_multi(inp, layers, nseq=4, trace=False):
    w = prep_weights(inp)
    x = np.asarray(inp["x"], np.float32)
    p = np.asarray(inp["p"], np.float32)
    TS_ = 2048
    nc = build(TS_, layers, MULTI=True)
    in_maps = []
    for c in range(8):
        b, r = c // 4, c % 4
        m = dict(w)
        m["x"] = np.ascontiguousarray(x[b, r * TS_:(r + 1) * TS_])
        m["p"] = np.ascontiguousarray(p[:, b, r * TS_:(r + 1) * TS_])
        msk = np.zeros((128, 18), np.float32)
        slot = lambda bb, rr: bb * 3 + rr
        if r < 3:
            msk[:, slot(b, r)] = 1.0
        for rr in range(r):
            msk[:, 6 + slot(b, rr)] = 1.0
        if r > 0:
            msk[:, 12 + slot(b, r - 1)] = 1.0
        m["msk"] = msk
        xh = np.zeros((128, 24), np.float32)
        if r > 0:
            prev = x[b, r * TS_ - 3:r * TS_, :]
            xh = np.ascontiguousarray(prev.reshape(3, 8, 128).transpose(2, 1, 0).reshape(128, 24))
        m["xhT"] = xh
        in_maps.append(m)
    res = run_bass_kernel_spmd(nc, in_maps, core_ids=list(range(8)))
    out = np.zeros((2, 4 * TS_, D), np.float32)
    for c in range(8):
        out[c // 4, (c % 4) * TS_:(c % 4 + 1) * TS_] = res.results[c]["out"]
    return out


def kernel(**inputs):
    return run_multi(inputs, [0, 1, 2, 3]).astype(np.float32)
```
